# Optimizing a Trainium2 kernel written in Bass

```python
import functools
import jax, jax.numpy as jnp
from jax import lax
import numpy as np

D_MODEL = 1024
BATCH = 16
SEQ = 2048
DEPTH = 1
DEC_BATCH = 32
DEC_SEQ = 1
PAST_LEN = 16384
PAGE_SIZE = 128

ATT_HEAD_DIM = 64
ATT_WIDTH = D_MODEL // 2
ATT_HEADS = ATT_WIDTH // ATT_HEAD_DIM
ATT_KV_HEADS = 2
ATT_GROUP = ATT_HEADS // ATT_KV_HEADS
KV_WIDTH = ATT_KV_HEADS * ATT_HEAD_DIM
ATT_SCALE = ATT_HEAD_DIM ** -0.5
IDX_HEADS = 8
IDX_DIM = 64
IDX_W_SCALE = (IDX_HEADS ** -0.5) * (IDX_DIM ** -0.5)
TOPK_MAX = 256
Q_BLOCK = 128
REC_DIM = 128
REC_WIDTH = D_MODEL - ATT_WIDTH
REC_HEADS = REC_WIDTH // REC_DIM
CHUNK = 64
D_FF = 4 * D_MODEL
MIX_WIDTH = ATT_WIDTH + REC_WIDTH
EPS = 1e-6
IN_SPLITS = (ATT_WIDTH, KV_WIDTH, KV_WIDTH, IDX_HEADS * IDX_DIM, IDX_DIM, IDX_HEADS,
             REC_WIDTH, REC_WIDTH, REC_WIDTH, REC_WIDTH)
IN_WIDTH = sum(IN_SPLITS)
IN_OFFSETS = [int(o) for o in np.cumsum(IN_SPLITS)[:-1]]

kernel_name = "hymba_dsa_hgrn2_decode_step"


def rms_norm(x, g):
    xf = x.astype(jnp.float32)
    y = xf * lax.rsqrt(jnp.mean(xf * xf, axis=-1, keepdims=True) + EPS)
    return (y * g.astype(jnp.float32)).astype(x.dtype)


def gather_rows(t, idx):
    return jax.vmap(lambda a, i: a[i])(t, idx)


def indexer_scores(iq, iw, ik):
    dots = jnp.einsum('bthd,bsd->btsh', iq.astype(jnp.float32), ik.astype(jnp.float32))
    return jnp.einsum('btsh,bth->bts', jax.nn.relu(dots), iw.astype(jnp.float32))


def sparse_attend(q, k_sel, v_sel, valid):
    B, T = q.shape[:2]
    qg = q.reshape(B, T, ATT_KV_HEADS, ATT_GROUP, ATT_HEAD_DIM)
    logits = jnp.einsum('btkgd,btskd->btkgs', qg, k_sel).astype(jnp.float32) * ATT_SCALE
    logits = jnp.where(valid[:, :, None, None, :], logits, -jnp.inf)
    p = jax.nn.softmax(logits, axis=-1).astype(v_sel.dtype)
    o = jnp.einsum('btkgs,btskd->btkgd', p, v_sel)
    return o.reshape(B, T, ATT_WIDTH)


def prompt_attention(aq, ak, av, iq, ik, iw):
    B, L = aq.shape[:2]
    n_sel = min(TOPK_MAX, L // 4)
    n_blk = L // Q_BLOCK

    def to_blocks(t):
        return jnp.moveaxis(t.reshape(B, n_blk, Q_BLOCK, *t.shape[2:]), 1, 0)

    qpos = jnp.arange(L, dtype=jnp.int32).reshape(n_blk, Q_BLOCK)
    kpos = jnp.arange(L, dtype=jnp.int32)

    def block(args):
        q_b, iq_b, iw_b, qp = args
        scores = indexer_scores(iq_b, iw_b, ik)
        causal = kpos[None, :] <= qp[:, None]
        scores = jnp.where(causal[None], scores, -jnp.inf)
        _, idx = lax.top_k(scores, n_sel)
        valid = idx <= qp[None, :, None]
        return sparse_attend(q_b, gather_rows(ak, idx), gather_rows(av, idx), valid)

    out = lax.map(block, (to_blocks(aq), to_blocks(iq), to_blocks(iw), qpos))
    return jnp.moveaxis(out, 0, 1).reshape(B, L, ATT_WIDTH)


def sample_attention(aq, ak, av, iq, ik, iw, cache_k, cache_v, cache_kidx, page_table):
    Bd, T = aq.shape[:2]
    past = page_table.shape[1] * PAGE_SIZE
    L = past + T
    n_sel = min(TOPK_MAX, L // 4)
    ik_past = cache_kidx[page_table].reshape(Bd, past, IDX_DIM)
    ik_all = jnp.concatenate([ik_past.astype(jnp.float32), ik.astype(jnp.float32)], axis=1)
    qpos = past + jnp.arange(T, dtype=jnp.int32)
    kpos = jnp.arange(L, dtype=jnp.int32)
    scores = indexer_scores(iq, iw, ik_all)
    scores = jnp.where((kpos[None, :] <= qpos[:, None])[None], scores, -jnp.inf)
    _, idx = lax.top_k(scores, n_sel)
    in_past = idx < past
    pidx = jnp.minimum(idx, past - 1)
    phys = jax.vmap(lambda pt, i: pt[i])(page_table, pidx // PAGE_SIZE)
    off = pidx % PAGE_SIZE
    nidx = jnp.clip(idx - past, 0, T - 1)
    sel = in_past[..., None, None]
    k_sel = jnp.where(sel, cache_k[phys, off], gather_rows(ak, nidx))
    v_sel = jnp.where(sel, cache_v[phys, off], gather_rows(av, nidx))
    valid = idx <= qpos[None, :, None]
    return sparse_attend(aq, k_sel, v_sel, valid)


def hgrn2_chunked(q, log_f, k, v, s0):
    B, L = q.shape[:2]
    C = CHUNK if L % CHUNK == 0 else L
    n = L // C

    def chunks(t):
        return jnp.moveaxis(t.reshape(B, n, C, *t.shape[2:]), 1, 0)

    causal = jnp.tril(jnp.ones((C, C), dtype=bool))[None, :, :, None, None]

    def step(S, inp):
        qc, gc, kc, vc = inp
        b = jnp.cumsum(gc, axis=1)
        o_inter = jnp.einsum('bchk,bhkv->bchv', qc * jnp.exp(b), S)
        diff = b[:, :, None] - b[:, None, :]
        decay = jnp.exp(jnp.where(causal, diff, -jnp.inf))
        A = jnp.einsum('bthk,bshk,btshk->bhts', qc, kc, decay)
        o_intra = jnp.einsum('bhts,bshv->bthv', A, vc)
        b_last = b[:, -1]
        S_new = jnp.exp(b_last)[..., None] * S + jnp.einsum(
            'bshk,bshv->bhkv', kc * jnp.exp(b_last[:, None] - b), vc)
        return S_new, o_inter + o_intra

    S_fin, o = lax.scan(step, s0, (chunks(q), chunks(log_f), chunks(k), chunks(v)))
    return jnp.moveaxis(o, 0, 1).reshape(B, L, *o.shape[3:]), S_fin


def hgrn2_mixer(rq, rf, ri, rg, lb, rec_norm_g, s0):
    B, L = rq.shape[:2]
    shp = (B, L, REC_HEADS, REC_DIM)
    f32 = jnp.float32
    q = (jax.nn.silu(rq.astype(f32)) * (REC_DIM ** -0.5)).reshape(shp)
    xf = rf.astype(f32).reshape(shp)
    lbh = lb.reshape(REC_HEADS, REC_DIM)
    log_f = jnp.logaddexp(jnp.log(lbh), jnp.log1p(-lbh) + jax.nn.log_sigmoid(xf))
    k = (1.0 - lbh) * jax.nn.sigmoid(-xf)
    v = ri.astype(f32).reshape(shp)
    o, S = hgrn2_chunked(q, log_f, k, v, s0.astype(f32))
    o = rms_norm(o, rec_norm_g).reshape(B, L, REC_WIDTH) * jax.nn.silu(rg.astype(f32))
    return o.astype(rg.dtype), S


def trunk_layer(x, s0, lb, attend, norm1_g, w_in, q_norm_g, k_norm_g, idx_k_norm_g,
                rec_norm_g, w_out, norm2_g, w_up, w_down):
    B, L, _ = x.shape
    h = rms_norm(x, norm1_g)
    aq, ak, av, iq, ik, iw, rq, rf, ri, rg = jnp.split(h @ w_in, IN_OFFSETS, axis=-1)
    aq = rms_norm(aq.reshape(B, L, ATT_HEADS, ATT_HEAD_DIM), q_norm_g)
    ak = rms_norm(ak.reshape(B, L, ATT_KV_HEADS, ATT_HEAD_DIM), k_norm_g)
    av = av.reshape(B, L, ATT_KV_HEADS, ATT_HEAD_DIM)
    iq = iq.reshape(B, L, IDX_HEADS, IDX_DIM)
    ik = rms_norm(ik, idx_k_norm_g)
    iw = iw * IDX_W_SCALE
    att = attend(aq, ak, av, iq, ik, iw)
    rec, S_new = hgrn2_mixer(rq, rf, ri, rg, lb, rec_norm_g, s0)
    x = x + jnp.concatenate([att, rec.astype(att.dtype)], axis=-1) @ w_out
    h2 = rms_norm(x, norm2_g)
    x = x + jnp.square(jax.nn.relu(h2 @ w_up)) @ w_down
    return x, ak, av, ik, S_new


def setup_inputs(seed: int = 0) -> dict:
    key = jax.random.key(seed)
    ks = jax.random.split(key, 20)
    f32 = jnp.float32
    n_pages = PAST_LEN // PAGE_SIZE
    n_used = DEC_BATCH * n_pages
    n_pool = n_used + max(1, n_used // 4)

    def nrm(k, shape, scale=1.0):
        return jax.random.normal(k, shape, f32) * scale

    def gain(k, shape):
        return 1.0 + 0.05 * jax.random.normal(k, shape, f32)

    page_table = jax.random.permutation(ks[6], n_pool)[:n_used].reshape(DEC_BATCH, n_pages).astype(jnp.int32)
    return {
        "x_prompt": nrm(ks[0], (BATCH, SEQ, D_MODEL)),
        "x_sample": nrm(ks[1], (DEC_BATCH, DEC_SEQ, D_MODEL)),
        "cache_k": nrm(ks[2], (DEPTH, n_pool, PAGE_SIZE, ATT_KV_HEADS, ATT_HEAD_DIM)),
        "cache_v": nrm(ks[3], (DEPTH, n_pool, PAGE_SIZE, ATT_KV_HEADS, ATT_HEAD_DIM)),
        "cache_kidx": nrm(ks[4], (DEPTH, n_pool, PAGE_SIZE, IDX_DIM)),
        "state_hgrn": nrm(ks[5], (DEPTH, DEC_BATCH, REC_HEADS, REC_DIM, REC_DIM), 0.5),
        "page_table": page_table,
        "norm1_g": gain(ks[7], (DEPTH, D_MODEL)),
        "w_in": nrm(ks[8], (DEPTH, D_MODEL, IN_WIDTH), D_MODEL ** -0.5),
        "q_norm_g": gain(ks[9], (DEPTH, ATT_HEAD_DIM)),
        "k_norm_g": gain(ks[10], (DEPTH, ATT_HEAD_DIM)),
        "idx_k_norm_g": gain(ks[11], (DEPTH, IDX_DIM)),
        "lower_bounds": nrm(ks[12], (DEPTH + 1, REC_WIDTH), 0.1),
        "rec_norm_g": gain(ks[13], (DEPTH, REC_DIM)),
        "w_out": nrm(ks[14], (DEPTH, MIX_WIDTH, D_MODEL), MIX_WIDTH ** -0.5),
        "norm2_g": gain(ks[15], (DEPTH, D_MODEL)),
        "w_up": nrm(ks[16], (DEPTH, D_MODEL, D_FF), D_MODEL ** -0.5),
        "w_down": nrm(ks[17], (DEPTH, D_FF, D_MODEL), D_FF ** -0.5),
    }


def reference(x_prompt, x_sample, cache_k, cache_v, cache_kidx, state_hgrn, page_table,
              norm1_g, w_in, q_norm_g, k_norm_g, idx_k_norm_g, lower_bounds, rec_norm_g,
              w_out, norm2_g, w_up, w_down):
    lb_all = jnp.cumsum(jax.nn.softmax(lower_bounds.astype(jnp.float32), axis=0), axis=0)
    s0_prompt = jnp.zeros((x_prompt.shape[0], REC_HEADS, REC_DIM, REC_DIM), jnp.float32)
    yp, ys = x_prompt, x_sample
    kp_l, vp_l, ip_l, sp_l, ks_l, vs_l, is_l, ss_l = [], [], [], [], [], [], [], []
    for l in range(DEPTH):
        params = (norm1_g[l], w_in[l], q_norm_g[l], k_norm_g[l], idx_k_norm_g[l], rec_norm_g[l],
                  w_out[l], norm2_g[l], w_up[l], w_down[l])
        yp, kp, vp, ip, sp = trunk_layer(yp, s0_prompt, lb_all[l], prompt_attention, *params)
        attend_sample = functools.partial(sample_attention, cache_k=cache_k[l], cache_v=cache_v[l],
                                          cache_kidx=cache_kidx[l], page_table=page_table)
        ys, k_s, v_s, i_s, s_s = trunk_layer(ys, state_hgrn[l], lb_all[l], attend_sample, *params)
        kp_l.append(kp); vp_l.append(vp); ip_l.append(ip); sp_l.append(sp)
        ks_l.append(k_s); vs_l.append(v_s); is_l.append(i_s); ss_l.append(s_s)
    new_k_prompt = jnp.stack(kp_l)
    new_v_prompt = jnp.stack(vp_l)
    new_kidx_prompt = jnp.stack(ip_l)
    new_state_prompt = jnp.stack(sp_l)
    new_k_sample = jnp.stack(ks_l)
    new_v_sample = jnp.stack(vs_l)
    new_kidx_sample = jnp.stack(is_l)
    new_state_sample = jnp.stack(ss_l)
    return (yp, ys, new_k_prompt, new_v_prompt, new_kidx_prompt, new_state_prompt,
            new_k_sample, new_v_sample, new_kidx_sample, new_state_sample)
```

```python
import os
from contextlib import ExitStack
import numpy as np
import concourse.bass as bass
import concourse.mybir as mybir
from concourse.bass_utils import run_bass_kernel_spmd

F32 = mybir.dt.float32
BF16 = mybir.dt.bfloat16
I32 = mybir.dt.int32
U32 = mybir.dt.uint32
AF = mybir.ActivationFunctionType
ALU = mybir.AluOpType
AX = mybir.AxisListType

NCORES = 8
D = 1024
SEQ = 2048
NT = SEQ // 128
NSEQ = 2
NTOK = NSEQ * SEQ
INW = 3400
O_AQ, O_KVI, O_IQ, O_RQ, O_RF, O_RI, O_RG = 0, 512, 840, 1352, 1864, 2376, 2888
EPS = 1e-6
N_IT = int(os.environ.get("K_NIT", "13"))
NI_S = 21
NPOOL = 5120
BIGNEG = -1.0e30
SAMPLE_ATT = int(os.environ.get('K_SATT', '1'))
IDX_W_SCALE = (8 ** -0.5) * (64 ** -0.5)
NEG = -1.0e4
COMPUTE = ("pe", "act", "dve", "pool")
DBG_NT = int(os.environ.get('K_NT', '16'))
DBG_NSEQ = int(os.environ.get('K_NSEQ', '2'))
DBG_B = int(os.environ.get('K_B', '1'))
DBG_STOP = float(os.environ.get('K_STOP', '99'))


class StopBuild(Exception):
    pass


def stage(n):
    if n > DBG_STOP:
        raise StopBuild()


class Prog:
    def __init__(self, nc, stack, tag, n_dma_sems=10):
        self.nc = nc
        self.ops = {e: [] for e in ("pe", "act", "dve", "pool", "sp")}
        self.cnt = {e: 0 for e in COMPUTE}
        self.sem = {e: stack.enter_context(nc.semaphore(f"{tag}s_{e}")) for e in COMPUTE}
        self.dsem, self.dcnt, self.dnext = {}, {}, {}
        for q in ("sp", "act", "pool"):
            self.dsem[q] = [stack.enter_context(nc.semaphore(f"{tag}d_{q}{i}")) for i in range(n_dma_sems)]
            self.dcnt[q] = [0] * n_dma_sems
            self.dnext[q] = 0
        self.seen = {e: {} for e in self.ops}
        self.res = {}
        self.out_tokens = []
        self.rec = None
        self.threads = {}

    def _need(self, consumer, tok, waits):
        if tok is None:
            return
        key, val, prod = tok
        if prod == "pe" and consumer == "pe":
            return
        if self.seen[consumer].get(key, 0) >= val:
            return
        self.seen[consumer][key] = val
        waits.append((key, val))

    def _deps(self, consumer, reads, writes):
        waits = []
        for r in reads:
            st = self.res.get(r)
            if st:
                self._need(consumer, st["w"], waits)
        for w in writes:
            st = self.res.get(w)
            if st:
                self._need(consumer, st["w"], waits)
                for t in st["r"]:
                    self._need(consumer, t, waits)
        return waits

    def _commit(self, tok, reads, writes):
        for r in reads:
            st = self.res.setdefault(r, {"w": None, "r": []})
            st["r"].append(tok)
            if len(st["r"]) > 40:
                st["r"] = st["r"][-40:]
        for w in writes:
            self.res[w] = {"w": tok, "r": []}

    def begin(self, name):
        self.rec = self.threads.setdefault(name, [])

    def end(self):
        self.rec = None

    def mark(self):
        if self.rec is not None:
            self.rec.append(("mark",))

    def merge(self, a, b):
        la, lb = self.threads.pop(a, []), self.threads.pop(b, [])
        self.rec = None
        marks = [i for i, it in enumerate(la) if it[0] == "mark"]
        mode = os.environ.get("K_MERGE", "4")
        if len(marks) >= 2 and mode == "0":
            m1, m2 = 0, marks[1]
        elif len(marks) >= 2 and mode == "1":
            m1, m2 = marks[0], len(la)
        elif len(marks) >= 2 and mode == "4":
            m1, m2 = marks[0], marks[1]
        elif len(marks) >= 2 and mode == "5":
            m1, m2 = max(0, marks[0] - 40), marks[1]
        elif len(marks) >= 2 and mode == "3":
            m1, m2 = marks[1], len(la)
        else:
            m1, m2 = 0, len(la)
        seg = [it for it in la[m1:m2] if it[0] != "mark"]
        for it in la[:m1]:
            if it[0] != "mark":
                self._replay(it)
        na, nb = len(seg), len(lb)
        ib = 0
        for ia, item in enumerate(seg):
            self._replay(item)
            want = ((ia + 1) * nb) // max(1, na)
            while ib < want:
                self._replay(lb[ib])
                ib += 1
        while ib < nb:
            self._replay(lb[ib])
            ib += 1
        for it in la[m2:]:
            if it[0] != "mark":
                self._replay(it)

    def _replay(self, item):
        if item[0] == "op":
            self.op(item[1], item[2], item[3], item[4])
        else:
            self.dma(item[1], item[2], item[3], item[4], item[5])

    def op(self, e, fn, reads=(), writes=()):
        reads, writes = list(reads), list(writes)
        if self.rec is not None:
            self.rec.append(("op", e, fn, reads, writes))
            return None
        waits = self._deps(e, reads, writes)
        self.cnt[e] += 1
        tok = (("c", e), self.cnt[e], e)
        self.ops[e].append((waits, fn, ("c", e)))
        self._commit(tok, reads, writes)
        return tok

    def dma(self, q, fn, reads=(), writes=(), is_output=False):
        reads, writes = list(reads), list(writes)
        if self.rec is not None:
            self.rec.append(("dma", q, fn, reads, writes, is_output))
            return None
        waits = self._deps(q, reads, writes)
        i = self.dnext[q]
        self.dnext[q] = (i + 1) % len(self.dsem[q])
        prev = self.dcnt[q][i]
        key = ("d", q, i)
        if prev > 0 and self.seen[q].get(key, 0) < prev:
            self.seen[q][key] = prev
            waits.append((key, prev))
        self.dcnt[q][i] = prev + 16
        tok = (key, prev + 16, "dma")
        self.ops[q].append((waits, fn, key))
        self._commit(tok, reads, writes)
        if is_output:
            self.out_tokens.append(tok)
        return tok

    def _semh(self, key):
        return self.sem[key[1]] if key[0] == "c" else self.dsem[key[1]][key[2]]

    def finish(self):
        for c in self.ops:
            waits = []
            for e in COMPUTE:
                if self.cnt[e] and e != c:
                    self._need(c, (("c", e), self.cnt[e], e), waits)
            for q in self.dsem:
                for i, v in enumerate(self.dcnt[q]):
                    if v:
                        self._need(c, (("d", q, i), v, "dma"), waits)
            self.ops[c].append((waits, None, None))

    def emit(self, block):
        prog = self

        def run(ename):
            def body(eng):
                for waits, fn, key in prog.ops[ename]:
                    for k, v in waits:
                        eng.wait_ge(prog._semh(k), v)
                    if fn is None:
                        continue
                    ins = fn(eng)
                    ins.then_inc(prog._semh(key), 1 if key[0] == "c" else 16)
            return body

        block.tensor(run("pe"))
        block.scalar(run("act"))
        block.vector(run("dve"))
        block.gpsimd(run("pool"))
        block.sync(run("sp"))


def bc(ap, axis, shape):
    return ap.unsqueeze(axis).to_broadcast(list(shape))


def build(with_sample=bool(int(os.environ.get("K_S", "1")))):
    nc = bass.Bass("TRN2", target_bir_lowering=False)

    def din(name, shape, dt=F32):
        return nc.dram_tensor(name, list(shape), dt, kind="ExternalInput").ap()

    def dout(name, shape, dt=F32):
        return nc.dram_tensor(name, list(shape), dt, kind="ExternalOutput").ap()

    xp = din("xp", [NTOK, D])
    w_in = din("w_in", [D, INW])
    w_out = din("w_out", [D, D])
    w_up = din("w_up", [D, 4 * D])
    w_down = din("w_down", [4 * D, D])
    c_ident = din("c_ident", [128, 128])
    c_mb = din("c_mb", [128, 128])
    c_ind = din("c_ind", [128, 2])
    c_ct = din("c_ct", [128, 128])
    c_negc = din("c_negc", [128, 128])
    c_p2 = din("c_p2", [128, N_IT + 1])
    p_g1T = din("p_g1T", [128, 8])
    p_g2T = din("p_g2T", [128, 8])
    p_qg2 = din("p_qg2", [128, 1])
    p_kig = din("p_kig", [128, 192])
    p_rng = din("p_rng", [128, 512])
    p_lb = din("p_lb", [128, 1024])

    xs_d = din("xs", [4, D])
    state_d = din("state_s", [4, 4, 128, 128])
    ptT_d = din("ptT", [128, 4], I32)
    ckidx_d = din("cache_kidx", [NPOOL, 8192])
    ckv_d = din("cache_kv", [NPOOL * 128, 256])
    c_selb = din("c_selb", [4, 4, 128])
    c_eye4 = din("c_eye4", [4, 4])
    c_oh4 = din("c_oh4", [128, 16])
    c_negeye = din("c_negeye", [4, 4])
    c_hsel = din("c_hsel", [8, 2])
    c_eye8 = din("c_eye8", [8, 8])
    c_sel8 = din("c_sel8", [8, 4, 4])
    c_ones = din("c_ones", [128, 128])
    c_p2s = din("c_p2s", [128, NI_S + 1])
    p_qgb = din("p_qgb", [128, 512])
    ys_d = dout("ys", [4, D])
    nks_d = dout("nks", [4, 128])
    nvs_d = dout("nvs", [4, 128])
    nkis_d = dout("nkis", [4, 64])
    nsts_d = dout("nsts", [4, 4, 128, 128])
    yp = dout("yp", [NTOK, D])
    nk = dout("nk", [NTOK, 128])
    nv = dout("nv", [NTOK, 128])
    nki = dout("nki", [NTOK, 64])
    nst = dout("nst", [NSEQ, 4, 128, 128])

    with ExitStack() as st0:
        def sb(stack, name, shape, dt):
            return stack.enter_context(nc.sbuf_tensor(name, list(shape), dt))

        identb = sb(st0, "identb", [128, 128], BF16)
        identf = sb(st0, "identf", [128, 128], F32)
        mbf = sb(st0, "mbf", [128, 128], F32)
        indf = sb(st0, "indf", [128, 2], F32)
        ctf = sb(st0, "ctf", [128, 128], F32)
        negc = sb(st0, "negc", [128, 128], F32)
        p2 = sb(st0, "p2", [128, N_IT + 1], F32)
        g1T = sb(st0, "g1T", [128, 8], F32)
        g2T = sb(st0, "g2T", [128, 8], F32)
        qg2 = sb(st0, "qg2", [128, 1], F32)
        kig = sb(st0, "kig", [128, 192], F32)
        rngb = sb(st0, "rngb", [128, 512], F32)
        lbt = sb(st0, "lbt", [128, 512], F32)
        oml = sb(st0, "oml", [128, 512], F32)
        PS = [st0.enter_context(nc.psum_tensor(f"ps{i}", [128, 512], F32)) for i in range(8)]
        PSB = [p[:].bitcast(BF16) for p in PS]
        P = Prog(nc, st0, "A")

        lb0t = sb(st0, "lb0t", [128, 512], F32)
        lb1t = sb(st0, "lb1t", [128, 512], F32)
        xmid_s = sb(st0, "xmid_s", [4, D], F32)
        stW = st0.enter_context(ExitStack())
        wi = sb(stW, "wi", [128, 8, INW], BF16)
        wo = sb(stW, "wo", [128, 8, D], BF16)
        epsc = sb(stW, "epsc", [128, 1], F32)
        eps_ap = epsc[:, 0:1]
        for c in range(8):
            P.dma("pool", lambda e, c=c: e.dma_start(out=wi[:, c, :], in_=w_in[c * 128:(c + 1) * 128, :]), writes=[f"wi{c}"])
        for c in range(8):
            P.dma("pool", lambda e, c=c: e.dma_start(out=wo[:, c, :], in_=w_out[c * 128:(c + 1) * 128, :]), writes=[f"wo{c}"])
        P.dma("pool", lambda e: e.dma_start(out=identb[:], in_=c_ident), writes=["identb"])
        for dst, src, nm in ((identf, c_ident, "identf"), (mbf, c_mb, "mbf"), (indf, c_ind, "indf"), (ctf, c_ct, "ctf"),
                             (negc, c_negc, "negc"), (p2, c_p2, "p2"), (g1T, p_g1T, "g1T"), (g2T, p_g2T, "g2T"),
                             (qg2, p_qg2, "qg2"), (kig, p_kig, "kig"), (rngb, p_rng, "rngb")):
            P.dma("sp", lambda e, dst=dst, src=src: e.dma_start(out=dst[:], in_=src), writes=[nm])
        P.dma("sp", lambda e: e.dma_start(out=lb0t[:], in_=p_lb[:, 0:512]), writes=["lb0t"])
        P.dma("sp", lambda e: e.dma_start(out=lb1t[:], in_=p_lb[:, 512:1024]), writes=["lb1t"])
        P.op("dve", lambda e: e.tensor_tensor(out=lb1t[:], in0=lb1t[:], in1=lb0t[:], op=ALU.subtract), reads=["lb0t", "lb1t"], writes=["lb1t"])
        P.op("act", lambda e: e.activation(out=lb1t[:], in_=lb1t[:], func=AF.Exp), reads=["lb1t"], writes=["lb1t"])
        P.op("dve", lambda e: e.tensor_scalar(out=lb1t[:], in0=lb1t[:], scalar1=1.0, scalar2=None, op0=ALU.add), reads=["lb1t"], writes=["lb1t"])
        P.op("dve", lambda e: e.reciprocal(out=lbt[:], in_=lb1t[:]), reads=["lb1t"], writes=["lbt"])
        P.op("dve", lambda e: e.tensor_scalar(out=oml[:], in0=lbt[:], scalar1=-1.0, scalar2=1.0, op0=ALU.mult, op1=ALU.add), reads=["lbt"], writes=["oml"])
        P.op("pool", lambda e: e.memset(epsc[:], EPS), writes=["epsc"])

        def rstd_from(ss_ap, out_ap, n, names):
            P.op("act", lambda e: e.activation(out=out_ap, in_=ss_ap, func=AF.Ln, scale=1.0 / n, bias=eps_ap[0:ss_ap.shape[0], :]), reads=names + ["epsc"], writes=names)
            P.op("act", lambda e: e.activation(out=out_ap, in_=out_ap, func=AF.Exp, scale=-0.5), reads=names, writes=names)


        def emit_sample(st):
            R4 = slice(0, 4)
            xts = sb(st, "xts", [4, D], F32)
            mixs = sb(st, "mixs", [4, D], F32)
            mixsb = sb(st, "mixsb", [4, D], BF16)
            mixTs = sb(st, "mixTs", [128, 8, 4], BF16)
            sms = sb(st, "sms", [128, 64], F32)
            kiks = sb(st, "kiks", [4, 192], F32)
            vfs = sb(st, "vfs", [4, 128], F32)
            qs = sb(st, "qs", [4, 512], F32)
            iqs = sb(st, "iqs", [4, 512], F32)
            selb = sb(st, "selb", [4, 4, 128], F32)
            eye4 = sb(st, "eye4", [4, 4], F32)
            oh4 = sb(st, "oh4", [128, 16], F32)
            negeye = sb(st, "negeye", [4, 4], F32)
            hsel = sb(st, "hsel", [8, 2], F32)
            eye8 = sb(st, "eye8", [8, 8], F32)
            sel8 = sb(st, "sel8", [8, 4, 4], F32)
            ones = sb(st, "ones", [128, 128], F32)
            p2s = sb(st, "p2s", [128, NI_S + 1], F32)
            qgb = sb(st, "qgb", [4, 512], F32)
            st1 = ExitStack()
            xns = sb(st1, "xns", [4, D], BF16)
            jks = sb(st1, "jks", [4, D], BF16)
            hTs = sb(st1, "hTs", [128, 8, 4], BF16)
            sqs = sb(st1, "sqs", [4, 704], F32)
            tS = [sb(st1, f"tS{i}", [4, 512], F32) for i in range(8)]
            fkq = sb(st1, "fkq", [128, 12, 4], F32)
            qsel = sb(st1, "qsel", [128, 4, 4, 4], F32)
            S0 = [sb(st1, f"S0{i}", [128, 4, 128], F32) for i in range(2)]
            Sn = [sb(st1, f"Sn{i}", [128, 4, 128], F32) for i in range(2)]
            tmpS = sb(st1, "tmpS", [128, 4, 128], F32)
            o_all = sb(st1, "o_all", [4, 512], F32)
            for dst, src, nm in ((selb, c_selb, "selb"), (eye4, c_eye4, "eye4"), (oh4, c_oh4, "oh4"), (negeye, c_negeye, "negeye"),
                                 (hsel, c_hsel, "hsel"), (eye8, c_eye8, "eye8"), (sel8, c_sel8, "sel8"), (ones, c_ones, "ones"),
                                 (p2s, c_p2s, "p2s"), (qgb, p_qgb[0:4, :], "qgb"), (xts, xs_d, "xts")):
                P.dma("sp", lambda e, dst=dst, src=src: e.dma_start(out=dst[:], in_=src), writes=[nm])
            P.op("act", lambda e: e.activation(out=jks[:], in_=xts[:], func=AF.Square, accum_out=sms[R4, 0:1]), reads=["xts"], writes=["jks", "s0"])
            rstd_from(sms[R4, 0:1], sms[R4, 1:2], D, ["s0"])
            P.op("act", lambda e: e.activation(out=xns[:], in_=xts[:], func=AF.Copy, scale=sms[R4, 1:2]), reads=["xts", "s0"], writes=["xns"])
            for c in range(8):
                P.op("pe", lambda e, c=c: e.transpose(PSB[0][:, c * 128:c * 128 + 4], xns[R4, c * 128:(c + 1) * 128], identb[0:4, 0:4]),
                     reads=["xns", "identb"], writes=["ps0"])
            P.op("dve", lambda e: e.tensor_tensor(out=hTs[:], in0=PSB[0][:].rearrange("p (c n) -> p c n", c=8)[:, :, 0:4],
                                                  in1=bc(g1T[:], 2, [128, 8, 4]), op=ALU.mult), reads=["ps0", "g1T"], writes=["hTs"])
            def proj_s(bank, off, width):
                for c in range(8):
                    P.op("pe", lambda e, c=c: e.matmul(PS[bank][R4, 0:width], lhsT=hTs[:, c, :], rhs=wi[:, c, off:off + width],
                                                       start=(c == 0), stop=(c == 7)), reads=["hTs", f"wi{c}"], writes=[f"ps{bank}"])
            proj_s(2, O_KVI, 328)
            proj_s(1, O_AQ, 512)
            proj_s(3, O_IQ, 512)
            proj_s(5, O_RF, 512)
            proj_s(4, O_RQ, 512)
            proj_s(7, O_RG, 512)
            proj_s(6, O_RI, 512)
            P.op("act", lambda e: e.activation(out=sqs[:, 0:192], in_=PS[2][R4, 0:192], func=AF.Square), reads=["ps2"], writes=["sqs"])
            P.op("act", lambda e: e.activation(out=sqs[:, 192:704], in_=PS[1][R4, 0:512], func=AF.Square), reads=["ps1"], writes=["sqs"])
            P.op("dve", lambda e: e.tensor_reduce(out=sms[R4, 8:19], in_=sqs[:].rearrange("p (a b) -> p a b", b=64), axis=AX.X, op=ALU.add),
                 reads=["sqs"], writes=["s8"])
            rstd_from(sms[R4, 8:19], sms[R4, 8:19], 64, ["s8"])
            for a in range(3):
                P.op("act", lambda e, a=a: e.activation(out=kiks[:, a * 64:(a + 1) * 64], in_=PS[2][R4, a * 64:(a + 1) * 64], func=AF.Copy, scale=sms[R4, 8 + a:9 + a]),
                     reads=["ps2", "s8"], writes=["kiks"])
            P.op("dve", lambda e: e.tensor_tensor(out=kiks[:], in0=kiks[:], in1=kig[R4, :], op=ALU.mult), reads=["kiks", "kig"], writes=["kiks"])
            P.op("act", lambda e: e.activation(out=vfs[:], in_=PS[2][R4, 192:320], func=AF.Copy), reads=["ps2"], writes=["vfs"])
            P.op("act", lambda e: e.activation(out=sms[R4, 24:32], in_=PS[2][R4, 320:328], func=AF.Copy, scale=IDX_W_SCALE), reads=["ps2"], writes=["s24"])
            P.dma("sp", lambda e: e.dma_start(out=nks_d, in_=kiks[:, 0:128]), reads=["kiks"], is_output=True)
            P.dma("sp", lambda e: e.dma_start(out=nkis_d, in_=kiks[:, 128:192]), reads=["kiks"], is_output=True)
            P.dma("sp", lambda e: e.dma_start(out=nvs_d, in_=vfs[:]), reads=["vfs"], is_output=True)
            for a in range(8):
                P.op("act", lambda e, a=a: e.activation(out=qs[:, a * 64:(a + 1) * 64], in_=PS[1][R4, a * 64:(a + 1) * 64], func=AF.Copy, scale=sms[R4, 11 + a:12 + a]),
                     reads=["ps1", "s8"], writes=["qs"])
            P.op("dve", lambda e: e.tensor_tensor(out=qs[:], in0=qs[:], in1=qgb[:], op=ALU.mult), reads=["qs", "qgb"], writes=["qs"])
            P.op("act", lambda e: e.activation(out=iqs[:], in_=PS[3][R4, :], func=AF.Copy), reads=["ps3"], writes=["iqs"])
            t_ef, t_eq, t_eg, t_f, t_k, t_q, t_gate, t_v = tS
            n_ef, n_eq, n_eg, n_f, n_k, n_q, n_gate, n_v = [f"tS{i}" for i in range(8)]
            P.op("act", lambda e: e.activation(out=t_ef[:], in_=PS[5][R4, :], func=AF.Exp, scale=-1.0), reads=["ps5"], writes=[n_ef])
            P.op("act", lambda e: e.activation(out=t_eq[:], in_=PS[4][R4, :], func=AF.Exp, scale=-1.0), reads=["ps4"], writes=[n_eq])
            P.op("act", lambda e: e.activation(out=t_eg[:], in_=PS[7][R4, :], func=AF.Exp, scale=-1.0), reads=["ps7"], writes=[n_eg])
            P.op("act", lambda e: e.activation(out=t_v[:], in_=PS[6][R4, :], func=AF.Copy), reads=["ps6"], writes=[n_v])
            for tt, nn in ((t_ef, n_ef), (t_eq, n_eq), (t_eg, n_eg)):
                P.op("dve", lambda e, tt=tt: e.tensor_scalar(out=tt[:], in0=tt[:], scalar1=1.0, scalar2=None, op0=ALU.add), reads=[nn], writes=[nn])
                P.op("dve", lambda e, tt=tt: e.reciprocal(out=tt[:], in_=tt[:]), reads=[nn], writes=[nn])
            P.op("dve", lambda e: e.tensor_tensor(out=t_f[:], in0=t_ef[:], in1=oml[R4, :], op=ALU.mult), reads=[n_ef, "oml"], writes=[n_f])
            P.op("dve", lambda e: e.tensor_tensor(out=t_f[:], in0=t_f[:], in1=lbt[R4, :], op=ALU.add), reads=[n_f, "lbt"], writes=[n_f])
            P.op("dve", lambda e: e.tensor_scalar(out=t_k[:], in0=t_f[:], scalar1=-1.0, scalar2=1.0, op0=ALU.mult, op1=ALU.add), reads=[n_f], writes=[n_k])
            P.op("act", lambda e: e.activation(out=t_q[:], in_=PS[4][R4, :], func=AF.Copy, scale=128 ** -0.5), reads=["ps4"], writes=[n_q])
            P.op("dve", lambda e: e.tensor_tensor(out=t_q[:], in0=t_q[:], in1=t_eq[:], op=ALU.mult), reads=[n_q, n_eq], writes=[n_q])
            P.op("act", lambda e: e.activation(out=t_gate[:], in_=PS[7][R4, :], func=AF.Copy), reads=["ps7"], writes=[n_gate])
            P.op("dve", lambda e: e.tensor_tensor(out=t_gate[:], in0=t_gate[:], in1=t_eg[:], op=ALU.mult), reads=[n_gate, n_eg], writes=[n_gate])
            for gi, (tt, nn) in enumerate(((t_f, n_f), (t_k, n_k), (t_q, n_q))):
                for h in range(4):
                    P.op("pe", lambda e, gi=gi, h=h, tt=tt: e.transpose(PS[4][:, (gi * 4 + h) * 4:(gi * 4 + h) * 4 + 4], tt[R4, h * 128:(h + 1) * 128], identf[0:4, 0:4]),
                         reads=[nn, "identf"], writes=["ps4"])
            P.op("act", lambda e: e.activation(out=fkq[:].rearrange("p a b -> p (a b)"), in_=PS[4][:, 0:48], func=AF.Copy), reads=["ps4"], writes=["fkq"])
            P.op("dve", lambda e: e.tensor_tensor(out=qsel[:], in0=fkq[:, 8:12, :].unsqueeze(1).to_broadcast([128, 4, 4, 4]),
                                                  in1=oh4[:].rearrange("p (b t) -> p b t", b=4).unsqueeze(2).to_broadcast([128, 4, 4, 4]), op=ALU.mult),
                 reads=["fkq", "oh4"], writes=["qsel"])
            P.op("dve", lambda e: e.memset(o_all[:], 0.0), writes=["o_all"])
            for b in range(4):
                s0, s0n = S0[b % 2], f"S0{b % 2}"
                sn, snn = Sn[b % 2], f"Sn{b % 2}"
                P.dma("sp", lambda e, b=b, s0=s0: e.dma_start(out=s0[:], in_=state_d[b].rearrange("h k v -> k h v")), writes=[s0n])
                P.op("pe", lambda e, b=b: e.matmul(PS[5][:], lhsT=selb[:, b, :], rhs=t_v[:], start=True, stop=True), reads=["selb", n_v], writes=["ps5"])
                for h in range(4):
                    P.op("act", lambda e, h=h, b=b: e.activation(out=tmpS[:, h, :], in_=PS[5][:, h * 128:(h + 1) * 128], func=AF.Copy, scale=fkq[:, 4 + h, b:b + 1]),
                         reads=["ps5", "fkq"], writes=["tmpS"])
                    P.op("dve", lambda e, h=h, b=b, s0=s0, sn=sn: e.scalar_tensor_tensor(out=sn[:, h, :], in0=s0[:, h, :], scalar=fkq[:, h, b:b + 1], in1=tmpS[:, h, :],
                                                                                 op0=ALU.mult, op1=ALU.add), reads=[s0n, "fkq", "tmpS"], writes=[snn])
                P.dma("sp", lambda e, b=b, sn=sn: e.dma_start(out=nsts_d[b].rearrange("h k v -> k h v"), in_=sn[:]), reads=[snn], is_output=True)
                for h in range(4):
                    P.op("pe", lambda e, h=h, b=b, sn=sn: e.matmul(PS[6][R4, h * 128:(h + 1) * 128], lhsT=qsel[:, b, h, :], rhs=sn[:, h, :], start=True, stop=True),
                         reads=["qsel", snn], writes=["ps6"])
                P.op("dve", lambda e: e.tensor_tensor(out=o_all[:], in0=PS[6][R4, :], in1=o_all[:], op=ALU.add), reads=["ps6", "o_all"], writes=["o_all"])
            P.op("act", lambda e: e.activation(out=t_ef[:], in_=o_all[:], func=AF.Square), reads=["o_all"], writes=[n_ef])
            P.op("dve", lambda e: e.tensor_reduce(out=sms[R4, 44:48], in_=t_ef[:].rearrange("p (a b) -> p a b", b=128), axis=AX.X, op=ALU.add), reads=[n_ef], writes=["s44"])
            rstd_from(sms[R4, 44:48], sms[R4, 44:48], 128, ["s44"])
            for h in range(4):
                P.op("act", lambda e, h=h: e.activation(out=mixs[:, 512 + h * 128:512 + (h + 1) * 128], in_=o_all[:, h * 128:(h + 1) * 128], func=AF.Copy, scale=sms[R4, 44 + h:45 + h]),
                     reads=["o_all", "s44"], writes=["mixs_r"])
            P.op("dve", lambda e: e.tensor_tensor(out=mixs[:, 512:1024], in0=mixs[:, 512:1024], in1=rngb[R4, :], op=ALU.mult), reads=["mixs_r", "rngb"], writes=["mixs_r"])
            P.op("dve", lambda e: e.tensor_tensor(out=mixs[:, 512:1024], in0=mixs[:, 512:1024], in1=t_gate[:], op=ALU.mult), reads=["mixs_r", n_gate], writes=["mixs_r"])
            P.finish()
            st1.close()
            if SAMPLE_ATT:
                emit_sample_att(st, dict(sms=sms, kiks=kiks, vfs=vfs, qs=qs, iqs=iqs, selb=selb, eye4=eye4, negeye=negeye, hsel=hsel, eye8=eye8,
                                         sel8=sel8, ones=ones, p2s=p2s, mixs=mixs))
            else:
                P.op("dve", lambda e: e.memset(mixs[:, 0:512], 0.0), writes=["mixs_a"])
            P.op("act", lambda e: e.activation(out=mixsb[:], in_=mixs[:], func=AF.Copy), reads=["mixs_a", "mixs_r"], writes=["mixsb"])
            for c in range(8):
                P.op("pe", lambda e, c=c: e.transpose(PSB[0][:, c * 128:c * 128 + 4], mixsb[R4, c * 128:(c + 1) * 128], identb[0:4, 0:4]),
                     reads=["mixsb", "identb"], writes=["ps0"])
            P.op("act", lambda e: e.activation(out=mixTs[:], in_=PSB[0][:].rearrange("p (c n) -> p c n", c=8)[:, :, 0:4], func=AF.Copy), reads=["ps0"], writes=["mixTs"])
            for half in range(2):
                ob = 1 + half
                for c in range(8):
                    P.op("pe", lambda e, c=c, half=half, ob=ob: e.matmul(PS[ob][R4, :], lhsT=mixTs[:, c, :], rhs=wo[:, c, half * 512:(half + 1) * 512],
                                                                         start=(c == 0), stop=(c == 7)), reads=["mixTs", f"wo{c}"], writes=[f"ps{ob}"])
                P.op("dve", lambda e, half=half, ob=ob: e.tensor_tensor(out=xmid_s[:, half * 512:(half + 1) * 512], in0=PS[ob][R4, :],
                                                                       in1=xts[:, half * 512:(half + 1) * 512], op=ALU.add), reads=[f"ps{ob}", "xts"], writes=["xmid_s"])


        def emit_sample_att(st, T):
            R4 = slice(0, 4)
            sms, kiks, vfs, qs, iqs = T["sms"], T["kiks"], T["vfs"], T["qs"], T["iqs"]
            selb, eye4, negeye, hsel, eye8, sel8, ones, p2s, mixs = (T[k] for k in ("selb", "eye4", "negeye", "hsel", "eye8", "sel8", "ones", "p2s", "mixs"))
            pti = sb(st, "pti", [128, 4], I32)
            pt128 = sb(st, "pt128", [128, 4], F32)
            G = sb(st, "G", [128, 8192], F32)
            ikTa = [sb(st, f"ikTa{i}", [64, 4, 128], BF16) for i in range(2)]
            IQs = sb(st, "IQs", [64, 8, 4], BF16)
            Rr = sb(st, "Rr", [128, 1024], F32)
            sc = sb(st, "sc", [128, 4, 129], F32)
            sc2 = sb(st, "sc2", [128, 128], F32)
            cmpb = sb(st, "cmpb", [128, 4, 129], F32)
            wbc = sb(st, "wbc", [128, 4, 8], F32)
            hfs = sb(st, "hfs", [128, 4, NI_S + 1], F32)
            s2 = sb(st, "s2", [128, 32], F32)
            tp = sb(st, "tp", [4, 512], F32)
            v16 = sb(st, "v16", [128, 16], F32)
            i16 = sb(st, "i16", [128, 16], U32)
            idxf = sb(st, "idxf", [128, 16], F32)
            rowi = sb(st, "rowi", [128, 16], I32)
            valid = sb(st, "valid", [128, 17], F32)
            KVs = sb(st, "KVs", [128, 17, 257], F32)
            qb = sb(st, "qb", [128, 512], F32)
            prodL = sb(st, "prodL", [128, 17, 64], F32)
            lg = sb(st, "lg", [128, 17, 8], F32)
            Pm = sb(st, "Pm", [128, 17, 8], F32)
            o8 = sb(st, "o8", [8, 130], F32)
            a8 = sb(st, "a8", [8, 64], F32)
            Y = sb(st, "Y", [8, 8, 64], F32)
            mid = s2[:, 0:4]
            midm = s2[:, 4:8]
            cnt4 = s2[:, 8:12]
            tot = s2[:, 12:16]
            ge2 = s2[:, 16:20]
            tmp4 = s2[:, 20:24]
            m4 = s2[:, 24:28]
            mtot = s2[:, 28:32]
            P.dma("sp", lambda e: e.dma_start(out=pti[:], in_=ptT_d), writes=["pti"])
            P.op("dve", lambda e: e.tensor_copy(out=pt128[:], in_=pti[:]), reads=["pti"], writes=["pt128"])
            P.op("dve", lambda e: e.tensor_scalar(out=pt128[:], in0=pt128[:], scalar1=128.0, scalar2=None, op0=ALU.mult), reads=["pt128"], writes=["pt128"])
            P.op("pool", lambda e: e.memset(mixs[:, 0:512], 0.0), writes=["mixs_a"])
            P.op("pool", lambda e: e.memset(KVs[:], 0.0), writes=["KVs"])
            P.op("pool", lambda e: e.memset(KVs[:, :, 256:257], 1.0), reads=["KVs"], writes=["KVs"])
            P.op("dve", lambda e: e.tensor_copy(out=KVs[R4, 16, 0:128], in_=kiks[:, 0:128]), reads=["kiks", "KVs"], writes=["KVs"])
            P.op("dve", lambda e: e.tensor_copy(out=KVs[R4, 16, 128:256], in_=vfs[:]), reads=["vfs", "KVs"], writes=["KVs"])
            for h in range(8):
                P.op("pe", lambda e, h=h: e.transpose(PS[1][0:64, h * 4:(h + 1) * 4], iqs[R4, h * 64:(h + 1) * 64], identf[0:4, 0:4]),
                     reads=["iqs", "identf"], writes=["ps1"])
            P.op("act", lambda e: e.activation(out=IQs[:].rearrange("p a b -> p (a b)"), in_=PS[1][0:64, 0:32], func=AF.Copy), reads=["ps1"], writes=["IQs"])
            for b in range(4):
                P.op("pe", lambda e, b=b: e.matmul(PS[7][:, b * 8:(b + 1) * 8], lhsT=selb[:, b, :], rhs=sms[R4, 24:32], start=True, stop=True),
                     reads=["selb", "s24"], writes=["ps7"])
            P.op("act", lambda e: e.activation(out=wbc[:].rearrange("p a b -> p (a b)"), in_=PS[7][:, 0:32], func=AF.Copy), reads=["ps7"], writes=["wbc"])
            P.op("dve", lambda e: e.tensor_tensor(out=tp[:].rearrange("p (h d) -> p h d", d=64), in0=iqs[:].rearrange("p (h d) -> p h d", d=64),
                                                  in1=kiks[:, 128:192].unsqueeze(1).to_broadcast([4, 8, 64]), op=ALU.mult), reads=["iqs", "kiks"], writes=["tp"])
            P.op("dve", lambda e: e.tensor_reduce(out=sms[R4, 32:40], in_=tp[:].rearrange("p (h d) -> p h d", d=64), axis=AX.X, op=ALU.add), reads=["tp"], writes=["s32"])
            P.op("dve", lambda e: e.tensor_scalar(out=sms[R4, 32:40], in0=sms[R4, 32:40], scalar1=0.0, scalar2=None, op0=ALU.max), reads=["s32"], writes=["s32"])
            P.op("dve", lambda e: e.tensor_tensor(out=sms[R4, 32:40], in0=sms[R4, 32:40], in1=sms[R4, 24:32], op=ALU.mult), reads=["s32", "s24"], writes=["s32"])
            P.op("dve", lambda e: e.tensor_reduce(out=sms[R4, 40:41], in_=sms[R4, 32:40], axis=AX.X, op=ALU.add), reads=["s32"], writes=["s40"])
            for b in range(4):
                P.dma("pool", lambda e, b=b: e.indirect_dma_start(out=G[:], out_offset=None, in_=ckidx_d,
                                                                in_offset=bass.IndirectOffsetOnAxis(ap=pti[:, b:b + 1], axis=0)),
                      reads=["pti"], writes=["G"])
                for grp in range(32):
                    tb = 2 + grp % 2
                    ib = grp % 2
                    for oo in range(4):
                        o = grp * 4 + oo
                        P.op("pe", lambda e, o=o, oo=oo, tb=tb: e.transpose(PS[tb][0:64, oo * 128:(oo + 1) * 128], G[:, o * 64:(o + 1) * 64], identf[:]),
                             reads=["G", "identf"], writes=[f"ps{tb}"])
                    P.op("act", lambda e, tb=tb, ib=ib: e.activation(out=ikTa[ib][:].rearrange("p a b -> p (a b)"), in_=PS[tb][0:64, :], func=AF.Copy),
                         reads=[f"ps{tb}"], writes=[f"ikTa{ib}"])
                    for oo in range(4):
                        o = grp * 4 + oo
                        sbk = 4 + o // 64
                        P.op("pe", lambda e, o=o, oo=oo, ib=ib, sbk=sbk, b=b: e.matmul(PS[sbk][:, (o % 64) * 8:(o % 64 + 1) * 8], lhsT=ikTa[ib][:, oo, :], rhs=IQs[:, :, b],
                                                                                      start=True, stop=True), reads=[f"ikTa{ib}", "IQs"], writes=[f"ps{sbk}"])
                P.op("act", lambda e: e.activation(out=Rr[:, 0:512], in_=PS[4][:], func=AF.Relu), reads=["ps4"], writes=["Rr"])
                P.op("act", lambda e: e.activation(out=Rr[:, 512:1024], in_=PS[5][:], func=AF.Relu), reads=["ps5"], writes=["Rr"])
                P.op("dve", lambda e, b=b: e.tensor_tensor(out=Rr[:].rearrange("p (o h) -> p o h", h=8), in0=Rr[:].rearrange("p (o h) -> p o h", h=8),
                                                           in1=wbc[:, b, :].unsqueeze(1).to_broadcast([128, 128, 8]), op=ALU.mult), reads=["Rr", "wbc"], writes=["Rr"])
                P.op("dve", lambda e, b=b: e.tensor_reduce(out=sc[:, b, 0:128], in_=Rr[:].rearrange("p (o h) -> p o h", h=8), axis=AX.X, op=ALU.add),
                     reads=["Rr"], writes=["sc"])
                P.op("dve", lambda e, b=b: e.memset(sc[:, b, 128:129], BIGNEG), reads=["sc"], writes=["sc"])
                P.op("dve", lambda e, b=b: e.scalar_tensor_tensor(out=sc[R4, b, 128:129], in0=sms[R4, 40:41], scalar=eye4[:, b:b + 1], in1=negeye[:, b:b + 1],
                                                                  op0=ALU.mult, op1=ALU.add), reads=["s40", "eye4", "negeye", "sc"], writes=["sc"])
            P.op("dve", lambda e: e.tensor_reduce(out=m4, in_=sc[:, :, 0:128], axis=AX.X, op=ALU.max, apply_absolute_value=True), reads=["sc"], writes=["m4"])
            P.op("pe", lambda e: e.transpose(PS[1][R4, 0:128], m4, identf[:]), reads=["identf", "m4"], writes=["ps1"])
            P.op("act", lambda e: e.activation(out=tp[:, 0:128], in_=PS[1][R4, 0:128], func=AF.Copy), reads=["ps1"], writes=["tp"])
            P.op("dve", lambda e: e.tensor_reduce(out=sms[R4, 41:42], in_=tp[:, 0:128], axis=AX.X, op=ALU.max), reads=["tp"], writes=["s41"])
            P.op("dve", lambda e: e.tensor_scalar(out=tp[:, 128:132], in0=eye4[:], scalar1=sms[R4, 41:42], scalar2=None, op0=ALU.mult), reads=["s41", "eye4"], writes=["tp"])
            P.op("pe", lambda e: e.matmul(PS[1][:, 0:4], lhsT=selb[:].rearrange("t b p -> t (b p)")[:, 0:128] if False else ones[R4, :], rhs=tp[:, 128:132], start=True, stop=True),
                 reads=["ones", "tp"], writes=["ps1"])
            P.op("act", lambda e: e.activation(out=mtot, in_=PS[1][:, 0:4], func=AF.Copy), reads=["ps1"], writes=["mtot"])
            P.op("dve", lambda e: e.tensor_scalar(out=mtot, in0=mtot, scalar1=1e-20, scalar2=None, op0=ALU.add), reads=["mtot"], writes=["mtot"])
            P.op("dve", lambda e: e.tensor_tensor(out=hfs[:], in0=p2s[:].unsqueeze(1).to_broadcast([128, 4, NI_S + 1]),
                                                  in1=mtot.unsqueeze(2).to_broadcast([128, 4, NI_S + 1]), op=ALU.mult), reads=["p2s", "mtot"], writes=["hfs"])
            P.op("dve", lambda e: e.memset(mid, 0.0), writes=["mid4"])
            for it in range(NI_S):
                last = it == NI_S - 1
                hb = NI_S - 1 if last else it + 1
                hd = NI_S if last else it + 1
                P.op("dve", lambda e: e.tensor_tensor(out=cmpb[:], in0=sc[:], in1=mid.unsqueeze(2).to_broadcast([128, 4, 129]), op=ALU.is_ge),
                     reads=["sc", "mid4"], writes=["cmpb"])
                P.op("dve", lambda e: e.tensor_reduce(out=cnt4, in_=cmpb[:], axis=AX.X, op=ALU.add), reads=["cmpb"], writes=["cnt4"])
                P.op("pe", lambda e: e.matmul(PS[1][:, 0:4], lhsT=ones[:], rhs=cnt4, start=True, stop=True), reads=["ones", "cnt4"], writes=["ps1"])
                P.op("act", lambda e: e.activation(out=tot, in_=PS[1][:, 0:4], func=AF.Copy), reads=["ps1"], writes=["tot4"])
                P.op("dve", lambda e, hb=hb: e.tensor_tensor(out=midm, in0=mid, in1=hfs[:, :, hb], op=ALU.subtract), reads=["mid4", "hfs"], writes=["midm4"])
                P.op("dve", lambda e: e.tensor_scalar(out=ge2, in0=tot, scalar1=255.5, scalar2=2.0, op0=ALU.is_ge, op1=ALU.mult), reads=["tot4"], writes=["ge24"])
                P.op("dve", lambda e, hd=hd: e.tensor_tensor(out=tmp4, in0=ge2, in1=hfs[:, :, hd], op=ALU.mult), reads=["ge24", "hfs"], writes=["tmp4"])
                P.op("dve", lambda e: e.tensor_tensor(out=mid, in0=tmp4, in1=midm, op=ALU.add), reads=["tmp4", "midm4"], writes=["mid4"])
            for b in range(4):
                P.op("dve", lambda e, b=b: e.max(out=v16[:, 0:8], in_=sc[:, b, 0:128]), reads=["sc"], writes=["v16"])
                P.op("dve", lambda e, b=b: e.max_index(out=i16[:, 0:8], in_max=v16[:, 0:8], in_values=sc[:, b, 0:128]), reads=["sc", "v16"], writes=["i16"])
                P.op("dve", lambda e, b=b: e.match_replace(out=sc2[:], in_to_replace=v16[:, 0:8], in_values=sc[:, b, 0:128], imm_value=BIGNEG),
                     reads=["sc", "v16"], writes=["sc2"])
                P.op("dve", lambda e: e.max(out=v16[:, 8:16], in_=sc2[:]), reads=["sc2", "v16"], writes=["v16"])
                P.op("dve", lambda e: e.max_index(out=i16[:, 8:16], in_max=v16[:, 8:16], in_values=sc2[:]), reads=["sc2", "v16", "i16"], writes=["i16"])
                P.op("dve", lambda e, b=b: e.tensor_scalar(out=valid[:, 0:16], in0=v16[:], scalar1=mid[:, b:b + 1], scalar2=None, op0=ALU.is_ge),
                     reads=["v16", "mid4"], writes=["valid"])
                P.op("dve", lambda e, b=b: e.tensor_scalar(out=valid[:, 16:17], in0=sc[:, b, 128:129], scalar1=mid[:, b:b + 1], scalar2=None, op0=ALU.is_ge),
                     reads=["sc", "mid4", "valid"], writes=["valid"])
                P.op("dve", lambda e: e.tensor_copy(out=idxf[:], in_=i16[:]), reads=["i16"], writes=["idxf"])
                P.op("dve", lambda e, b=b: e.tensor_scalar(out=idxf[:], in0=idxf[:], scalar1=pt128[:, b:b + 1], scalar2=None, op0=ALU.add), reads=["idxf", "pt128"], writes=["idxf"])
                P.op("dve", lambda e: e.tensor_copy(out=rowi[:], in_=idxf[:]), reads=["idxf"], writes=["rowi"])
                for r in range(16):
                    P.dma("pool", lambda e, r=r: e.indirect_dma_start(out=KVs[:, r, 0:256], out_offset=None, in_=ckv_d,
                                                                    in_offset=bass.IndirectOffsetOnAxis(ap=rowi[:, r:r + 1], axis=0)),
                          reads=["rowi", "KVs"], writes=[f"KVs{r}"])
                P.op("pe", lambda e, b=b: e.matmul(PS[2][:], lhsT=selb[:, b, :], rhs=qs[:], start=True, stop=True), reads=["selb", "qs"], writes=["ps2"])
                P.op("act", lambda e: e.activation(out=qb[:], in_=PS[2][:], func=AF.Copy), reads=["ps2"], writes=["qb"])
                kall = [f"KVs{r}" for r in range(16)] + ["KVs"]
                vall = kall
                qbv = qb[:].rearrange("p (m two d) -> p m two d", two=2, d=64)
                for g in range(2):
                    for m in range(4):
                        P.op("dve", lambda e, g=g, m=m: e.tensor_tensor(out=prodL[:], in0=KVs[:, :, g * 64:(g + 1) * 64],
                                                                        in1=qbv[:, m, g, :].unsqueeze(1).to_broadcast([128, 17, 64]), op=ALU.mult),
                             reads=kall + ["qb"], writes=["prodL"])
                        P.op("dve", lambda e, g=g, m=m: e.tensor_reduce(out=lg[:, :, g * 4 + m], in_=prodL[:], axis=AX.X, op=ALU.add), reads=["prodL"], writes=["lg"])
                P.op("act", lambda e: e.activation(out=lg[:], in_=lg[:], func=AF.Exp, scale=0.125), reads=["lg"], writes=["lg"])
                P.op("dve", lambda e: e.tensor_tensor(out=Pm[:], in0=lg[:], in1=valid[:].unsqueeze(2).to_broadcast([128, 17, 8]), op=ALU.mult),
                     reads=["lg", "valid"], writes=["Pm"])
                for r in range(17):
                    P.op("pe", lambda e, r=r: e.matmul(PS[3][0:8, 0:129], lhsT=Pm[:, r, :], rhs=KVs[:, r, 128:257], start=(r == 0), stop=(r == 16)),
                         reads=["Pm"] + vall, writes=["ps3"])
                P.op("act", lambda e: e.activation(out=o8[:, 0:129], in_=PS[3][0:8, 0:129], func=AF.Copy), reads=["ps3"], writes=["o8"])
                P.op("dve", lambda e: e.reciprocal(out=o8[:, 129:130], in_=o8[:, 128:129]), reads=["o8"], writes=["o8"])
                P.op("dve", lambda e: e.tensor_scalar(out=a8[:], in0=o8[:, 0:64], scalar1=hsel[:, 0:1], scalar2=None, op0=ALU.mult), reads=["o8", "hsel"], writes=["a8"])
                P.op("dve", lambda e: e.scalar_tensor_tensor(out=a8[:], in0=o8[:, 64:128], scalar=hsel[:, 1:2], in1=a8[:], op0=ALU.mult, op1=ALU.add),
                     reads=["o8", "hsel", "a8"], writes=["a8"])
                P.op("dve", lambda e: e.tensor_scalar(out=a8[:], in0=a8[:], scalar1=o8[:, 129:130], scalar2=None, op0=ALU.mult), reads=["a8", "o8"], writes=["a8"])
                P.op("dve", lambda e: e.tensor_tensor(out=Y[:], in0=a8[:].unsqueeze(1).to_broadcast([8, 8, 64]), in1=eye8[:].unsqueeze(2).to_broadcast([8, 8, 64]), op=ALU.mult),
                     reads=["a8", "eye8"], writes=["Y"])
                P.op("pe", lambda e, b=b: e.matmul(PS[6][R4, :], lhsT=sel8[:, b, :], rhs=Y[:].rearrange("p a b -> p (a b)"), start=True, stop=True),
                     reads=["sel8", "Y"], writes=["ps6"])
                P.op("dve", lambda e: e.tensor_tensor(out=mixs[:, 0:512], in0=PS[6][R4, :], in1=mixs[:, 0:512], op=ALU.add), reads=["ps6", "mixs_a"], writes=["mixs_a"])

        if with_sample:
            with ExitStack() as st:
                emit_sample(st)
                P.finish()

        with ExitStack() as st:
            xt = [sb(st, f"xt{i}", [128, D], F32) for i in range(2)]
            xn = sb(st, "xn", [128, D], BF16)
            junkx = sb(st, "junkx", [128, D], BF16)
            junkb = sb(st, "junkb", [128, 2048], BF16)
            hTp = [sb(st, f"hT{i}", [128, 8, 128], BF16) for i in range(2)]
            recbp = [sb(st, f"recb{i}", [128, 512], BF16) for i in range(2)]
            sm = sb(st, "sm", [128, 64], F32)
            kikf = sb(st, "kikf", [128, 192], F32)
            vf = sb(st, "vf", [128, 128], F32)
            kd = sb(st, "kd", [128, 256], BF16)
            sq = sb(st, "sq", [128, 704], F32)
            qn = sb(st, "qn", [128, 512], BF16)
            qT = sb(st, "qT", [128, 4, 128], BF16)
            iqT = sb(st, "iqT", [128, 4, 128], BF16)
            diagw = sb(st, "diagw", [128, 8, 128], BF16)
            kT = sb(st, "kT", [128, SEQ], BF16)
            ikT = sb(st, "ikT", [128, SEQ], BF16)
            vaug = sb(st, "vaug", [128, NT, 2, 65], BF16)
            scores = sb(st, "scores", [128, SEQ], F32)
            rl = [sb(st, f"rl{i}", [128, 512], BF16) for i in range(8)]
            maskT = sb(st, "maskT", [128, NT, 128], BF16)
            pebuf = [sb(st, f"pebuf{i}", [128, 512], BF16) for i in range(4)]
            ptb = sb(st, "ptb", [128, NT, 512], BF16)
            att = sb(st, "att", [128, 512], BF16)
            tA = [sb(st, f"tA{i}", [128, 512], F32) for i in range(8)]
            vbf = sb(st, "vbf", [128, 512], BF16)
            qp = sb(st, "qp", [128, 512], BF16)
            kp = sb(st, "kp", [128, 512], BF16)
            qkT = sb(st, "qkT", [128, 8, 128], BF16)
            AT = sb(st, "AT", [128, 4, 128], BF16)
            S = sb(st, "S", [128, 4, 128], F32)
            Sdec = sb(st, "Sdec", [128, 4, 128], F32)
            Sbf = sb(st, "Sbf", [128, 4, 128], BF16)
            mixT = sb(st, "mixT", [128, 8, 128], BF16)
            xm = [sb(st, "xm0", [128, D], F32)] * 2
            halfs = sb(st, "halfs", [128, N_IT + 1], F32)
            sm3 = sb(st, "sm3", [128, 16], F32)
            sup = sb(st, "sup", [128, 512], F32)
            P.op("pool", lambda e: e.memset(vaug[:], 1.0), writes=["vaug"])
            P.op("pool", lambda e: e.memset(AT[:], 0.0), writes=["AT"])

            def tile_parts(sq_i, j):
                ti = sq_i * NT + j
                r0 = ti * 128
                L = 128 * (j + 1)
                xb = xt[ti % 2]
                xbn = f"xt{ti % 2}"
                xmb = xm[ti % 2]
                xmn = "xm0"
                par = ti % 2
                hT = hTp[par]
                hTn = f"hT{par}"
                recb = recbp[par]
                recn = f"recb{par}"
                def proj_tok(bank, off, width, nm):
                    for c in range(8):
                        P.op("pe", lambda e, c=c: e.matmul(PS[bank][:, 0:width], lhsT=hT[:, c, :], rhs=wi[:, c, off:off + width],
                                                           start=(c == 0), stop=(c == 7)), reads=[hTn, f"wi{c}"], writes=[nm])
                def part_E():
                    if j == 0:
                        P.op("pool", lambda e: e.memset(S[:], 0.0), writes=["S"])
                    P.dma("sp", lambda e, xb=xb, r0=r0: e.dma_start(out=xb[:], in_=xp[r0:r0 + 128, :]), writes=[xbn])
                    P.op("act", lambda e, xb=xb: e.activation(out=junkx[:], in_=xb[:], func=AF.Square, accum_out=sm[:, 0:1]),
                         reads=[xbn], writes=["junkx", "sm0"])
                    rstd_from(sm[:, 0:1], sm[:, 1:2], D, ["sm0"])
                    P.op("act", lambda e, xb=xb: e.activation(out=xn[:], in_=xb[:], func=AF.Copy, scale=sm[:, 1:2]), reads=[xbn, "sm0"], writes=["xn"])
                    for c in range(8):
                        P.op("pe", lambda e, c=c: e.transpose(PSB[4][:, c * 128:(c + 1) * 128], xn[:, c * 128:(c + 1) * 128], identb[:]),
                             reads=["xn", "identb"], writes=["ps4"])
                    P.op("dve", lambda e: e.tensor_tensor(out=hT[:], in0=PSB[4][:].rearrange("p (c n) -> p c n", c=8),
                                                          in1=bc(g1T[:], 2, [128, 8, 128]), op=ALU.mult), reads=["ps4", "g1T"], writes=[hTn])
                    proj_tok(5, O_RF, 512, "ps5")
                    proj_tok(4, O_RQ, 512, "ps4")
                    proj_tok(7, O_RG, 512, "ps7")
                    proj_tok(6, O_RI, 512, "ps6")
                    if not os.environ.get('K_NOH'):
                        t_ef, t_eq, t_eg, t_f, t_k, t_q, t_gate, t_x = tA
                        n_ef, n_eq, n_eg, n_f, n_k, n_q, n_gate, n_x = [f"tA{i}" for i in range(8)]
                        P.op("act", lambda e: e.activation(out=t_ef[:], in_=PS[5][:], func=AF.Exp, scale=-1.0), reads=["ps5"], writes=[n_ef])
                        P.op("act", lambda e: e.activation(out=t_eq[:], in_=PS[4][:], func=AF.Exp, scale=-1.0), reads=["ps4"], writes=[n_eq])
                        P.op("act", lambda e: e.activation(out=t_eg[:], in_=PS[7][:], func=AF.Exp, scale=-1.0), reads=["ps7"], writes=[n_eg])
                        P.op("act", lambda e: e.activation(out=vbf[:], in_=PS[6][:], func=AF.Copy), reads=["ps6"], writes=["vbf"])
                        for tt, nn in ((t_ef, n_ef), (t_eq, n_eq), (t_eg, n_eg)):
                            P.op("pool", lambda e, tt=tt: e.tensor_scalar(out=tt[:], in0=tt[:], scalar1=1.0, scalar2=1.0, op0=ALU.mult, op1=ALU.add), reads=[nn], writes=[nn])
                            P.op("dve", lambda e, tt=tt: e.reciprocal(out=tt[:], in_=tt[:]), reads=[nn], writes=[nn])
                        P.op("pool", lambda e: e.tensor_tensor(out=t_f[:], in0=t_ef[:], in1=oml[:], op=ALU.mult), reads=[n_ef, "oml"], writes=[n_f])
                        P.op("pool", lambda e: e.tensor_tensor(out=t_f[:], in0=t_f[:], in1=lbt[:], op=ALU.add), reads=[n_f, "lbt"], writes=[n_f])
                        P.op("pool", lambda e: e.tensor_scalar(out=t_k[:], in0=t_f[:], scalar1=-1.0, scalar2=1.0, op0=ALU.mult, op1=ALU.add), reads=[n_f], writes=[n_k])
                        P.op("act", lambda e: e.activation(out=t_f[:], in_=t_f[:], func=AF.Ln), reads=[n_f], writes=[n_f])
                        P.op("dve", lambda e: e.scalar_tensor_tensor(out=t_q[:], in0=PS[4][:], scalar=128 ** -0.5, in1=t_eq[:], op0=ALU.mult, op1=ALU.mult),
                             reads=["ps4", n_eq], writes=[n_q])
                        P.op("dve", lambda e: e.tensor_tensor(out=t_gate[:], in0=PS[7][:], in1=t_eg[:], op=ALU.mult), reads=["ps7", n_eg], writes=[n_gate])
                        P.op("pe", lambda e: e.matmul(PS[4][:], lhsT=mbf[:], rhs=t_f[:], start=True, stop=True), reads=["mbf", n_f], writes=["ps4"])
                        for h in range(4):
                            P.op("pe", lambda e, h=h: e.matmul(PS[6][:, 2 * h:2 * h + 2], lhsT=t_f[:, h * 128:(h + 1) * 128], rhs=indf[:], start=True, stop=True),
                                 reads=[n_f, "indf"], writes=["ps6"])
                        P.op("act", lambda e: e.activation(out=sm3[:, 0:8], in_=PS[6][:, 0:8], func=AF.Copy), reads=["ps6"], writes=["sm3"])
                        cs = sm3[:, 0:8].rearrange("p (h two) -> p h two", two=2)
                        P.op("act", lambda e: e.activation(out=sm[:, 32:36], in_=cs[:, :, 0], func=AF.Exp), reads=["sm3"], writes=["sm32"])
                        P.op("act", lambda e: e.activation(out=sm[:, 36:40], in_=cs[:, :, 1], func=AF.Exp), reads=["sm3"], writes=["sm36"])
                        P.op("dve", lambda e: e.tensor_tensor(out=sm[:, 40:44], in0=cs[:, :, 1], in1=cs[:, :, 0], op=ALU.subtract), reads=["sm3"], writes=["sm40"])
                        P.op("act", lambda e: e.activation(out=sm[:, 40:44], in_=sm[:, 40:44], func=AF.Exp), reads=["sm40"], writes=["sm40"])
                        P.op("act", lambda e: e.activation(out=t_x[:], in_=PS[4][:], func=AF.Exp), reads=["ps4"], writes=[n_x])
                        P.op("dve", lambda e: e.tensor_tensor(out=qp[:], in0=t_q[:], in1=t_x[:], op=ALU.mult), reads=[n_q, n_x], writes=["qp"])
                        P.op("act", lambda e: e.activation(out=t_x[:], in_=PS[4][:], func=AF.Exp, scale=-1.0), reads=["ps4"], writes=[n_x])
                        P.op("dve", lambda e: e.tensor_tensor(out=kp[:], in0=t_k[:], in1=t_x[:], op=ALU.mult), reads=[n_k, n_x], writes=["kp"])
                        for h in range(4):
                            P.op("pe", lambda e, h=h: e.transpose(PSB[5][:, h * 128:(h + 1) * 128], qp[:, h * 128:(h + 1) * 128], identb[:]),
                                 reads=["qp", "identb"], writes=["ps5"])
                        for h in range(4):
                            P.op("pe", lambda e, h=h: e.transpose(PSB[5][:, (4 + h) * 128:(5 + h) * 128], kp[:, h * 128:(h + 1) * 128], identb[:]),
                                 reads=["kp", "identb"], writes=["ps5"])
                        P.op("act", lambda e: e.activation(out=qkT[:].rearrange("p a b -> p (a b)"), in_=PSB[5][:], func=AF.Copy), reads=["ps5"], writes=["qkT"])
                        for h in range(4):
                            P.op("pe", lambda e, h=h: e.matmul(PS[4][0:64, h * 128:(h + 1) * 128], lhsT=qkT[:, 4 + h, 0:64], rhs=qkT[:, h, :], start=True, stop=True),
                                 reads=["qkT"], writes=["ps4"])
                            P.op("pe", lambda e, h=h: e.matmul(PS[4][64:128, h * 128 + 64:(h + 1) * 128], lhsT=qkT[:, 4 + h, 64:128], rhs=qkT[:, h, 64:128], start=True, stop=True),
                                 reads=["qkT"], writes=["ps4"])
                        P.op("dve", lambda e: e.tensor_tensor(out=AT[0:64, :, :], in0=PS[4][0:64, :].rearrange("p (h t) -> p h t", h=4),
                                                              in1=bc(ctf[0:64, :], 1, [64, 4, 128]), op=ALU.mult), reads=["ps4", "ctf"], writes=["AT"])
                        P.op("dve", lambda e: e.tensor_tensor(out=AT[64:128, :, 64:128], in0=PS[4][64:128, :].rearrange("p (h t) -> p h t", h=4)[:, :, 64:128],
                                                              in1=bc(ctf[64:128, 64:128], 1, [64, 4, 64]), op=ALU.mult), reads=["ps4", "ctf"], writes=["AT"])
                        for h in range(4):
                            P.op("pool", lambda e, h=h: e.tensor_scalar(out=Sbf[:, h, :], in0=S[:, h, :], scalar1=sm[:, 32 + h:33 + h], scalar2=0.0, op0=ALU.mult, op1=ALU.add),
                                 reads=["S", "sm32"], writes=["Sbf"])
                            P.op("pool", lambda e, h=h: e.tensor_scalar(out=Sdec[:, h, :], in0=S[:, h, :], scalar1=sm[:, 36 + h:37 + h], scalar2=0.0, op0=ALU.mult, op1=ALU.add),
                                 reads=["S", "sm36"], writes=["Sdec"])
                        for h in range(4):
                            P.op("pe", lambda e, h=h: e.matmul(PS[6][:, h * 128:(h + 1) * 128], lhsT=qkT[:, h, :], rhs=Sbf[:, h, :], start=True, stop=False),
                                 reads=["qkT", "Sbf"], writes=["ps6"])
                            P.op("pe", lambda e, h=h: e.matmul(PS[6][:, h * 128:(h + 1) * 128], lhsT=AT[:, h, :], rhs=vbf[:, h * 128:(h + 1) * 128], start=False, stop=True),
                                 reads=["AT", "vbf"], writes=["ps6"])
                        for h in range(4):
                            P.op("pe", lambda e, h=h: e.matmul(PS[7][:, h * 128:(h + 1) * 128], lhsT=kp[:, h * 128:(h + 1) * 128], rhs=vbf[:, h * 128:(h + 1) * 128], start=True, stop=True),
                                 reads=["kp", "vbf"], writes=["ps7"])
                        for h in range(4):
                            P.op("act", lambda e, h=h: e.activation(out=sup[:, h * 128:(h + 1) * 128], in_=PS[7][:, h * 128:(h + 1) * 128], func=AF.Copy, scale=sm[:, 40 + h:41 + h]),
                                 reads=["ps7", "sm40"], writes=["sup"])
                        P.op("dve", lambda e: e.tensor_tensor(out=S[:].rearrange("p a b -> p (a b)"), in0=sup[:], in1=Sdec[:].rearrange("p a b -> p (a b)"), op=ALU.add),
                             reads=["sup", "Sdec"], writes=["S"])
                        if j == NT - 1:
                            P.dma("sp", lambda e, sq_i=sq_i: e.dma_start(out=nst[sq_i].rearrange("h k v -> k h v"), in_=S[:]), reads=["S"], is_output=True)
                        P.op("act", lambda e: e.activation(out=t_x[:], in_=PS[6][:], func=AF.Square), reads=["ps6"], writes=[n_x])
                        P.op("dve", lambda e: e.tensor_reduce(out=sm[:, 44:48], in_=t_x[:].rearrange("p (a b) -> p a b", b=128), axis=AX.X, op=ALU.add),
                             reads=[n_x], writes=["sm44"])
                        rstd_from(sm[:, 44:48], sm[:, 44:48], 128, ["sm44"])
                        for h in range(4):
                            P.op("act", lambda e, h=h: e.activation(out=t_x[:, h * 128:(h + 1) * 128], in_=PS[6][:, h * 128:(h + 1) * 128], func=AF.Copy, scale=sm[:, 44 + h:45 + h]),
                                 reads=["ps6", "sm44"], writes=[n_x])
                        P.op("pool", lambda e: e.tensor_tensor(out=t_x[:], in0=t_x[:], in1=rngb[:], op=ALU.mult), reads=[n_x, "rngb"], writes=[n_x])
                        P.op("pool", lambda e: e.tensor_tensor(out=recb[:], in0=t_x[:], in1=t_gate[:], op=ALU.mult), reads=[n_x, n_gate], writes=[recn])

                def part_prefix_proj():
                    proj_tok(2, O_KVI, 328, "ps2")
                    proj_tok(1, O_AQ, 512, "ps1")
                    for m in range(4):
                        for c in range(8):
                            P.op("pe", lambda e, c=c, m=m: e.matmul(PS[3][:, m * 128:(m + 1) * 128], lhsT=wi[:, c, O_IQ + m * 128:O_IQ + (m + 1) * 128],
                                                                    rhs=hT[:, c, :], start=(c == 0), stop=(c == 7)), reads=[hTn, f"wi{c}"], writes=["ps3"])
                def part_prefix():
                    P.op("act", lambda e: e.activation(out=sq[:, 0:192], in_=PS[2][:, 0:192], func=AF.Square), reads=["ps2"], writes=["sqk"])
                    P.op("dve", lambda e: e.tensor_reduce(out=sm[:, 8:11], in_=sq[:, 0:192].rearrange("p (a b) -> p a b", b=64), axis=AX.X, op=ALU.add),
                         reads=["sqk"], writes=["sm8"])
                    rstd_from(sm[:, 8:11], sm[:, 8:11], 64, ["sm8"])
                    stage(2.1)
                    for a in range(3 if os.environ.get("K_VAR", "0") != "2" else 0):
                        P.op("act", lambda e, a=a: e.activation(out=kikf[:, a * 64:(a + 1) * 64], in_=PS[2][:, a * 64:(a + 1) * 64], func=AF.Copy, scale=sm[:, 8 + a:9 + a]),
                             reads=["ps2", "sm8"], writes=["kikf"])
                    if os.environ.get("K_VAR", "0") != "1":
                        P.op("dve", lambda e: e.tensor_tensor(out=kikf[:], in0=kikf[:], in1=kig[:], op=ALU.mult), reads=["kikf", "kig"], writes=["kikf"])
                    stage(2.2)
                    P.op("act", lambda e: e.activation(out=vf[:], in_=PS[2][:, 192:320], func=AF.Copy), reads=["ps2"], writes=["vf"])
                    P.op("act", lambda e: e.activation(out=sm[:, 24:32], in_=PS[2][:, 320:328], func=AF.Copy, scale=IDX_W_SCALE), reads=["ps2"], writes=["sm24"])
                    stage(2.3)
                    P.dma("sp", lambda e, r0=r0: e.dma_start(out=nk[r0:r0 + 128, :], in_=kikf[:, 0:128]), reads=["kikf"], is_output=True)
                    P.dma("sp", lambda e, r0=r0: e.dma_start(out=nki[r0:r0 + 128, :], in_=kikf[:, 128:192]), reads=["kikf"], is_output=True)
                    P.dma("sp", lambda e, r0=r0: e.dma_start(out=nv[r0:r0 + 128, :], in_=vf[:]), reads=["vf"], is_output=True)
                    stage(2.4)
                    P.op("act", lambda e: e.copy(out=kd[:, 0:192], in_=kikf[:]), reads=["kikf"], writes=["kd"])
                    P.op("act", lambda e: e.copy(out=kd[:, 192:256], in_=kikf[:, 128:192]), reads=["kikf"], writes=["kd"])
                    P.op("dve", lambda e, j=j: e.tensor_copy(out=vaug[:, j, :, 0:64], in_=vf[:].rearrange("p (a b) -> p a b", b=64)),
                         reads=["vf"], writes=["vaug"])
                    for h in range(8):
                        P.op("dve", lambda e, h=h: e.tensor_scalar(out=diagw[:, h, :], in0=identb[:], scalar1=sm[:, 24 + h:25 + h], scalar2=None, op0=ALU.mult),
                             reads=["identb", "sm24"], writes=["diagw"])
                    stage(3)
                    for m in range(2):
                        P.op("pe", lambda e, m=m: e.transpose(PSB[0][:, (4 + m) * 128:(5 + m) * 128], kd[:, m * 128:(m + 1) * 128], identb[:]),
                             reads=["kd", "identb"], writes=["ps0"])
                    P.op("act", lambda e, j=j: e.activation(out=kT[:, j * 128:(j + 1) * 128], in_=PSB[0][:, 512:640], func=AF.Copy), reads=["ps0"], writes=["kT"])
                    P.op("act", lambda e, j=j: e.activation(out=ikT[:, j * 128:(j + 1) * 128], in_=PSB[0][:, 640:768], func=AF.Copy), reads=["ps0"], writes=["ikT"])
                    P.op("act", lambda e: e.activation(out=iqT[:].rearrange("p a b -> p (a b)"), in_=PS[3][:], func=AF.Copy), reads=["ps3"], writes=["iqT"])

                def part_a():
                    nblk = (L + 511) // 512
                    rounds = [(blk, h) for blk in range(0 if os.environ.get('K_NOIDX') else nblk) for h in range(8)]

                    ibanks = [0, 2, 4, 5, 6, 7]

                    def idx_dots(i):
                        blk, h = rounds[i]
                        wd = min(512, L - blk * 512)
                        db = ibanks[i % 6]
                        pr = slice((h % 2) * 64, (h % 2) * 64 + 64)
                        P.op("pe", lambda e: e.matmul(PS[db][:, 0:wd], lhsT=iqT[pr, h // 2, :], rhs=ikT[pr, blk * 512:blk * 512 + wd], start=True, stop=True),
                             reads=["iqT", "ikT"], writes=[f"ps{db}"])

                    def idx_relu(i):
                        blk, h = rounds[i]
                        wd = min(512, L - blk * 512)
                        db = ibanks[i % 6]
                        rb = i % 8
                        if True:
                            P.op("act", lambda e: e.activation(out=rl[rb][:, 0:wd], in_=PS[db][:, 0:wd], func=AF.Relu), reads=[f"ps{db}"], writes=[f"rl{rb}"])
                        else:
                            P.op("dve", lambda e: e.tensor_scalar(out=rl[rb][:, 0:wd], in0=PS[db][:, 0:wd], scalar1=0.0, scalar2=None, op0=ALU.max),
                                 reads=[f"ps{db}"], writes=[f"rl{rb}"])

                    def idx_diag(i):
                        blk, h = rounds[i]
                        wd = min(512, L - blk * 512)
                        rb = i % 8
                        P.op("pe", lambda e: e.matmul(PS[3][:, 0:wd], lhsT=diagw[:, h, :], rhs=rl[rb][:, 0:wd], start=(h == 0), stop=(h == 7)),
                             reads=["diagw", f"rl{rb}"], writes=["ps3"])
                        if h == 7:
                            P.op("act", lambda e: e.activation(out=scores[:, blk * 512:blk * 512 + wd], in_=PS[3][:, 0:wd], func=AF.Copy),
                                 reads=["ps3"], writes=["scores"])

                    groups = [list(range(k, min(k + 4, len(rounds)))) for k in range(0, len(rounds), 4)]
                    if groups:
                        for i in groups[0]:
                            idx_dots(i)
                    for gi, grp in enumerate(groups):
                        for i in grp:
                            idx_relu(i)
                        if gi + 1 < len(groups):
                            for i in groups[gi + 1]:
                                idx_dots(i)
                        for i in grp:
                            idx_diag(i)
                    P.mark()
                    P.op("act", lambda e: e.activation(out=sq[:, 192:704], in_=PS[1][:, 0:512], func=AF.Square), reads=["ps1"], writes=["sqq"])
                    P.op("dve", lambda e: e.tensor_reduce(out=sm[:, 11:19], in_=sq[:, 192:704].rearrange("p (a b) -> p a b", b=64), axis=AX.X, op=ALU.add),
                         reads=["sqq"], writes=["sm11"])
                    rstd_from(sm[:, 11:19], sm[:, 11:19], 64, ["sm11"])
                    for a in range(8):
                        P.op("act", lambda e, a=a: e.activation(out=qn[:, a * 64:(a + 1) * 64], in_=PS[1][:, a * 64:(a + 1) * 64], func=AF.Copy, scale=sm[:, 11 + a:12 + a]),
                             reads=["ps1", "sm11"], writes=["qn"])
                    for m in range(4):
                        P.op("pe", lambda e, m=m: e.transpose(PSB[0][:, m * 128:(m + 1) * 128], qn[:, m * 128:(m + 1) * 128], identb[:]),
                             reads=["qn", "identb"], writes=["ps0"])
                    P.op("act", lambda e: e.activation(out=qT[:].rearrange("p a b -> p (a b)"), in_=PSB[0][:, 0:512], func=AF.Copy, scale=qg2[:, 0:1]),
                         reads=["ps0", "qg2"], writes=["qT"])
                    if j >= 2:
                        P.op("dve", lambda e, L=L: e.tensor_reduce(out=sm[:, 48:49], in_=scores[:, 0:L], axis=AX.X, op=ALU.max, apply_absolute_value=True),
                             reads=["scores"], writes=["sm48"])
                        P.op("dve", lambda e: e.tensor_scalar(out=sm[:, 2:3], in0=sm[:, 48:49], scalar1=1e-20, scalar2=None, op0=ALU.add), reads=["sm48"], writes=["sm48"])
                        P.op("dve", lambda e: e.tensor_scalar(out=halfs[:], in0=p2[:], scalar1=sm[:, 2:3], scalar2=None, op0=ALU.mult),
                             reads=["p2", "sm48"], writes=["halfs"])
                    P.op("dve", lambda e, L=L: e.tensor_tensor(out=scores[:, L - 128:L], in0=scores[:, L - 128:L], in1=negc[:], op=ALU.add),
                         reads=["scores", "negc"], writes=["scores"])
                    if j >= 2:
                        P.op("dve", lambda e: e.memset(sm[:, 50:51], 0.0), writes=["mid"])
                        for it in range(N_IT):
                            last = it == N_IT - 1
                            hb = N_IT - 1 if last else it + 1
                            hd = N_IT if last else it + 1
                            P.op("dve", lambda e, L=L: e.tensor_scalar(out=junkb[:, 0:L], in0=scores[:, 0:L], scalar1=sm[:, 50:51], scalar2=None,
                                                                       op0=ALU.is_ge, op1=ALU.add, accum_out=sm[:, 52:53]),
                                 reads=["scores", "mid"], writes=["junkb", "cnt"])
                            P.op("dve", lambda e, hb=hb: e.tensor_tensor(out=sm[:, 51:52], in0=sm[:, 50:51], in1=halfs[:, hb:hb + 1], op=ALU.subtract),
                                 reads=["mid", "halfs"], writes=["midm"])
                            P.op("dve", lambda e: e.tensor_scalar(out=sm[:, 53:54], in0=sm[:, 52:53], scalar1=255.5, scalar2=2.0, op0=ALU.is_ge, op1=ALU.mult),
                                 reads=["cnt"], writes=["ge2"])
                            P.op("dve", lambda e, hd=hd: e.scalar_tensor_tensor(out=sm[:, 50:51], in0=sm[:, 53:54], scalar=halfs[:, hd:hd + 1], in1=sm[:, 51:52],
                                                                               op0=ALU.mult, op1=ALU.add), reads=["ge2", "halfs", "midm"], writes=["mid"])
                    else:
                        P.op("dve", lambda e: e.memset(sm[:, 50:51], -5000.0), writes=["mid"])
                    P.op("dve", lambda e, L=L: e.tensor_scalar(out=junkb[:, 0:L], in0=scores[:, 0:L], scalar1=sm[:, 50:51], scalar2=None, op0=ALU.is_ge),
                         reads=["scores", "mid"], writes=["junkb"])
                    P.mark()
                    for g0 in range(0, j + 1, 8):
                        g1 = min(j + 1, g0 + 8)
                        for kc in range(g0, g1):
                            P.op("pe", lambda e, kc=kc, g0=g0: e.transpose(PSB[0][:, (kc - g0) * 128:(kc - g0 + 1) * 128], junkb[:, kc * 128:(kc + 1) * 128], identb[:]),
                                 reads=["junkb", "identb"], writes=["ps0"])
                        P.op("act", lambda e, g0=g0, g1=g1: e.activation(out=maskT[:, g0:g1, :].rearrange("p a b -> p (a b)"), in_=PSB[0][:, 0:(g1 - g0) * 128], func=AF.Copy),
                             reads=["ps0"], writes=["maskT"])
                    stage(7)
                    abanks = [1, 2, 4, 5, 6, 7]
                    mi = 0
                    for g in range(0 if os.environ.get('K_NOATT') else 2):
                        pr = slice(g * 64, g * 64 + 64)
                        kcs = list(range(j + 1))
                        agroups = [kcs[k:k + 3] for k in range(0, len(kcs), 3)]
                        slot = {}
                        for kc in kcs:
                            slot[kc] = mi
                            mi += 1

                        def a_qk(kc, prl=None):
                            prl = pr
                            db = abanks[slot[kc] % 6]
                            P.op("pe", lambda e: e.matmul(PS[db][:], lhsT=kT[prl, kc * 128:(kc + 1) * 128],
                                                          rhs=qT[prl, :, :].rearrange("p a b -> p (a b)"), start=True, stop=True),
                                 reads=["kT", "qT"], writes=[f"ps{db}"])

                        def a_exp(kc):
                            db = abanks[slot[kc] % 6]
                            pb = slot[kc] % 4
                            P.op("act", lambda e: e.activation(out=pebuf[pb][:], in_=PS[db][:], func=AF.Exp, scale=0.125),
                                 reads=[f"ps{db}"], writes=[f"pebuf{pb}"])

                        def a_mask(kc):
                            pb = slot[kc] % 4
                            P.op("dve", lambda e: e.tensor_tensor(out=ptb[:, kc, :].rearrange("p (h t) -> p h t", h=4),
                                                                  in0=pebuf[pb][:].rearrange("p (h t) -> p h t", h=4),
                                                                  in1=bc(maskT[:, kc, :], 1, [128, 4, 128]), op=ALU.mult),
                                 reads=[f"pebuf{pb}", "maskT"], writes=[f"ptb{kc}"])

                        for kc in agroups[0]:
                            a_qk(kc)
                        for gi, grp in enumerate(agroups):
                            for kc in grp:
                                a_exp(kc)
                            if gi + 1 < len(agroups):
                                for kc in agroups[gi + 1]:
                                    a_qk(kc)
                            for kc in grp:
                                a_mask(kc)
                        pvb = 3 if g == 0 else 0
                        for hh in range(0 if os.environ.get('K_NOPV') else 4):
                            for kc in range(j + 1):
                                P.op("pe", lambda e, hh=hh, kc=kc, g=g, pvb=pvb: e.matmul(PS[pvb][:, hh * 65:(hh + 1) * 65], lhsT=ptb[:, kc, hh * 128:(hh + 1) * 128],
                                                                                         rhs=vaug[:, kc, g, :], start=(kc == 0), stop=(kc == j)),
                                     reads=[f"ptb{kc}", "vaug"], writes=[f"ps{pvb}"])
                        pv = PS[pvb][:, 0:260].rearrange("p (h c) -> p h c", c=65)
                        P.op("dve", lambda e, pv=pv, g=g: e.reciprocal(out=sm[:, 56 + 4 * g:60 + 4 * g], in_=pv[:, :, 64]), reads=[f"ps{pvb}"], writes=[f"rs{g}"])
                        for hh in range(4):
                            P.op("act", lambda e, hh=hh, g=g, pvb=pvb: e.activation(out=att[:, g * 256 + hh * 64:g * 256 + (hh + 1) * 64], in_=PS[pvb][:, hh * 65:hh * 65 + 64],
                                                                                  func=AF.Copy, scale=sm[:, 56 + 4 * g + hh:57 + 4 * g + hh]),
                                 reads=[f"ps{pvb}", f"rs{g}"], writes=["att_a"])
                def part_tail():
                    for c in range(8):
                        src_ap = att[:, c * 128:(c + 1) * 128] if c < 4 else recb[:, (c - 4) * 128:(c - 3) * 128]
                        P.op("pe", lambda e, c=c, src_ap=src_ap: e.transpose(PSB[6][:, c * 128:(c + 1) * 128], src_ap, identb[:]),
                             reads=["att_a", recn, "identb"], writes=["ps6"])
                    P.op("act", lambda e: e.activation(out=mixT[:].rearrange("p a b -> p (a b)"), in_=PSB[6][:], func=AF.Copy), reads=["ps6"], writes=["mixT"])
                    for half in range(2):
                        ob = 4 + half
                        for c in range(8):
                            P.op("pe", lambda e, c=c, half=half, ob=ob: e.matmul(PS[ob][:], lhsT=mixT[:, c, :], rhs=wo[:, c, half * 512:(half + 1) * 512],
                                                                                 start=(c == 0), stop=(c == 7)), reads=["mixT", f"wo{c}"], writes=[f"ps{ob}"])
                        P.op("dve", lambda e, half=half, ob=ob, xb=xb, xmb=xmb: e.tensor_tensor(out=xmb[:, half * 512:(half + 1) * 512], in0=PS[ob][:],
                                                                                               in1=xb[:, half * 512:(half + 1) * 512], op=ALU.add),
                             reads=[f"ps{ob}", xbn], writes=[xmn])
                    P.dma("sp", lambda e, r0=r0, xmb=xmb: e.dma_start(out=yp[r0:r0 + 128, :], in_=xmb[:]), reads=[xmn], writes=[f"yd{ti}"], is_output=True)
                return part_E, part_prefix, part_a, part_tail, part_prefix_proj

            tiles = [tile_parts(sq_i, j) for sq_i in range(DBG_NSEQ) for j in range(DBG_NT)]
            if tiles:
                tiles[0][0]()
                tiles[0][4]()
            for t, (pE, pP, pA, pT, pPP) in enumerate(tiles):
                pP()
                P.begin("a")
                pA()
                P.end()
                if t + 1 < len(tiles):
                    P.begin("e")
                    tiles[t + 1][0]()
                    P.end()
                P.merge("a", "e")
                if t + 1 < len(tiles):
                    tiles[t + 1][4]()
                pT()
            P.finish()

        stW.close()

        with ExitStack() as st:
            wu = sb(st, "wu", [128, 8, 4 * D], BF16)
            wd_ = sb(st, "wd", [128, 32, D], BF16)
            xs = [sb(st, f"xs{i}", [128, 2, D], F32) for i in range(2)]
            xn2 = sb(st, "xn2", [128, D], BF16)
            junk2 = sb(st, "junk2", [128, D], BF16)
            h2T = sb(st, "h2T", [128, 8, 256], BF16)
            rl2 = [sb(st, f"r2{i}", [128, 256], F32) for i in range(2)]
            uT = sb(st, "uT", [128, 32, 256], BF16)
            yo = [sb(st, f"yo{i}", [128, D], F32) for i in range(2)]
            sm2 = sb(st, "sm2", [128, 8], F32)
            eps2 = sb(st, "eps2", [128, 1], F32)
            P.op("pool", lambda e: e.memset(eps2[:], EPS), writes=["eps2"])
            for c in range(8):
                P.dma("pool", lambda e, c=c: e.dma_start(out=wu[:, c, :], in_=w_up[c * 128:(c + 1) * 128, :]), writes=[f"wu{c}"])
            for c in range(32):
                P.dma("pool", lambda e, c=c: e.dma_start(out=wd_[:, c, :], in_=w_down[c * 128:(c + 1) * 128, :]), writes=[f"wd{c}"])
            NSUP = NTOK // 256 if DBG_B else 0
            for su in range(NSUP):
                xb = xs[su % 2]
                xbn = f"xs{su % 2}"
                r0 = su * 256
                P.dma("sp", lambda e, xb=xb, r0=r0: e.dma_start(out=xb[:], in_=yp[r0:r0 + 256, :].rearrange("(a p) n -> p a n", p=128)), writes=[xbn])
                for a in range(2):
                    P.op("act", lambda e, xb=xb, a=a: e.activation(out=junk2[:], in_=xb[:, a, :], func=AF.Square, accum_out=sm2[:, 0:1]),
                         reads=[xbn], writes=["junk2", "sm2"])
                    P.op("act", lambda e: e.activation(out=sm2[:, 1:2], in_=sm2[:, 0:1], func=AF.Ln, scale=1.0 / D, bias=eps2[:, 0:1]), reads=["sm2", "eps2"], writes=["sm2"])
                    P.op("act", lambda e: e.activation(out=sm2[:, 1:2], in_=sm2[:, 1:2], func=AF.Exp, scale=-0.5), reads=["sm2"], writes=["sm2"])
                    P.op("act", lambda e, xb=xb, a=a: e.activation(out=xn2[:], in_=xb[:, a, :], func=AF.Copy, scale=sm2[:, 1:2]), reads=[xbn, "sm2"], writes=["xn2"])
                    for c in range(8):
                        P.op("pe", lambda e, c=c: e.transpose(PSB[0][:, c * 128:(c + 1) * 128], xn2[:, c * 128:(c + 1) * 128], identb[:]),
                             reads=["xn2"], writes=["ps0"])
                    P.op("dve", lambda e, a=a: e.tensor_tensor(out=h2T[:, :, a * 128:(a + 1) * 128], in0=PSB[0][:].rearrange("p (c n) -> p c n", c=8),
                                                              in1=bc(g2T[:], 2, [128, 8, 128]), op=ALU.mult), reads=["ps0"], writes=["h2T"])
                for hc in range(32):
                    ub = 1 + (hc % 2)
                    rb = hc % 2
                    for c in range(8):
                        P.op("pe", lambda e, c=c, hc=hc, ub=ub: e.matmul(PS[ub][:, 0:256], lhsT=wu[:, c, hc * 128:(hc + 1) * 128], rhs=h2T[:, c, :],
                                                                         start=(c == 0), stop=(c == 7)), reads=[f"wu{c}", "h2T"], writes=[f"ps{ub}"])
                    P.op("act", lambda e, ub=ub, rb=rb: e.activation(out=rl2[rb][:], in_=PS[ub][:, 0:256], func=AF.Relu), reads=[f"ps{ub}"], writes=[f"r2{rb}"])
                    sqe = "dve"
                    P.op(sqe, lambda e, rb=rb, hc=hc: e.tensor_tensor(out=uT[:, hc, :], in0=rl2[rb][:], in1=rl2[rb][:], op=ALU.mult),
                         reads=[f"r2{rb}"], writes=["uT"])
                for a in range(2):
                    yb = yo[a]
                    for half in range(2):
                        ob = 3 + half + 2 * a
                        for hc in range(32):
                            P.op("pe", lambda e, hc=hc, a=a, half=half, ob=ob: e.matmul(PS[ob][:], lhsT=uT[:, hc, a * 128:(a + 1) * 128],
                                                                                        rhs=wd_[:, hc, half * 512:(half + 1) * 512], start=(hc == 0), stop=(hc == 31)),
                                 reads=["uT", f"wd{hc}"], writes=[f"ps{ob}"])
                        P.op("dve", lambda e, a=a, half=half, ob=ob, xb=xb, yb=yb: e.tensor_tensor(out=yb[:, half * 512:(half + 1) * 512], in0=PS[ob][:],
                                                                                                  in1=xb[:, a, half * 512:(half + 1) * 512], op=ALU.add),
                             reads=[f"ps{ob}", xbn], writes=[f"yo{a}"])
                    P.dma("sp", lambda e, r0=r0, a=a, yb=yb: e.dma_start(out=yp[r0 + a * 128:r0 + (a + 1) * 128, :], in_=yb[:]), reads=[f"yo{a}"], is_output=True)
            if with_sample:
                R4 = slice(0, 4)
                h2Ts = sb(st, "h2Ts", [128, 8, 4], BF16)
                uTs = sb(st, "uTs", [128, 32, 4], BF16)
                r2s = [sb(st, f"r2s{i}", [128, 4], F32) for i in range(2)]
                yos = sb(st, "yos", [4, D], F32)
                P.op("act", lambda e: e.activation(out=junk2[R4, :], in_=xmid_s[:], func=AF.Square, accum_out=sm2[R4, 4:5]), reads=["xmid_s"], writes=["junk2", "sm2s"])
                P.op("act", lambda e: e.activation(out=sm2[R4, 5:6], in_=sm2[R4, 4:5], func=AF.Ln, scale=1.0 / D, bias=eps2[R4, 0:1]), reads=["sm2s", "eps2"], writes=["sm2s"])
                P.op("act", lambda e: e.activation(out=sm2[R4, 5:6], in_=sm2[R4, 5:6], func=AF.Exp, scale=-0.5), reads=["sm2s"], writes=["sm2s"])
                P.op("act", lambda e: e.activation(out=xn2[R4, :], in_=xmid_s[:], func=AF.Copy, scale=sm2[R4, 5:6]), reads=["xmid_s", "sm2s"], writes=["xn2"])
                for c in range(8):
                    P.op("pe", lambda e, c=c: e.transpose(PSB[0][:, c * 128:c * 128 + 4], xn2[R4, c * 128:(c + 1) * 128], identb[0:4, 0:4]),
                         reads=["xn2"], writes=["ps0"])
                P.op("dve", lambda e: e.tensor_tensor(out=h2Ts[:], in0=PSB[0][:].rearrange("p (c n) -> p c n", c=8)[:, :, 0:4],
                                                      in1=bc(g2T[:], 2, [128, 8, 4]), op=ALU.mult), reads=["ps0"], writes=["h2Ts"])
                for hc in range(32):
                    ub = 1 + (hc % 2)
                    rb = hc % 2
                    for c in range(8):
                        P.op("pe", lambda e, c=c, hc=hc, ub=ub: e.matmul(PS[ub][:, 0:4], lhsT=wu[:, c, hc * 128:(hc + 1) * 128], rhs=h2Ts[:, c, :],
                                                                         start=(c == 0), stop=(c == 7)), reads=[f"wu{c}", "h2Ts"], writes=[f"ps{ub}"])
                    P.op("act", lambda e, ub=ub, rb=rb: e.activation(out=r2s[rb][:], in_=PS[ub][:, 0:4], func=AF.Relu), reads=[f"ps{ub}"], writes=[f"r2s{rb}"])
                    P.op("dve", lambda e, rb=rb, hc=hc: e.tensor_tensor(out=uTs[:, hc, :], in0=r2s[rb][:], in1=r2s[rb][:], op=ALU.mult),
                         reads=[f"r2s{rb}"], writes=["uTs"])
                for half in range(2):
                    ob = 3 + half
                    for hc in range(32):
                        P.op("pe", lambda e, hc=hc, half=half, ob=ob: e.matmul(PS[ob][R4, :], lhsT=uTs[:, hc, :], rhs=wd_[:, hc, half * 512:(half + 1) * 512],
                                                                               start=(hc == 0), stop=(hc == 31)), reads=["uTs", f"wd{hc}"], writes=[f"ps{ob}"])
                    P.op("dve", lambda e, half=half, ob=ob: e.tensor_tensor(out=yos[:, half * 512:(half + 1) * 512], in0=PS[ob][R4, :],
                                                                           in1=xmid_s[:, half * 512:(half + 1) * 512], op=ALU.add), reads=[f"ps{ob}", "xmid_s"], writes=["yos"])
                P.dma("sp", lambda e: e.dma_start(out=ys_d, in_=yos[:]), reads=["yos"], is_output=True)
            P.finish()
        block = st0.enter_context(nc.Block("blk"))
        P.emit(block)
        global _LAST_PROG
        _LAST_PROG = P
    return nc


def _consts():
    c = {}
    c["c_ident"] = np.eye(128, dtype=np.float32)
    s = np.arange(128)[:, None]
    t = np.arange(128)[None, :]
    mb = np.zeros((128, 128), np.float32)
    mb[(s >= 64) & (s <= t)] = 1.0
    mb[(s < 64) & (s > t)] = -1.0
    c["c_mb"] = mb
    ind = np.zeros((128, 2), np.float32)
    ind[:64, 0] = 1.0
    ind[:, 1] = 1.0
    c["c_ind"] = ind
    c["c_ct"] = (s <= t).astype(np.float32)
    c["c_negc"] = np.where(t <= s, 0.0, NEG).astype(np.float32)
    p2 = (1.001 * 0.5 ** np.arange(N_IT + 1)).astype(np.float32)
    c["c_p2"] = np.tile(p2[None, :], (128, 1))
    selb = np.zeros((4, 4, 128), np.float32)
    for b in range(4):
        selb[b, b, :] = 1.0
    c["c_selb"] = selb
    c["c_eye4"] = np.eye(4, dtype=np.float32)
    c["c_oh4"] = np.tile(np.eye(4, dtype=np.float32).reshape(1, 16), (128, 1))
    c["c_negeye"] = ((1.0 - np.eye(4)) * BIGNEG).astype(np.float32)
    hs = np.zeros((8, 2), np.float32)
    hs[:4, 0] = 1.0
    hs[4:, 1] = 1.0
    c["c_hsel"] = hs
    c["c_eye8"] = np.eye(8, dtype=np.float32)
    sel8 = np.zeros((8, 4, 4), np.float32)
    for b in range(4):
        sel8[:, b, b] = 1.0
    c["c_sel8"] = sel8
    c["c_ones"] = np.ones((128, 128), np.float32)
    p2s = (1.001 * 0.5 ** np.arange(NI_S + 1)).astype(np.float32)
    c["c_p2s"] = np.tile(p2s[None, :], (128, 1))
    return c


_NC_CACHE = {}
_LAST_PROG = None


def kernel(x_prompt, x_sample, cache_k, cache_v, cache_kidx, state_hgrn, page_table,
           norm1_g, w_in, q_norm_g, k_norm_g, idx_k_norm_g, lower_bounds, rec_norm_g,
           w_out, norm2_g, w_up, w_down):
    f32 = np.float32
    x_prompt = np.asarray(x_prompt, f32)
    w_in0 = np.asarray(w_in, f32)[0]
    aq = w_in0[:, 0:512].reshape(D, 8, 64)
    aq_perm = np.stack([aq[:, [m, m + 4], :] for m in range(4)], axis=1).reshape(D, 512)
    ak, av = w_in0[:, 512:640], w_in0[:, 640:768]
    iq, ik, iw = w_in0[:, 768:1280], w_in0[:, 1280:1344], w_in0[:, 1344:1352]
    rest = w_in0[:, 1352:]
    w_in_p = np.ascontiguousarray(np.concatenate([aq_perm, ak, ik, av, iw, iq, rest], axis=1))
    assert w_in_p.shape == (D, INW)
    shared = dict(_consts())
    shared["w_in"] = w_in_p
    shared["w_out"] = np.ascontiguousarray(np.asarray(w_out, f32)[0])
    shared["w_up"] = np.ascontiguousarray(np.asarray(w_up, f32)[0])
    shared["w_down"] = np.ascontiguousarray(np.asarray(w_down, f32)[0])
    shared["p_g1T"] = np.ascontiguousarray(np.asarray(norm1_g, f32)[0].reshape(8, 128).T)
    shared["p_g2T"] = np.ascontiguousarray(np.asarray(norm2_g, f32)[0].reshape(8, 128).T)
    qg = np.asarray(q_norm_g, f32)[0]
    kg = np.asarray(k_norm_g, f32)[0]
    ig = np.asarray(idx_k_norm_g, f32)[0]
    shared["p_qg2"] = np.concatenate([qg, qg])[:, None].astype(f32)
    shared["p_kig"] = np.tile(np.concatenate([kg, kg, ig])[None, :], (128, 1)).astype(f32)
    shared["p_rng"] = np.tile(np.tile(np.asarray(rec_norm_g, f32)[0], 4)[None, :], (128, 1)).astype(f32)
    lbs = np.asarray(lower_bounds, f32)
    shared["p_lb"] = np.tile(np.concatenate([lbs[0], lbs[1]])[None, :], (128, 1)).astype(f32)
    shared["p_qgb"] = np.tile(np.tile(qg, 8)[None, :], (128, 1)).astype(f32)
    shared["cache_kidx"] = np.asarray(cache_kidx, f32)[0].reshape(NPOOL, 8192)
    shared["cache_kv"] = np.concatenate([np.asarray(cache_k, f32)[0].reshape(NPOOL * 128, 128),
                                         np.asarray(cache_v, f32)[0].reshape(NPOOL * 128, 128)], axis=1)
    x_sample = np.asarray(x_sample, f32)
    state_hgrn = np.asarray(state_hgrn, f32)
    page_table = np.asarray(page_table, np.int32)

    if "nc" not in _NC_CACHE:
        _NC_CACHE["nc"] = build()
    nc = _NC_CACHE["nc"]
    in_maps = []
    for c in range(NCORES):
        m = dict(shared)
        m["xp"] = np.ascontiguousarray(x_prompt[NSEQ * c:NSEQ * (c + 1)].reshape(NTOK, D))
        m["xs"] = np.ascontiguousarray(x_sample[4 * c:4 * (c + 1), 0, :])
        m["state_s"] = np.ascontiguousarray(state_hgrn[0, 4 * c:4 * (c + 1)])
        m["ptT"] = np.ascontiguousarray(page_table[4 * c:4 * (c + 1)].T)
        in_maps.append(m)
    res = run_bass_kernel_spmd(nc, in_maps, core_ids=list(range(NCORES)))
    R = res.results
    B = x_prompt.shape[0]
    y_prompt = np.concatenate([r["yp"].reshape(NSEQ, SEQ, D) for r in R], axis=0)
    nkp = np.concatenate([r["nk"].reshape(NSEQ, SEQ, 2, 64) for r in R], axis=0)[None]
    nvp = np.concatenate([r["nv"].reshape(NSEQ, SEQ, 2, 64) for r in R], axis=0)[None]
    nkip = np.concatenate([r["nki"].reshape(NSEQ, SEQ, 64) for r in R], axis=0)[None]
    nstp = np.concatenate([r["nst"] for r in R], axis=0)[None]
    y_sample = np.concatenate([r["ys"] for r in R], axis=0)[:, None, :]
    nks = np.concatenate([r["nks"].reshape(4, 1, 2, 64) for r in R], axis=0)[None]
    nvs = np.concatenate([r["nvs"].reshape(4, 1, 2, 64) for r in R], axis=0)[None]
    nkis = np.concatenate([r["nkis"].reshape(4, 1, 64) for r in R], axis=0)[None]
    nsts = np.concatenate([r["nsts"] for r in R], axis=0)[None]
    return (y_prompt.astype(f32), y_sample.astype(f32), nkp.astype(f32), nvp.astype(f32), nkip.astype(f32), nstp.astype(f32),
            nks.astype(f32), nvs.astype(f32), nkis.astype(f32), nsts.astype(f32))
```

```python
import os
from contextlib import ExitStack
import numpy as np
import concourse.bass as bass
import concourse.mybir as mybir
from concourse.bass_utils import run_bass_kernel_spmd

F32 = mybir.dt.float32
BF16 = mybir.dt.bfloat16
I32 = mybir.dt.int32
U32 = mybir.dt.uint32
AF = mybir.ActivationFunctionType
ALU = mybir.AluOpType
AX = mybir.AxisListType

NCORES = 8
D = 1024
SEQ = 2048
NT = SEQ // 128
NSEQ = 2
NTOK = NSEQ * SEQ
INW = 3400
O_AQ, O_KVI, O_IQ, O_RQ, O_RF, O_RI, O_RG = 0, 512, 840, 1352, 1864, 2376, 2888
EPS = 1e-6
N_IT = int(os.environ.get("K_NIT", "13"))
NI_S = 21
NPOOL = 5120
BIGNEG = -1.0e30
SAMPLE_ATT = int(os.environ.get('K_SATT', '1'))
IDX_W_SCALE = (8 ** -0.5) * (64 ** -0.5)
NEG = -1.0e4
COMPUTE = ("pe", "act", "dve", "pool")
DBG_NT = int(os.environ.get('K_NT', '16'))
DBG_NSEQ = int(os.environ.get('K_NSEQ', '2'))
DBG_B = int(os.environ.get('K_B', '1'))
DBG_STOP = float(os.environ.get('K_STOP', '99'))


class StopBuild(Exception):
    pass


def stage(n):
    if n > DBG_STOP:
        raise StopBuild()


class Prog:
    def __init__(self, nc, stack, tag, n_dma_sems=10):
        self.nc = nc
        self.ops = {e: [] for e in ("pe", "act", "dve", "pool", "sp")}
        self.cnt = {e: 0 for e in COMPUTE}
        self.sem = {e: stack.enter_context(nc.semaphore(f"{tag}s_{e}")) for e in COMPUTE}
        self.dsem, self.dcnt, self.dnext = {}, {}, {}
        for q in ("sp", "act", "pool"):
            self.dsem[q] = [stack.enter_context(nc.semaphore(f"{tag}d_{q}{i}")) for i in range(n_dma_sems)]
            self.dcnt[q] = [0] * n_dma_sems
            self.dnext[q] = 0
        self.seen = {e: {} for e in self.ops}
        self.res = {}
        self.out_tokens = []
        self.rec = None
        self.threads = {}

    def _need(self, consumer, tok, waits):
        if tok is None:
            return
        key, val, prod = tok
        if prod == "pe" and consumer == "pe":
            return
        if self.seen[consumer].get(key, 0) >= val:
            return
        self.seen[consumer][key] = val
        waits.append((key, val))

    def _deps(self, consumer, reads, writes):
        waits = []
        for r in reads:
            st = self.res.get(r)
            if st:
                self._need(consumer, st["w"], waits)
        for w in writes:
            st = self.res.get(w)
            if st:
                self._need(consumer, st["w"], waits)
                for t in st["r"]:
                    self._need(consumer, t, waits)
        return waits

    def _commit(self, tok, reads, writes):
        for r in reads:
            st = self.res.setdefault(r, {"w": None, "r": []})
            st["r"].append(tok)
            if len(st["r"]) > 40:
                st["r"] = st["r"][-40:]
        for w in writes:
            self.res[w] = {"w": tok, "r": []}

    def begin(self, name):
        self.rec = self.threads.setdefault(name, [])

    def end(self):
        self.rec = None

    def mark(self):
        if self.rec is not None:
            self.rec.append(("mark",))

    def merge(self, a, b):
        la, lb = self.threads.pop(a, []), self.threads.pop(b, [])
        self.rec = None
        marks = [i for i, it in enumerate(la) if it[0] == "mark"]
        mode = os.environ.get("K_MERGE", "4")
        if len(marks) >= 2 and mode == "0":
            m1, m2 = 0, marks[1]
        elif len(marks) >= 2 and mode == "1":
            m1, m2 = marks[0], len(la)
        elif len(marks) >= 2 and mode == "4":
            m1, m2 = marks[0], marks[1]
        elif len(marks) >= 2 and mode == "5":
            m1, m2 = max(0, marks[0] - 40), marks[1]
        elif len(marks) >= 2 and mode == "3":
            m1, m2 = marks[1], len(la)
        else:
            m1, m2 = 0, len(la)
        seg = [it for it in la[m1:m2] if it[0] != "mark"]
        for it in la[:m1]:
            if it[0] != "mark":
                self._replay(it)
        na, nb = len(seg), len(lb)
        ib = 0
        for ia, item in enumerate(seg):
            self._replay(item)
            want = ((ia + 1) * nb) // max(1, na)
            while ib < want:
                self._replay(lb[ib])
                ib += 1
        while ib < nb:
            self._replay(lb[ib])
            ib += 1
        for it in la[m2:]:
            if it[0] != "mark":
                self._replay(it)

    def _replay(self, item):
        if item[0] == "op":
            self.op(item[1], item[2], item[3], item[4])
        else:
            self.dma(item[1], item[2], item[3], item[4], item[5])

    def op(self, e, fn, reads=(), writes=()):
        reads, writes = list(reads), list(writes)
        if self.rec is not None:
            self.rec.append(("op", e, fn, reads, writes))
            return None
        waits = self._deps(e, reads, writes)
        self.cnt[e] += 1
        tok = (("c", e), self.cnt[e], e)
        self.ops[e].append((waits, fn, ("c", e)))
        self._commit(tok, reads, writes)
        return tok

    def dma(self, q, fn, reads=(), writes=(), is_output=False):
        reads, writes = list(reads), list(writes)
        if self.rec is not None:
            self.rec.append(("dma", q, fn, reads, writes, is_output))
            return None
        waits = self._deps(q, reads, writes)
        i = self.dnext[q]
        self.dnext[q] = (i + 1) % len(self.dsem[q])
        prev = self.dcnt[q][i]
        key = ("d", q, i)
        if prev > 0 and self.seen[q].get(key, 0) < prev:
            self.seen[q][key] = prev
            waits.append((key, prev))
        self.dcnt[q][i] = prev + 16
        tok = (key, prev + 16, "dma")
        self.ops[q].append((waits, fn, key))
        self._commit(tok, reads, writes)
        if is_output:
            self.out_tokens.append(tok)
        return tok

    def _semh(self, key):
        return self.sem[key[1]] if key[0] == "c" else self.dsem[key[1]][key[2]]

    def finish(self):
        for c in self.ops:
            waits = []
            for e in COMPUTE:
                if self.cnt[e] and e != c:
                    self._need(c, (("c", e), self.cnt[e], e), waits)
            for q in self.dsem:
                for i, v in enumerate(self.dcnt[q]):
                    if v:
                        self._need(c, (("d", q, i), v, "dma"), waits)
            self.ops[c].append((waits, None, None))

    def emit(self, block):
        prog = self

        def run(ename):
            def body(eng):
                for waits, fn, key in prog.ops[ename]:
                    for k, v in waits:
                        eng.wait_ge(prog._semh(k), v)
                    if fn is None:
                        continue
                    ins = fn(eng)
                    ins.then_inc(prog._semh(key), 1 if key[0] == "c" else 16)
            return body

        block.tensor(run("pe"))
        block.scalar(run("act"))
        block.vector(run("dve"))
        block.gpsimd(run("pool"))
        block.sync(run("sp"))


def bc(ap, axis, shape):
    return ap.unsqueeze(axis).to_broadcast(list(shape))


def build(with_sample=bool(int(os.environ.get("K_S", "1")))):
    nc = bass.Bass("TRN2", target_bir_lowering=False)

    def din(name, shape, dt=F32):
        return nc.dram_tensor(name, list(shape), dt, kind="ExternalInput").ap()

    def dout(name, shape, dt=F32):
        return nc.dram_tensor(name, list(shape), dt, kind="ExternalOutput").ap()

    xp = din("xp", [NTOK, D])
    w_in = din("w_in", [D, INW])
    w_out = din("w_out", [D, D])
    w_up = din("w_up", [D, 4 * D])
    w_down = din("w_down", [4 * D, D])
    c_ident = din("c_ident", [128, 128])
    c_mb = din("c_mb", [128, 128])
    c_ind = din("c_ind", [128, 2])
    c_ct = din("c_ct", [128, 128])
    c_negc = din("c_negc", [128, 128])
    c_p2 = din("c_p2", [128, N_IT + 1])
    p_g1T = din("p_g1T", [128, 8])
    p_g2T = din("p_g2T", [128, 8])
    p_qg2 = din("p_qg2", [128, 1])
    p_kig = din("p_kig", [128, 192])
    p_rng = din("p_rng", [128, 512])
    p_lb = din("p_lb", [128, 1024])

    xs_d = din("xs", [4, D])
    state_d = din("state_s", [4, 4, 128, 128])
    ptT_d = din("ptT", [128, 4], I32)
    ckidx_d = din("cache_kidx", [NPOOL, 8192])
    ckv_d = din("cache_kv", [NPOOL * 128, 256])
    c_selb = din("c_selb", [4, 4, 128])
    c_eye4 = din("c_eye4", [4, 4])
    c_oh4 = din("c_oh4", [128, 16])
    c_negeye = din("c_negeye", [4, 4])
    c_hsel = din("c_hsel", [8, 2])
    c_eye8 = din("c_eye8", [8, 8])
    c_sel8 = din("c_sel8", [8, 4, 4])
    c_ones = din("c_ones", [128, 128])
    c_p2s = din("c_p2s", [128, NI_S + 1])
    p_qgb = din("p_qgb", [128, 512])
    ys_d = dout("ys", [4, D])
    nks_d = dout("nks", [4, 128])
    nvs_d = dout("nvs", [4, 128])
    nkis_d = dout("nkis", [4, 64])
    nsts_d = dout("nsts", [4, 4, 128, 128])
    yp = dout("yp", [NTOK, D])
    nk = dout("nk", [NTOK, 128])
    nv = dout("nv", [NTOK, 128])
    nki = dout("nki", [NTOK, 64])
    nst = dout("nst", [NSEQ, 4, 128, 128])

    with ExitStack() as st0:
        def sb(stack, name, shape, dt):
            return stack.enter_context(nc.sbuf_tensor(name, list(shape), dt))

        identb = sb(st0, "identb", [128, 128], BF16)
        identf = sb(st0, "identf", [128, 128], F32)
        mbf = sb(st0, "mbf", [128, 128], F32)
        indf = sb(st0, "indf", [128, 2], F32)
        ctf = sb(st0, "ctf", [128, 128], F32)
        negc = sb(st0, "negc", [128, 128], F32)
        p2 = sb(st0, "p2", [128, N_IT + 1], F32)
        g1T = sb(st0, "g1T", [128, 8], F32)
        g2T = sb(st0, "g2T", [128, 8], F32)
        qg2 = sb(st0, "qg2", [128, 1], F32)
        kig = sb(st0, "kig", [128, 192], F32)
        rngb = sb(st0, "rngb", [128, 512], F32)
        lbt = sb(st0, "lbt", [128, 512], F32)
        oml = sb(st0, "oml", [128, 512], F32)
        PS = [st0.enter_context(nc.psum_tensor(f"ps{i}", [128, 512], F32)) for i in range(8)]
        PSB = [p[:].bitcast(BF16) for p in PS]
        P = Prog(nc, st0, "A")

        lb0t = sb(st0, "lb0t", [128, 512], F32)
        lb1t = sb(st0, "lb1t", [128, 512], F32)
        xmid_s = sb(st0, "xmid_s", [4, D], F32)
        stW = st0.enter_context(ExitStack())
        wi = sb(stW, "wi", [128, 8, INW], BF16)
        wo = sb(stW, "wo", [128, 8, D], BF16)
        epsc = sb(stW, "epsc", [128, 1], F32)
        eps_ap = epsc[:, 0:1]
        for c in range(8):
            P.dma("pool", lambda e, c=c: e.dma_start(out=wi[:, c, :], in_=w_in[c * 128:(c + 1) * 128, :]), writes=[f"wi{c}"])
        for c in range(8):
            P.dma("pool", lambda e, c=c: e.dma_start(out=wo[:, c, :], in_=w_out[c * 128:(c + 1) * 128, :]), writes=[f"wo{c}"])
        P.dma("pool", lambda e: e.dma_start(out=identb[:], in_=c_ident), writes=["identb"])
        for dst, src, nm in ((identf, c_ident, "identf"), (mbf, c_mb, "mbf"), (indf, c_ind, "indf"), (ctf, c_ct, "ctf"),
                             (negc, c_negc, "negc"), (p2, c_p2, "p2"), (g1T, p_g1T, "g1T"), (g2T, p_g2T, "g2T"),
                             (qg2, p_qg2, "qg2"), (kig, p_kig, "kig"), (rngb, p_rng, "rngb")):
            P.dma("sp", lambda e, dst=dst, src=src: e.dma_start(out=dst[:], in_=src), writes=[nm])
        P.dma("sp", lambda e: e.dma_start(out=lb0t[:], in_=p_lb[:, 0:512]), writes=["lb0t"])
        P.dma("sp", lambda e: e.dma_start(out=lb1t[:], in_=p_lb[:, 512:1024]), writes=["lb1t"])
        P.op("dve", lambda e: e.tensor_tensor(out=lb1t[:], in0=lb1t[:], in1=lb0t[:], op=ALU.subtract), reads=["lb0t", "lb1t"], writes=["lb1t"])
        P.op("act", lambda e: e.activation(out=lb1t[:], in_=lb1t[:], func=AF.Exp), reads=["lb1t"], writes=["lb1t"])
        P.op("dve", lambda e: e.tensor_scalar(out=lb1t[:], in0=lb1t[:], scalar1=1.0, scalar2=None, op0=ALU.add), reads=["lb1t"], writes=["lb1t"])
        P.op("dve", lambda e: e.reciprocal(out=lbt[:], in_=lb1t[:]), reads=["lb1t"], writes=["lbt"])
        P.op("dve", lambda e: e.tensor_scalar(out=oml[:], in0=lbt[:], scalar1=-1.0, scalar2=1.0, op0=ALU.mult, op1=ALU.add), reads=["lbt"], writes=["oml"])
        P.op("pool", lambda e: e.memset(epsc[:], EPS), writes=["epsc"])

        def rstd_from(ss_ap, out_ap, n, names):
            P.op("act", lambda e: e.activation(out=out_ap, in_=ss_ap, func=AF.Ln, scale=1.0 / n, bias=eps_ap[0:ss_ap.shape[0], :]), reads=names + ["epsc"], writes=names)
            P.op("act", lambda e: e.activation(out=out_ap, in_=out_ap, func=AF.Exp, scale=-0.5), reads=names, writes=names)


        def emit_sample(st):
            R4 = slice(0, 4)
            xts = sb(st, "xts", [4, D], F32)
            mixs = sb(st, "mixs", [4, D], F32)
            mixsb = sb(st, "mixsb", [4, D], BF16)
            mixTs = sb(st, "mixTs", [128, 8, 4], BF16)
            sms = sb(st, "sms", [128, 64], F32)
            kiks = sb(st, "kiks", [4, 192], F32)
            vfs = sb(st, "vfs", [4, 128], F32)
            qs = sb(st, "qs", [4, 512], F32)
            iqs = sb(st, "iqs", [4, 512], F32)
            selb = sb(st, "selb", [4, 4, 128], F32)
            eye4 = sb(st, "eye4", [4, 4], F32)
            oh4 = sb(st, "oh4", [128, 16], F32)
            negeye = sb(st, "negeye", [4, 4], F32)
            hsel = sb(st, "hsel", [8, 2], F32)
            eye8 = sb(st, "eye8", [8, 8], F32)
            sel8 = sb(st, "sel8", [8, 4, 4], F32)
            ones = sb(st, "ones", [128, 128], F32)
            p2s = sb(st, "p2s", [128, NI_S + 1], F32)
            qgb = sb(st, "qgb", [4, 512], F32)
            st1 = ExitStack()
            xns = sb(st1, "xns", [4, D], BF16)
            jks = sb(st1, "jks", [4, D], BF16)
            hTs = sb(st1, "hTs", [128, 8, 4], BF16)
            sqs = sb(st1, "sqs", [4, 704], F32)
            tS = [sb(st1, f"tS{i}", [4, 512], F32) for i in range(8)]
            fkq = sb(st1, "fkq", [128, 12, 4], F32)
            qsel = sb(st1, "qsel", [128, 4, 4, 4], F32)
            S0 = [sb(st1, f"S0{i}", [128, 4, 128], F32) for i in range(2)]
            Sn = [sb(st1, f"Sn{i}", [128, 4, 128], F32) for i in range(2)]
            tmpS = sb(st1, "tmpS", [128, 4, 128], F32)
            o_all = sb(st1, "o_all", [4, 512], F32)
            for dst, src, nm in ((selb, c_selb, "selb"), (eye4, c_eye4, "eye4"), (oh4, c_oh4, "oh4"), (negeye, c_negeye, "negeye"),
                                 (hsel, c_hsel, "hsel"), (eye8, c_eye8, "eye8"), (sel8, c_sel8, "sel8"), (ones, c_ones, "ones"),
                                 (p2s, c_p2s, "p2s"), (qgb, p_qgb[0:4, :], "qgb"), (xts, xs_d, "xts")):
                P.dma("sp", lambda e, dst=dst, src=src: e.dma_start(out=dst[:], in_=src), writes=[nm])
            P.op("act", lambda e: e.activation(out=jks[:], in_=xts[:], func=AF.Square, accum_out=sms[R4, 0:1]), reads=["xts"], writes=["jks", "s0"])
            rstd_from(sms[R4, 0:1], sms[R4, 1:2], D, ["s0"])
            P.op("act", lambda e: e.activation(out=xns[:], in_=xts[:], func=AF.Copy, scale=sms[R4, 1:2]), reads=["xts", "s0"], writes=["xns"])
            for c in range(8):
                P.op("pe", lambda e, c=c: e.transpose(PSB[0][:, c * 128:c * 128 + 4], xns[R4, c * 128:(c + 1) * 128], identb[0:4, 0:4]),
                     reads=["xns", "identb"], writes=["ps0"])
            P.op("dve", lambda e: e.tensor_tensor(out=hTs[:], in0=PSB[0][:].rearrange("p (c n) -> p c n", c=8)[:, :, 0:4],
                                                  in1=bc(g1T[:], 2, [128, 8, 4]), op=ALU.mult), reads=["ps0", "g1T"], writes=["hTs"])
            def proj_s(bank, off, width):
                for c in range(8):
                    P.op("pe", lambda e, c=c: e.matmul(PS[bank][R4, 0:width], lhsT=hTs[:, c, :], rhs=wi[:, c, off:off + width],
                                                       start=(c == 0), stop=(c == 7)), reads=["hTs", f"wi{c}"], writes=[f"ps{bank}"])
            proj_s(2, O_KVI, 328)
            proj_s(1, O_AQ, 512)
            proj_s(3, O_IQ, 512)
            proj_s(5, O_RF, 512)
            proj_s(4, O_RQ, 512)
            proj_s(7, O_RG, 512)
            proj_s(6, O_RI, 512)
            P.op("act", lambda e: e.activation(out=sqs[:, 0:192], in_=PS[2][R4, 0:192], func=AF.Square), reads=["ps2"], writes=["sqs"])
            P.op("act", lambda e: e.activation(out=sqs[:, 192:704], in_=PS[1][R4, 0:512], func=AF.Square), reads=["ps1"], writes=["sqs"])
            P.op("dve", lambda e: e.tensor_reduce(out=sms[R4, 8:19], in_=sqs[:].rearrange("p (a b) -> p a b", b=64), axis=AX.X, op=ALU.add),
                 reads=["sqs"], writes=["s8"])
            rstd_from(sms[R4, 8:19], sms[R4, 8:19], 64, ["s8"])
            for a in range(3):
                P.op("act", lambda e, a=a: e.activation(out=kiks[:, a * 64:(a + 1) * 64], in_=PS[2][R4, a * 64:(a + 1) * 64], func=AF.Copy, scale=sms[R4, 8 + a:9 + a]),
                     reads=["ps2", "s8"], writes=["kiks"])
            P.op("dve", lambda e: e.tensor_tensor(out=kiks[:], in0=kiks[:], in1=kig[R4, :], op=ALU.mult), reads=["kiks", "kig"], writes=["kiks"])
            P.op("act", lambda e: e.activation(out=vfs[:], in_=PS[2][R4, 192:320], func=AF.Copy), reads=["ps2"], writes=["vfs"])
            P.op("act", lambda e: e.activation(out=sms[R4, 24:32], in_=PS[2][R4, 320:328], func=AF.Copy, scale=IDX_W_SCALE), reads=["ps2"], writes=["s24"])
            P.dma("sp", lambda e: e.dma_start(out=nks_d, in_=kiks[:, 0:128]), reads=["kiks"], is_output=True)
            P.dma("sp", lambda e: e.dma_start(out=nkis_d, in_=kiks[:, 128:192]), reads=["kiks"], is_output=True)
            P.dma("sp", lambda e: e.dma_start(out=nvs_d, in_=vfs[:]), reads=["vfs"], is_output=True)
            for a in range(8):
                P.op("act", lambda e, a=a: e.activation(out=qs[:, a * 64:(a + 1) * 64], in_=PS[1][R4, a * 64:(a + 1) * 64], func=AF.Copy, scale=sms[R4, 11 + a:12 + a]),
                     reads=["ps1", "s8"], writes=["qs"])
            P.op("dve", lambda e: e.tensor_tensor(out=qs[:], in0=qs[:], in1=qgb[:], op=ALU.mult), reads=["qs", "qgb"], writes=["qs"])
            P.op("act", lambda e: e.activation(out=iqs[:], in_=PS[3][R4, :], func=AF.Copy), reads=["ps3"], writes=["iqs"])
            t_ef, t_eq, t_eg, t_f, t_k, t_q, t_gate, t_v = tS
            n_ef, n_eq, n_eg, n_f, n_k, n_q, n_gate, n_v = [f"tS{i}" for i in range(8)]
            P.op("act", lambda e: e.activation(out=t_ef[:], in_=PS[5][R4, :], func=AF.Exp, scale=-1.0), reads=["ps5"], writes=[n_ef])
            P.op("act", lambda e: e.activation(out=t_eq[:], in_=PS[4][R4, :], func=AF.Exp, scale=-1.0), reads=["ps4"], writes=[n_eq])
            P.op("act", lambda e: e.activation(out=t_eg[:], in_=PS[7][R4, :], func=AF.Exp, scale=-1.0), reads=["ps7"], writes=[n_eg])
            P.op("act", lambda e: e.activation(out=t_v[:], in_=PS[6][R4, :], func=AF.Copy), reads=["ps6"], writes=[n_v])
            for tt, nn in ((t_ef, n_ef), (t_eq, n_eq), (t_eg, n_eg)):
                P.op("dve", lambda e, tt=tt: e.tensor_scalar(out=tt[:], in0=tt[:], scalar1=1.0, scalar2=None, op0=ALU.add), reads=[nn], writes=[nn])
                P.op("dve", lambda e, tt=tt: e.reciprocal(out=tt[:], in_=tt[:]), reads=[nn], writes=[nn])
            P.op("dve", lambda e: e.tensor_tensor(out=t_f[:], in0=t_ef[:], in1=oml[R4, :], op=ALU.mult), reads=[n_ef, "oml"], writes=[n_f])
            P.op("dve", lambda e: e.tensor_tensor(out=t_f[:], in0=t_f[:], in1=lbt[R4, :], op=ALU.add), reads=[n_f, "lbt"], writes=[n_f])
            P.op("dve", lambda e: e.tensor_scalar(out=t_k[:], in0=t_f[:], scalar1=-1.0, scalar2=1.0, op0=ALU.mult, op1=ALU.add), reads=[n_f], writes=[n_k])
            P.op("act", lambda e: e.activation(out=t_q[:], in_=PS[4][R4, :], func=AF.Copy, scale=128 ** -0.5), reads=["ps4"], writes=[n_q])
            P.op("dve", lambda e: e.tensor_tensor(out=t_q[:], in0=t_q[:], in1=t_eq[:], op=ALU.mult), reads=[n_q, n_eq], writes=[n_q])
            P.op("act", lambda e: e.activation(out=t_gate[:], in_=PS[7][R4, :], func=AF.Copy), reads=["ps7"], writes=[n_gate])
            P.op("dve", lambda e: e.tensor_tensor(out=t_gate[:], in0=t_gate[:], in1=t_eg[:], op=ALU.mult), reads=[n_gate, n_eg], writes=[n_gate])
            for gi, (tt, nn) in enumerate(((t_f, n_f), (t_k, n_k), (t_q, n_q))):
                for h in range(4):
                    P.op("pe", lambda e, gi=gi, h=h, tt=tt: e.transpose(PS[4][:, (gi * 4 + h) * 4:(gi * 4 + h) * 4 + 4], tt[R4, h * 128:(h + 1) * 128], identf[0:4, 0:4]),
                         reads=[nn, "identf"], writes=["ps4"])
            P.op("act", lambda e: e.activation(out=fkq[:].rearrange("p a b -> p (a b)"), in_=PS[4][:, 0:48], func=AF.Copy), reads=["ps4"], writes=["fkq"])
            P.op("dve", lambda e: e.tensor_tensor(out=qsel[:], in0=fkq[:, 8:12, :].unsqueeze(1).to_broadcast([128, 4, 4, 4]),
                                                  in1=oh4[:].rearrange("p (b t) -> p b t", b=4).unsqueeze(2).to_broadcast([128, 4, 4, 4]), op=ALU.mult),
                 reads=["fkq", "oh4"], writes=["qsel"])
            P.op("dve", lambda e: e.memset(o_all[:], 0.0), writes=["o_all"])
            for b in range(4):
                s0, s0n = S0[b % 2], f"S0{b % 2}"
                sn, snn = Sn[b % 2], f"Sn{b % 2}"
                P.dma("sp", lambda e, b=b, s0=s0: e.dma_start(out=s0[:], in_=state_d[b].rearrange("h k v -> k h v")), writes=[s0n])
                P.op("pe", lambda e, b=b: e.matmul(PS[5][:], lhsT=selb[:, b, :], rhs=t_v[:], start=True, stop=True), reads=["selb", n_v], writes=["ps5"])
                for h in range(4):
                    P.op("act", lambda e, h=h, b=b: e.activation(out=tmpS[:, h, :], in_=PS[5][:, h * 128:(h + 1) * 128], func=AF.Copy, scale=fkq[:, 4 + h, b:b + 1]),
                         reads=["ps5", "fkq"], writes=["tmpS"])
                    P.op("dve", lambda e, h=h, b=b, s0=s0, sn=sn: e.scalar_tensor_tensor(out=sn[:, h, :], in0=s0[:, h, :], scalar=fkq[:, h, b:b + 1], in1=tmpS[:, h, :],
                                                                                 op0=ALU.mult, op1=ALU.add), reads=[s0n, "fkq", "tmpS"], writes=[snn])
                P.dma("sp", lambda e, b=b, sn=sn: e.dma_start(out=nsts_d[b].rearrange("h k v -> k h v"), in_=sn[:]), reads=[snn], is_output=True)
                for h in range(4):
                    P.op("pe", lambda e, h=h, b=b, sn=sn: e.matmul(PS[6][R4, h * 128:(h + 1) * 128], lhsT=qsel[:, b, h, :], rhs=sn[:, h, :], start=True, stop=True),
                         reads=["qsel", snn], writes=["ps6"])
                P.op("dve", lambda e: e.tensor_tensor(out=o_all[:], in0=PS[6][R4, :], in1=o_all[:], op=ALU.add), reads=["ps6", "o_all"], writes=["o_all"])
            P.op("act", lambda e: e.activation(out=t_ef[:], in_=o_all[:], func=AF.Square), reads=["o_all"], writes=[n_ef])
            P.op("dve", lambda e: e.tensor_reduce(out=sms[R4, 44:48], in_=t_ef[:].rearrange("p (a b) -> p a b", b=128), axis=AX.X, op=ALU.add), reads=[n_ef], writes=["s44"])
            rstd_from(sms[R4, 44:48], sms[R4, 44:48], 128, ["s44"])
            for h in range(4):
                P.op("act", lambda e, h=h: e.activation(out=mixs[:, 512 + h * 128:512 + (h + 1) * 128], in_=o_all[:, h * 128:(h + 1) * 128], func=AF.Copy, scale=sms[R4, 44 + h:45 + h]),
                     reads=["o_all", "s44"], writes=["mixs_r"])
            P.op("dve", lambda e: e.tensor_tensor(out=mixs[:, 512:1024], in0=mixs[:, 512:1024], in1=rngb[R4, :], op=ALU.mult), reads=["mixs_r", "rngb"], writes=["mixs_r"])
            P.op("dve", lambda e: e.tensor_tensor(out=mixs[:, 512:1024], in0=mixs[:, 512:1024], in1=t_gate[:], op=ALU.mult), reads=["mixs_r", n_gate], writes=["mixs_r"])
            P.finish()
            st1.close()
            if SAMPLE_ATT:
                emit_sample_att(st, dict(sms=sms, kiks=kiks, vfs=vfs, qs=qs, iqs=iqs, selb=selb, eye4=eye4, negeye=negeye, hsel=hsel, eye8=eye8,
                                         sel8=sel8, ones=ones, p2s=p2s, mixs=mixs))
            else:
                P.op("dve", lambda e: e.memset(mixs[:, 0:512], 0.0), writes=["mixs_a"])
            P.op("act", lambda e: e.activation(out=mixsb[:], in_=mixs[:], func=AF.Copy), reads=["mixs_a", "mixs_r"], writes=["mixsb"])
            for c in range(8):
                P.op("pe", lambda e, c=c: e.transpose(PSB[0][:, c * 128:c * 128 + 4], mixsb[R4, c * 128:(c + 1) * 128], identb[0:4, 0:4]),
                     reads=["mixsb", "identb"], writes=["ps0"])
            P.op("act", lambda e: e.activation(out=mixTs[:], in_=PSB[0][:].rearrange("p (c n) -> p c n", c=8)[:, :, 0:4], func=AF.Copy), reads=["ps0"], writes=["mixTs"])
            for half in range(2):
                ob = 1 + half
                for c in range(8):
                    P.op("pe", lambda e, c=c, half=half, ob=ob: e.matmul(PS[ob][R4, :], lhsT=mixTs[:, c, :], rhs=wo[:, c, half * 512:(half + 1) * 512],
                                                                         start=(c == 0), stop=(c == 7)), reads=["mixTs", f"wo{c}"], writes=[f"ps{ob}"])
                P.op("dve", lambda e, half=half, ob=ob: e.tensor_tensor(out=xmid_s[:, half * 512:(half + 1) * 512], in0=PS[ob][R4, :],
                                                                       in1=xts[:, half * 512:(half + 1) * 512], op=ALU.add), reads=[f"ps{ob}", "xts"], writes=["xmid_s"])


        def emit_sample_att(st, T):
            R4 = slice(0, 4)
            sms, kiks, vfs, qs, iqs = T["sms"], T["kiks"], T["vfs"], T["qs"], T["iqs"]
            selb, eye4, negeye, hsel, eye8, sel8, ones, p2s, mixs = (T[k] for k in ("selb", "eye4", "negeye", "hsel", "eye8", "sel8", "ones", "p2s", "mixs"))
            pti = sb(st, "pti", [128, 4], I32)
            pt128 = sb(st, "pt128", [128, 4], F32)
            G = sb(st, "G", [128, 8192], F32)
            ikTa = [sb(st, f"ikTa{i}", [64, 4, 128], BF16) for i in range(2)]
            IQs = sb(st, "IQs", [64, 8, 4], BF16)
            Rr = sb(st, "Rr", [128, 1024], F32)
            sc = sb(st, "sc", [128, 4, 129], F32)
            sc2 = sb(st, "sc2", [128, 128], F32)
            cmpb = sb(st, "cmpb", [128, 4, 129], F32)
            wbc = sb(st, "wbc", [128, 4, 8], F32)
            hfs = sb(st, "hfs", [128, 4, NI_S + 1], F32)
            s2 = sb(st, "s2", [128, 32], F32)
            tp = sb(st, "tp", [4, 512], F32)
            v16 = sb(st, "v16", [128, 16], F32)
            i16 = sb(st, "i16", [128, 16], U32)
            idxf = sb(st, "idxf", [128, 16], F32)
            rowi = sb(st, "rowi", [128, 16], I32)
            valid = sb(st, "valid", [128, 17], F32)
            KVs = sb(st, "KVs", [128, 17, 257], F32)
            qb = sb(st, "qb", [128, 512], F32)
            prodL = sb(st, "prodL", [128, 17, 64], F32)
            lg = sb(st, "lg", [128, 17, 8], F32)
            Pm = sb(st, "Pm", [128, 17, 8], F32)
            o8 = sb(st, "o8", [8, 130], F32)
            a8 = sb(st, "a8", [8, 64], F32)
            Y = sb(st, "Y", [8, 8, 64], F32)
            mid = s2[:, 0:4]
            midm = s2[:, 4:8]
            cnt4 = s2[:, 8:12]
            tot = s2[:, 12:16]
            ge2 = s2[:, 16:20]
            tmp4 = s2[:, 20:24]
            m4 = s2[:, 24:28]
            mtot = s2[:, 28:32]
            P.dma("sp", lambda e: e.dma_start(out=pti[:], in_=ptT_d), writes=["pti"])
            P.op("dve", lambda e: e.tensor_copy(out=pt128[:], in_=pti[:]), reads=["pti"], writes=["pt128"])
            P.op("dve", lambda e: e.tensor_scalar(out=pt128[:], in0=pt128[:], scalar1=128.0, scalar2=None, op0=ALU.mult), reads=["pt128"], writes=["pt128"])
            P.op("pool", lambda e: e.memset(mixs[:, 0:512], 0.0), writes=["mixs_a"])
            P.op("pool", lambda e: e.memset(KVs[:], 0.0), writes=["KVs"])
            P.op("pool", lambda e: e.memset(KVs[:, :, 256:257], 1.0), reads=["KVs"], writes=["KVs"])
            P.op("dve", lambda e: e.tensor_copy(out=KVs[R4, 16, 0:128], in_=kiks[:, 0:128]), reads=["kiks", "KVs"], writes=["KVs"])
            P.op("dve", lambda e: e.tensor_copy(out=KVs[R4, 16, 128:256], in_=vfs[:]), reads=["vfs", "KVs"], writes=["KVs"])
            for h in range(8):
                P.op("pe", lambda e, h=h: e.transpose(PS[1][0:64, h * 4:(h + 1) * 4], iqs[R4, h * 64:(h + 1) * 64], identf[0:4, 0:4]),
                     reads=["iqs", "identf"], writes=["ps1"])
            P.op("act", lambda e: e.activation(out=IQs[:].rearrange("p a b -> p (a b)"), in_=PS[1][0:64, 0:32], func=AF.Copy), reads=["ps1"], writes=["IQs"])
            for b in range(4):
                P.op("pe", lambda e, b=b: e.matmul(PS[7][:, b * 8:(b + 1) * 8], lhsT=selb[:, b, :], rhs=sms[R4, 24:32], start=True, stop=True),
                     reads=["selb", "s24"], writes=["ps7"])
            P.op("act", lambda e: e.activation(out=wbc[:].rearrange("p a b -> p (a b)"), in_=PS[7][:, 0:32], func=AF.Copy), reads=["ps7"], writes=["wbc"])
            P.op("dve", lambda e: e.tensor_tensor(out=tp[:].rearrange("p (h d) -> p h d", d=64), in0=iqs[:].rearrange("p (h d) -> p h d", d=64),
                                                  in1=kiks[:, 128:192].unsqueeze(1).to_broadcast([4, 8, 64]), op=ALU.mult), reads=["iqs", "kiks"], writes=["tp"])
            P.op("dve", lambda e: e.tensor_reduce(out=sms[R4, 32:40], in_=tp[:].rearrange("p (h d) -> p h d", d=64), axis=AX.X, op=ALU.add), reads=["tp"], writes=["s32"])
            P.op("dve", lambda e: e.tensor_scalar(out=sms[R4, 32:40], in0=sms[R4, 32:40], scalar1=0.0, scalar2=None, op0=ALU.max), reads=["s32"], writes=["s32"])
            P.op("dve", lambda e: e.tensor_tensor(out=sms[R4, 32:40], in0=sms[R4, 32:40], in1=sms[R4, 24:32], op=ALU.mult), reads=["s32", "s24"], writes=["s32"])
            P.op("dve", lambda e: e.tensor_reduce(out=sms[R4, 40:41], in_=sms[R4, 32:40], axis=AX.X, op=ALU.add), reads=["s32"], writes=["s40"])
            for b in range(4):
                P.dma("pool", lambda e, b=b: e.indirect_dma_start(out=G[:], out_offset=None, in_=ckidx_d,
                                                                in_offset=bass.IndirectOffsetOnAxis(ap=pti[:, b:b + 1], axis=0)),
                      reads=["pti"], writes=["G"])
                for grp in range(32):
                    tb = 2 + grp % 2
                    ib = grp % 2
                    for oo in range(4):
                        o = grp * 4 + oo
                        P.op("pe", lambda e, o=o, oo=oo, tb=tb: e.transpose(PS[tb][0:64, oo * 128:(oo + 1) * 128], G[:, o * 64:(o + 1) * 64], identf[:]),
                             reads=["G", "identf"], writes=[f"ps{tb}"])
                    P.op("act", lambda e, tb=tb, ib=ib: e.activation(out=ikTa[ib][:].rearrange("p a b -> p (a b)"), in_=PS[tb][0:64, :], func=AF.Copy),
                         reads=[f"ps{tb}"], writes=[f"ikTa{ib}"])
                    for oo in range(4):
                        o = grp * 4 + oo
                        sbk = 4 + o // 64
                        P.op("pe", lambda e, o=o, oo=oo, ib=ib, sbk=sbk, b=b: e.matmul(PS[sbk][:, (o % 64) * 8:(o % 64 + 1) * 8], lhsT=ikTa[ib][:, oo, :], rhs=IQs[:, :, b],
                                                                                      start=True, stop=True), reads=[f"ikTa{ib}", "IQs"], writes=[f"ps{sbk}"])
                P.op("act", lambda e: e.activation(out=Rr[:, 0:512], in_=PS[4][:], func=AF.Relu), reads=["ps4"], writes=["Rr"])
                P.op("act", lambda e: e.activation(out=Rr[:, 512:1024], in_=PS[5][:], func=AF.Relu), reads=["ps5"], writes=["Rr"])
                P.op("dve", lambda e, b=b: e.tensor_tensor(out=Rr[:].rearrange("p (o h) -> p o h", h=8), in0=Rr[:].rearrange("p (o h) -> p o h", h=8),
                                                           in1=wbc[:, b, :].unsqueeze(1).to_broadcast([128, 128, 8]), op=ALU.mult), reads=["Rr", "wbc"], writes=["Rr"])
                P.op("dve", lambda e, b=b: e.tensor_reduce(out=sc[:, b, 0:128], in_=Rr[:].rearrange("p (o h) -> p o h", h=8), axis=AX.X, op=ALU.add),
                     reads=["Rr"], writes=["sc"])
                P.op("dve", lambda e, b=b: e.memset(sc[:, b, 128:129], BIGNEG), reads=["sc"], writes=["sc"])
                P.op("dve", lambda e, b=b: e.scalar_tensor_tensor(out=sc[R4, b, 128:129], in0=sms[R4, 40:41], scalar=eye4[:, b:b + 1], in1=negeye[:, b:b + 1],
                                                                  op0=ALU.mult, op1=ALU.add), reads=["s40", "eye4", "negeye", "sc"], writes=["sc"])
            P.op("dve", lambda e: e.tensor_reduce(out=m4, in_=sc[:, :, 0:128], axis=AX.X, op=ALU.max, apply_absolute_value=True), reads=["sc"], writes=["m4"])
            P.op("pe", lambda e: e.transpose(PS[1][R4, 0:128], m4, identf[:]), reads=["identf", "m4"], writes=["ps1"])
            P.op("act", lambda e: e.activation(out=tp[:, 0:128], in_=PS[1][R4, 0:128], func=AF.Copy), reads=["ps1"], writes=["tp"])
            P.op("dve", lambda e: e.tensor_reduce(out=sms[R4, 41:42], in_=tp[:, 0:128], axis=AX.X, op=ALU.max), reads=["tp"], writes=["s41"])
            P.op("dve", lambda e: e.tensor_scalar(out=tp[:, 128:132], in0=eye4[:], scalar1=sms[R4, 41:42], scalar2=None, op0=ALU.mult), reads=["s41", "eye4"], writes=["tp"])
            P.op("pe", lambda e: e.matmul(PS[1][:, 0:4], lhsT=selb[:].rearrange("t b p -> t (b p)")[:, 0:128] if False else ones[R4, :], rhs=tp[:, 128:132], start=True, stop=True),
                 reads=["ones", "tp"], writes=["ps1"])
            P.op("act", lambda e: e.activation(out=mtot, in_=PS[1][:, 0:4], func=AF.Copy), reads=["ps1"], writes=["mtot"])
            P.op("dve", lambda e: e.tensor_scalar(out=mtot, in0=mtot, scalar1=1e-20, scalar2=None, op0=ALU.add), reads=["mtot"], writes=["mtot"])
            P.op("dve", lambda e: e.tensor_tensor(out=hfs[:], in0=p2s[:].unsqueeze(1).to_broadcast([128, 4, NI_S + 1]),
                                                  in1=mtot.unsqueeze(2).to_broadcast([128, 4, NI_S + 1]), op=ALU.mult), reads=["p2s", "mtot"], writes=["hfs"])
            P.op("dve", lambda e: e.memset(mid, 0.0), writes=["mid4"])
            for it in range(NI_S):
                last = it == NI_S - 1
                hb = NI_S - 1 if last else it + 1
                hd = NI_S if last else it + 1
                P.op("dve", lambda e: e.tensor_tensor(out=cmpb[:], in0=sc[:], in1=mid.unsqueeze(2).to_broadcast([128, 4, 129]), op=ALU.is_ge),
                     reads=["sc", "mid4"], writes=["cmpb"])
                P.op("dve", lambda e: e.tensor_reduce(out=cnt4, in_=cmpb[:], axis=AX.X, op=ALU.add), reads=["cmpb"], writes=["cnt4"])
                P.op("pe", lambda e: e.matmul(PS[1][:, 0:4], lhsT=ones[:], rhs=cnt4, start=True, stop=True), reads=["ones", "cnt4"], writes=["ps1"])
                P.op("act", lambda e: e.activation(out=tot, in_=PS[1][:, 0:4], func=AF.Copy), reads=["ps1"], writes=["tot4"])
                P.op("dve", lambda e, hb=hb: e.tensor_tensor(out=midm, in0=mid, in1=hfs[:, :, hb], op=ALU.subtract), reads=["mid4", "hfs"], writes=["midm4"])
                P.op("dve", lambda e: e.tensor_scalar(out=ge2, in0=tot, scalar1=255.5, scalar2=2.0, op0=ALU.is_ge, op1=ALU.mult), reads=["tot4"], writes=["ge24"])
                P.op("dve", lambda e, hd=hd: e.tensor_tensor(out=tmp4, in0=ge2, in1=hfs[:, :, hd], op=ALU.mult), reads=["ge24", "hfs"], writes=["tmp4"])
                P.op("dve", lambda e: e.tensor_tensor(out=mid, in0=tmp4, in1=midm, op=ALU.add), reads=["tmp4", "midm4"], writes=["mid4"])
            for b in range(4):
                P.op("dve", lambda e, b=b: e.max(out=v16[:, 0:8], in_=sc[:, b, 0:128]), reads=["sc"], writes=["v16"])
                P.op("dve", lambda e, b=b: e.max_index(out=i16[:, 0:8], in_max=v16[:, 0:8], in_values=sc[:, b, 0:128]), reads=["sc", "v16"], writes=["i16"])
                P.op("dve", lambda e, b=b: e.match_replace(out=sc2[:], in_to_replace=v16[:, 0:8], in_values=sc[:, b, 0:128], imm_value=BIGNEG),
                     reads=["sc", "v16"], writes=["sc2"])
                P.op("dve", lambda e: e.max(out=v16[:, 8:16], in_=sc2[:]), reads=["sc2", "v16"], writes=["v16"])
                P.op("dve", lambda e: e.max_index(out=i16[:, 8:16], in_max=v16[:, 8:16], in_values=sc2[:]), reads=["sc2", "v16", "i16"], writes=["i16"])
                P.op("dve", lambda e, b=b: e.tensor_scalar(out=valid[:, 0:16], in0=v16[:], scalar1=mid[:, b:b + 1], scalar2=None, op0=ALU.is_ge),
                     reads=["v16", "mid4"], writes=["valid"])
                P.op("dve", lambda e, b=b: e.tensor_scalar(out=valid[:, 16:17], in0=sc[:, b, 128:129], scalar1=mid[:, b:b + 1], scalar2=None, op0=ALU.is_ge),
                     reads=["sc", "mid4", "valid"], writes=["valid"])
                P.op("dve", lambda e: e.tensor_copy(out=idxf[:], in_=i16[:]), reads=["i16"], writes=["idxf"])
                P.op("dve", lambda e, b=b: e.tensor_scalar(out=idxf[:], in0=idxf[:], scalar1=pt128[:, b:b + 1], scalar2=None, op0=ALU.add), reads=["idxf", "pt128"], writes=["idxf"])
                P.op("dve", lambda e: e.tensor_copy(out=rowi[:], in_=idxf[:]), reads=["idxf"], writes=["rowi"])
                for r in range(16):
                    P.dma("pool", lambda e, r=r: e.indirect_dma_start(out=KVs[:, r, 0:256], out_offset=None, in_=ckv_d,
                                                                    in_offset=bass.IndirectOffsetOnAxis(ap=rowi[:, r:r + 1], axis=0)),
                          reads=["rowi", "KVs"], writes=[f"KVs{r}"])
                P.op("pe", lambda e, b=b: e.matmul(PS[2][:], lhsT=selb[:, b, :], rhs=qs[:], start=True, stop=True), reads=["selb", "qs"], writes=["ps2"])
                P.op("act", lambda e: e.activation(out=qb[:], in_=PS[2][:], func=AF.Copy), reads=["ps2"], writes=["qb"])
                kall = [f"KVs{r}" for r in range(16)] + ["KVs"]
                vall = kall
                qbv = qb[:].rearrange("p (m two d) -> p m two d", two=2, d=64)
                for g in range(2):
                    for m in range(4):
                        P.op("dve", lambda e, g=g, m=m: e.tensor_tensor(out=prodL[:], in0=KVs[:, :, g * 64:(g + 1) * 64],
                                                                        in1=qbv[:, m, g, :].unsqueeze(1).to_broadcast([128, 17, 64]), op=ALU.mult),
                             reads=kall + ["qb"], writes=["prodL"])
                        P.op("dve", lambda e, g=g, m=m: e.tensor_reduce(out=lg[:, :, g * 4 + m], in_=prodL[:], axis=AX.X, op=ALU.add), reads=["prodL"], writes=["lg"])
                P.op("act", lambda e: e.activation(out=lg[:], in_=lg[:], func=AF.Exp, scale=0.125), reads=["lg"], writes=["lg"])
                P.op("dve", lambda e: e.tensor_tensor(out=Pm[:], in0=lg[:], in1=valid[:].unsqueeze(2).to_broadcast([128, 17, 8]), op=ALU.mult),
                     reads=["lg", "valid"], writes=["Pm"])
                for r in range(17):
                    P.op("pe", lambda e, r=r: e.matmul(PS[3][0:8, 0:129], lhsT=Pm[:, r, :], rhs=KVs[:, r, 128:257], start=(r == 0), stop=(r == 16)),
                         reads=["Pm"] + vall, writes=["ps3"])
                P.op("act", lambda e: e.activation(out=o8[:, 0:129], in_=PS[3][0:8, 0:129], func=AF.Copy), reads=["ps3"], writes=["o8"])
                P.op("dve", lambda e: e.reciprocal(out=o8[:, 129:130], in_=o8[:, 128:129]), reads=["o8"], writes=["o8"])
                P.op("dve", lambda e: e.tensor_scalar(out=a8[:], in0=o8[:, 0:64], scalar1=hsel[:, 0:1], scalar2=None, op0=ALU.mult), reads=["o8", "hsel"], writes=["a8"])
                P.op("dve", lambda e: e.scalar_tensor_tensor(out=a8[:], in0=o8[:, 64:128], scalar=hsel[:, 1:2], in1=a8[:], op0=ALU.mult, op1=ALU.add),
                     reads=["o8", "hsel", "a8"], writes=["a8"])
                P.op("dve", lambda e: e.tensor_scalar(out=a8[:], in0=a8[:], scalar1=o8[:, 129:130], scalar2=None, op0=ALU.mult), reads=["a8", "o8"], writes=["a8"])
                P.op("dve", lambda e: e.tensor_tensor(out=Y[:], in0=a8[:].unsqueeze(1).to_broadcast([8, 8, 64]), in1=eye8[:].unsqueeze(2).to_broadcast([8, 8, 64]), op=ALU.mult),
                     reads=["a8", "eye8"], writes=["Y"])
                P.op("pe", lambda e, b=b: e.matmul(PS[6][R4, :], lhsT=sel8[:, b, :], rhs=Y[:].rearrange("p a b -> p (a b)"), start=True, stop=True),
                     reads=["sel8", "Y"], writes=["ps6"])
                P.op("dve", lambda e: e.tensor_tensor(out=mixs[:, 0:512], in0=PS[6][R4, :], in1=mixs[:, 0:512], op=ALU.add), reads=["ps6", "mixs_a"], writes=["mixs_a"])

        if with_sample:
            with ExitStack() as st:
                emit_sample(st)
                P.finish()

        with ExitStack() as st:
            xt = [sb(st, f"xt{i}", [128, D], F32) for i in range(2)]
            xn = sb(st, "xn", [128, D], BF16)
            junkx = sb(st, "junkx", [128, D], BF16)
            junkb = sb(st, "junkb", [128, 2048], BF16)
            hTp = [sb(st, f"hT{i}", [128, 8, 128], BF16) for i in range(2)]
            recbp = [sb(st, f"recb{i}", [128, 512], BF16) for i in range(2)]
            sm = sb(st, "sm", [128, 64], F32)
            kikf = sb(st, "kikf", [128, 192], F32)
            vf = sb(st, "vf", [128, 128], F32)
            kd = sb(st, "kd", [128, 256], BF16)
            sq = sb(st, "sq", [128, 704], F32)
            qn = sb(st, "qn", [128, 512], BF16)
            qT = sb(st, "qT", [128, 4, 128], BF16)
            iqT = sb(st, "iqT", [128, 4, 128], BF16)
            diagw = sb(st, "diagw", [128, 8, 128], BF16)
            kT = sb(st, "kT", [128, SEQ], BF16)
            ikT = sb(st, "ikT", [128, SEQ], BF16)
            vaug = sb(st, "vaug", [128, NT, 2, 65], BF16)
            scores = sb(st, "scores", [128, SEQ], F32)
            rl = [sb(st, f"rl{i}", [128, 512], BF16) for i in range(8)]
            maskT = sb(st, "maskT", [128, NT, 128], BF16)
            pebuf = [sb(st, f"pebuf{i}", [128, 512], BF16) for i in range(4)]
            ptb = sb(st, "ptb", [128, NT, 512], BF16)
            att = sb(st, "att", [128, 512], BF16)
            tA = [sb(st, f"tA{i}", [128, 512], F32) for i in range(8)]
            vbf = sb(st, "vbf", [128, 512], BF16)
            qp = sb(st, "qp", [128, 512], BF16)
            kp = sb(st, "kp", [128, 512], BF16)
            qkT = sb(st, "qkT", [128, 8, 128], BF16)
            AT = sb(st, "AT", [128, 4, 128], BF16)
            S = sb(st, "S", [128, 4, 128], F32)
            Sdec = sb(st, "Sdec", [128, 4, 128], F32)
            Sbf = sb(st, "Sbf", [128, 4, 128], BF16)
            mixT = sb(st, "mixT", [128, 8, 128], BF16)
            xm = [sb(st, "xm0", [128, D], F32)] * 2
            halfs = sb(st, "halfs", [128, N_IT + 1], F32)
            sm3 = sb(st, "sm3", [128, 16], F32)
            sup = sb(st, "sup", [128, 512], F32)
            P.op("pool", lambda e: e.memset(vaug[:], 1.0), writes=["vaug"])
            P.op("pool", lambda e: e.memset(AT[:], 0.0), writes=["AT"])

            def tile_parts(sq_i, j):
                ti = sq_i * NT + j
                r0 = ti * 128
                L = 128 * (j + 1)
                xb = xt[ti % 2]
                xbn = f"xt{ti % 2}"
                xmb = xm[ti % 2]
                xmn = "xm0"
                par = ti % 2
                hT = hTp[par]
                hTn = f"hT{par}"
                recb = recbp[par]
                recn = f"recb{par}"
                def proj_tok(bank, off, width, nm):
                    for c in range(8):
                        P.op("pe", lambda e, c=c: e.matmul(PS[bank][:, 0:width], lhsT=hT[:, c, :], rhs=wi[:, c, off:off + width],
                                                           start=(c == 0), stop=(c == 7)), reads=[hTn, f"wi{c}"], writes=[nm])
                def part_E():
                    if j == 0:
                        P.op("pool", lambda e: e.memset(S[:], 0.0), writes=["S"])
                    P.dma("sp", lambda e, xb=xb, r0=r0: e.dma_start(out=xb[:], in_=xp[r0:r0 + 128, :]), writes=[xbn])
                    P.op("act", lambda e, xb=xb: e.activation(out=junkx[:], in_=xb[:], func=AF.Square, accum_out=sm[:, 0:1]),
                         reads=[xbn], writes=["junkx", "sm0"])
                    rstd_from(sm[:, 0:1], sm[:, 1:2], D, ["sm0"])
                    P.op("act", lambda e, xb=xb: e.activation(out=xn[:], in_=xb[:], func=AF.Copy, scale=sm[:, 1:2]), reads=[xbn, "sm0"], writes=["xn"])
                    for c in range(8):
                        P.op("pe", lambda e, c=c: e.transpose(PSB[4][:, c * 128:(c + 1) * 128], xn[:, c * 128:(c + 1) * 128], identb[:]),
                             reads=["xn", "identb"], writes=["ps4"])
                    P.op("dve", lambda e: e.tensor_tensor(out=hT[:], in0=PSB[4][:].rearrange("p (c n) -> p c n", c=8),
                                                          in1=bc(g1T[:], 2, [128, 8, 128]), op=ALU.mult), reads=["ps4", "g1T"], writes=[hTn])
                    proj_tok(5, O_RF, 512, "ps5")
                    proj_tok(4, O_RQ, 512, "ps4")
                    proj_tok(7, O_RG, 512, "ps7")
                    proj_tok(6, O_RI, 512, "ps6")
                    if not os.environ.get('K_NOH'):
                        t_ef, t_eq, t_eg, t_f, t_k, t_q, t_gate, t_x = tA
                        n_ef, n_eq, n_eg, n_f, n_k, n_q, n_gate, n_x = [f"tA{i}" for i in range(8)]
                        P.op("act", lambda e: e.activation(out=t_ef[:], in_=PS[5][:], func=AF.Exp, scale=-1.0), reads=["ps5"], writes=[n_ef])
                        P.op("act", lambda e: e.activation(out=t_eq[:], in_=PS[4][:], func=AF.Exp, scale=-1.0), reads=["ps4"], writes=[n_eq])
                        P.op("act", lambda e: e.activation(out=t_eg[:], in_=PS[7][:], func=AF.Exp, scale=-1.0), reads=["ps7"], writes=[n_eg])
                        P.op("act", lambda e: e.activation(out=vbf[:], in_=PS[6][:], func=AF.Copy), reads=["ps6"], writes=["vbf"])
                        for tt, nn in ((t_ef, n_ef), (t_eq, n_eq), (t_eg, n_eg)):
                            P.op("pool", lambda e, tt=tt: e.tensor_scalar(out=tt[:], in0=tt[:], scalar1=1.0, scalar2=1.0, op0=ALU.mult, op1=ALU.add), reads=[nn], writes=[nn])
                            P.op("dve", lambda e, tt=tt: e.reciprocal(out=tt[:], in_=tt[:]), reads=[nn], writes=[nn])
                        P.op("pool", lambda e: e.tensor_tensor(out=t_f[:], in0=t_ef[:], in1=oml[:], op=ALU.mult), reads=[n_ef, "oml"], writes=[n_f])
                        P.op("pool", lambda e: e.tensor_tensor(out=t_f[:], in0=t_f[:], in1=lbt[:], op=ALU.add), reads=[n_f, "lbt"], writes=[n_f])
                        P.op("pool", lambda e: e.tensor_scalar(out=t_k[:], in0=t_f[:], scalar1=-1.0, scalar2=1.0, op0=ALU.mult, op1=ALU.add), reads=[n_f], writes=[n_k])
                        P.op("act", lambda e: e.activation(out=t_f[:], in_=t_f[:], func=AF.Ln), reads=[n_f], writes=[n_f])
                        P.op("dve", lambda e: e.scalar_tensor_tensor(out=t_q[:], in0=PS[4][:], scalar=128 ** -0.5, in1=t_eq[:], op0=ALU.mult, op1=ALU.mult),
                             reads=["ps4", n_eq], writes=[n_q])
                        P.op("dve", lambda e: e.tensor_tensor(out=t_gate[:], in0=PS[7][:], in1=t_eg[:], op=ALU.mult), reads=["ps7", n_eg], writes=[n_gate])
                        P.op("pe", lambda e: e.matmul(PS[4][:], lhsT=mbf[:], rhs=t_f[:], start=True, stop=True), reads=["mbf", n_f], writes=["ps4"])
                        for h in range(4):
                            P.op("pe", lambda e, h=h: e.matmul(PS[6][:, 2 * h:2 * h + 2], lhsT=t_f[:, h * 128:(h + 1) * 128], rhs=indf[:], start=True, stop=True),
                                 reads=[n_f, "indf"], writes=["ps6"])
                        P.op("act", lambda e: e.activation(out=sm3[:, 0:8], in_=PS[6][:, 0:8], func=AF.Copy), reads=["ps6"], writes=["sm3"])
                        cs = sm3[:, 0:8].rearrange("p (h two) -> p h two", two=2)
                        P.op("act", lambda e: e.activation(out=sm[:, 32:36], in_=cs[:, :, 0], func=AF.Exp), reads=["sm3"], writes=["sm32"])
                        P.op("act", lambda e: e.activation(out=sm[:, 36:40], in_=cs[:, :, 1], func=AF.Exp), reads=["sm3"], writes=["sm36"])
                        P.op("dve", lambda e: e.tensor_tensor(out=sm[:, 40:44], in0=cs[:, :, 1], in1=cs[:, :, 0], op=ALU.subtract), reads=["sm3"], writes=["sm40"])
                        P.op("act", lambda e: e.activation(out=sm[:, 40:44], in_=sm[:, 40:44], func=AF.Exp), reads=["sm40"], writes=["sm40"])
                        P.op("act", lambda e: e.activation(out=t_x[:], in_=PS[4][:], func=AF.Exp), reads=["ps4"], writes=[n_x])
                        P.op("dve", lambda e: e.tensor_tensor(out=qp[:], in0=t_q[:], in1=t_x[:], op=ALU.mult), reads=[n_q, n_x], writes=["qp"])
                        P.op("act", lambda e: e.activation(out=t_x[:], in_=PS[4][:], func=AF.Exp, scale=-1.0), reads=["ps4"], writes=[n_x])
                        P.op("dve", lambda e: e.tensor_tensor(out=kp[:], in0=t_k[:], in1=t_x[:], op=ALU.mult), reads=[n_k, n_x], writes=["kp"])
                        for h in range(4):
                            P.op("pe", lambda e, h=h: e.transpose(PSB[5][:, h * 128:(h + 1) * 128], qp[:, h * 128:(h + 1) * 128], identb[:]),
                                 reads=["qp", "identb"], writes=["ps5"])
                        for h in range(4):
                            P.op("pe", lambda e, h=h: e.transpose(PSB[5][:, (4 + h) * 128:(5 + h) * 128], kp[:, h * 128:(h + 1) * 128], identb[:]),
                                 reads=["kp", "identb"], writes=["ps5"])
                        P.op("act", lambda e: e.activation(out=qkT[:].rearrange("p a b -> p (a b)"), in_=PSB[5][:], func=AF.Copy), reads=["ps5"], writes=["qkT"])
                        for h in range(4):
                            P.op("pe", lambda e, h=h: e.matmul(PS[4][0:64, h * 128:(h + 1) * 128], lhsT=qkT[:, 4 + h, 0:64], rhs=qkT[:, h, :], start=True, stop=True),
                                 reads=["qkT"], writes=["ps4"])
                            P.op("pe", lambda e, h=h: e.matmul(PS[4][64:128, h * 128 + 64:(h + 1) * 128], lhsT=qkT[:, 4 + h, 64:128], rhs=qkT[:, h, 64:128], start=True, stop=True),
                                 reads=["qkT"], writes=["ps4"])
                        P.op("dve", lambda e: e.tensor_tensor(out=AT[0:64, :, :], in0=PS[4][0:64, :].rearrange("p (h t) -> p h t", h=4),
                                                              in1=bc(ctf[0:64, :], 1, [64, 4, 128]), op=ALU.mult), reads=["ps4", "ctf"], writes=["AT"])
                        P.op("dve", lambda e: e.tensor_tensor(out=AT[64:128, :, 64:128], in0=PS[4][64:128, :].rearrange("p (h t) -> p h t", h=4)[:, :, 64:128],
                                                              in1=bc(ctf[64:128, 64:128], 1, [64, 4, 64]), op=ALU.mult), reads=["ps4", "ctf"], writes=["AT"])
                        for h in range(4):
                            P.op("pool", lambda e, h=h: e.tensor_scalar(out=Sbf[:, h, :], in0=S[:, h, :], scalar1=sm[:, 32 + h:33 + h], scalar2=0.0, op0=ALU.mult, op1=ALU.add),
                                 reads=["S", "sm32"], writes=["Sbf"])
                            P.op("pool", lambda e, h=h: e.tensor_scalar(out=Sdec[:, h, :], in0=S[:, h, :], scalar1=sm[:, 36 + h:37 + h], scalar2=0.0, op0=ALU.mult, op1=ALU.add),
                                 reads=["S", "sm36"], writes=["Sdec"])
                        for h in range(4):
                            P.op("pe", lambda e, h=h: e.matmul(PS[6][:, h * 128:(h + 1) * 128], lhsT=qkT[:, h, :], rhs=Sbf[:, h, :], start=True, stop=False),
                                 reads=["qkT", "Sbf"], writes=["ps6"])
                            P.op("pe", lambda e, h=h: e.matmul(PS[6][:, h * 128:(h + 1) * 128], lhsT=AT[:, h, :], rhs=vbf[:, h * 128:(h + 1) * 128], start=False, stop=True),
                                 reads=["AT", "vbf"], writes=["ps6"])
                        for h in range(4):
                            P.op("pe", lambda e, h=h: e.matmul(PS[7][:, h * 128:(h + 1) * 128], lhsT=kp[:, h * 128:(h + 1) * 128], rhs=vbf[:, h * 128:(h + 1) * 128], start=True, stop=True),
                                 reads=["kp", "vbf"], writes=["ps7"])
                        for h in range(4):
                            P.op("act", lambda e, h=h: e.activation(out=sup[:, h * 128:(h + 1) * 128], in_=PS[7][:, h * 128:(h + 1) * 128], func=AF.Copy, scale=sm[:, 40 + h:41 + h]),
                                 reads=["ps7", "sm40"], writes=["sup"])
                        P.op("dve", lambda e: e.tensor_tensor(out=S[:].rearrange("p a b -> p (a b)"), in0=sup[:], in1=Sdec[:].rearrange("p a b -> p (a b)"), op=ALU.add),
                             reads=["sup", "Sdec"], writes=["S"])
                        if j == NT - 1:
                            P.dma("sp", lambda e, sq_i=sq_i: e.dma_start(out=nst[sq_i].rearrange("h k v -> k h v"), in_=S[:]), reads=["S"], is_output=True)
                        P.op("act", lambda e: e.activation(out=t_x[:], in_=PS[6][:], func=AF.Square), reads=["ps6"], writes=[n_x])
                        P.op("dve", lambda e: e.tensor_reduce(out=sm[:, 44:48], in_=t_x[:].rearrange("p (a b) -> p a b", b=128), axis=AX.X, op=ALU.add),
                             reads=[n_x], writes=["sm44"])
                        rstd_from(sm[:, 44:48], sm[:, 44:48], 128, ["sm44"])
                        for h in range(4):
                            P.op("act", lambda e, h=h: e.activation(out=t_x[:, h * 128:(h + 1) * 128], in_=PS[6][:, h * 128:(h + 1) * 128], func=AF.Copy, scale=sm[:, 44 + h:45 + h]),
                                 reads=["ps6", "sm44"], writes=[n_x])
                        P.op("pool", lambda e: e.tensor_tensor(out=t_x[:], in0=t_x[:], in1=rngb[:], op=ALU.mult), reads=[n_x, "rngb"], writes=[n_x])
                        P.op("pool", lambda e: e.tensor_tensor(out=recb[:], in0=t_x[:], in1=t_gate[:], op=ALU.mult), reads=[n_x, n_gate], writes=[recn])

                def part_prefix_proj():
                    proj_tok(2, O_KVI, 328, "ps2")
                    proj_tok(1, O_AQ, 512, "ps1")
                    for m in range(4):
                        for c in range(8):
                            P.op("pe", lambda e, c=c, m=m: e.matmul(PS[3][:, m * 128:(m + 1) * 128], lhsT=wi[:, c, O_IQ + m * 128:O_IQ + (m + 1) * 128],
                                                                    rhs=hT[:, c, :], start=(c == 0), stop=(c == 7)), reads=[hTn, f"wi{c}"], writes=["ps3"])
                def part_prefix():
                    P.op("act", lambda e: e.activation(out=sq[:, 0:192], in_=PS[2][:, 0:192], func=AF.Square), reads=["ps2"], writes=["sq"])
                    P.op("act", lambda e: e.activation(out=sq[:, 192:704], in_=PS[1][:, 0:512], func=AF.Square), reads=["ps1"], writes=["sq"])
                    P.op("dve", lambda e: e.tensor_reduce(out=sm[:, 8:19], in_=sq[:].rearrange("p (a b) -> p a b", b=64), axis=AX.X, op=ALU.add),
                         reads=["sq"], writes=["sm8"])
                    rstd_from(sm[:, 8:19], sm[:, 8:19], 64, ["sm8"])
                    stage(2.1)
                    for a in range(3 if os.environ.get("K_VAR", "0") != "2" else 0):
                        P.op("act", lambda e, a=a: e.activation(out=kikf[:, a * 64:(a + 1) * 64], in_=PS[2][:, a * 64:(a + 1) * 64], func=AF.Copy, scale=sm[:, 8 + a:9 + a]),
                             reads=["ps2", "sm8"], writes=["kikf"])
                    if os.environ.get("K_VAR", "0") != "1":
                        P.op("dve", lambda e: e.tensor_tensor(out=kikf[:], in0=kikf[:], in1=kig[:], op=ALU.mult), reads=["kikf", "kig"], writes=["kikf"])
                    stage(2.2)
                    P.op("act", lambda e: e.activation(out=vf[:], in_=PS[2][:, 192:320], func=AF.Copy), reads=["ps2"], writes=["vf"])
                    P.op("act", lambda e: e.activation(out=sm[:, 24:32], in_=PS[2][:, 320:328], func=AF.Copy, scale=IDX_W_SCALE), reads=["ps2"], writes=["sm24"])
                    stage(2.3)
                    P.dma("sp", lambda e, r0=r0: e.dma_start(out=nk[r0:r0 + 128, :], in_=kikf[:, 0:128]), reads=["kikf"], is_output=True)
                    P.dma("sp", lambda e, r0=r0: e.dma_start(out=nki[r0:r0 + 128, :], in_=kikf[:, 128:192]), reads=["kikf"], is_output=True)
                    P.dma("sp", lambda e, r0=r0: e.dma_start(out=nv[r0:r0 + 128, :], in_=vf[:]), reads=["vf"], is_output=True)
                    stage(2.4)
                    P.op("act", lambda e: e.copy(out=kd[:, 0:192], in_=kikf[:]), reads=["kikf"], writes=["kd"])
                    P.op("act", lambda e: e.copy(out=kd[:, 192:256], in_=kikf[:, 128:192]), reads=["kikf"], writes=["kd"])
                    P.op("dve", lambda e, j=j: e.tensor_copy(out=vaug[:, j, :, 0:64], in_=vf[:].rearrange("p (a b) -> p a b", b=64)),
                         reads=["vf"], writes=["vaug"])
                    for h in range(8):
                        P.op("dve", lambda e, h=h: e.tensor_scalar(out=diagw[:, h, :], in0=identb[:], scalar1=sm[:, 24 + h:25 + h], scalar2=None, op0=ALU.mult),
                             reads=["identb", "sm24"], writes=["diagw"])
                    stage(3)
                    for a in range(8):
                        P.op("act", lambda e, a=a: e.activation(out=qn[:, a * 64:(a + 1) * 64], in_=PS[1][:, a * 64:(a + 1) * 64], func=AF.Copy, scale=sm[:, 11 + a:12 + a]),
                             reads=["ps1", "sm8"], writes=["qn"])
                    for m in range(4):
                        P.op("pe", lambda e, m=m: e.transpose(PSB[0][:, m * 128:(m + 1) * 128], qn[:, m * 128:(m + 1) * 128], identb[:]),
                             reads=["qn", "identb"], writes=["ps0"])
                    for m in range(2):
                        P.op("pe", lambda e, m=m: e.transpose(PSB[0][:, (4 + m) * 128:(5 + m) * 128], kd[:, m * 128:(m + 1) * 128], identb[:]),
                             reads=["kd", "identb"], writes=["ps0"])
                    P.op("act", lambda e: e.activation(out=qT[:].rearrange("p a b -> p (a b)"), in_=PSB[0][:, 0:512], func=AF.Copy, scale=qg2[:, 0:1]),
                         reads=["ps0", "qg2"], writes=["qT"])
                    P.op("act", lambda e, j=j: e.activation(out=kT[:, j * 128:(j + 1) * 128], in_=PSB[0][:, 512:640], func=AF.Copy), reads=["ps0"], writes=["kT"])
                    P.op("act", lambda e, j=j: e.activation(out=ikT[:, j * 128:(j + 1) * 128], in_=PSB[0][:, 640:768], func=AF.Copy), reads=["ps0"], writes=["ikT"])
                    P.op("act", lambda e: e.activation(out=iqT[:].rearrange("p a b -> p (a b)"), in_=PS[3][:], func=AF.Copy), reads=["ps3"], writes=["iqT"])

                def part_a():
                    nblk = (L + 511) // 512
                    rounds = [(blk, h) for blk in range(0 if os.environ.get('K_NOIDX') else nblk) for h in range(8)]

                    ibanks = [0, 1, 2, 4, 5, 6, 7]

                    def idx_dots(i):
                        blk, h = rounds[i]
                        wd = min(512, L - blk * 512)
                        db = ibanks[i % 7]
                        pr = slice((h % 2) * 64, (h % 2) * 64 + 64)
                        P.op("pe", lambda e: e.matmul(PS[db][:, 0:wd], lhsT=iqT[pr, h // 2, :], rhs=ikT[pr, blk * 512:blk * 512 + wd], start=True, stop=True),
                             reads=["iqT", "ikT"], writes=[f"ps{db}"])

                    def idx_relu(i):
                        blk, h = rounds[i]
                        wd = min(512, L - blk * 512)
                        db = ibanks[i % 7]
                        rb = i % 8
                        if True:
                            P.op("act", lambda e: e.activation(out=rl[rb][:, 0:wd], in_=PS[db][:, 0:wd], func=AF.Relu), reads=[f"ps{db}"], writes=[f"rl{rb}"])
                        else:
                            P.op("dve", lambda e: e.tensor_scalar(out=rl[rb][:, 0:wd], in0=PS[db][:, 0:wd], scalar1=0.0, scalar2=None, op0=ALU.max),
                                 reads=[f"ps{db}"], writes=[f"rl{rb}"])

                    def idx_diag(i):
                        blk, h = rounds[i]
                        wd = min(512, L - blk * 512)
                        rb = i % 8
                        P.op("pe", lambda e: e.matmul(PS[3][:, 0:wd], lhsT=diagw[:, h, :], rhs=rl[rb][:, 0:wd], start=(h == 0), stop=(h == 7)),
                             reads=["diagw", f"rl{rb}"], writes=["ps3"])
                        if h == 7:
                            P.op("act", lambda e: e.activation(out=scores[:, blk * 512:blk * 512 + wd], in_=PS[3][:, 0:wd], func=AF.Copy),
                                 reads=["ps3"], writes=["scores"])

                    groups = [list(range(k, min(k + 4, len(rounds)))) for k in range(0, len(rounds), 4)]
                    if groups:
                        for i in groups[0]:
                            idx_dots(i)
                    for gi, grp in enumerate(groups):
                        for i in grp:
                            idx_relu(i)
                        if gi + 1 < len(groups):
                            for i in groups[gi + 1]:
                                idx_dots(i)
                        for i in grp:
                            idx_diag(i)
                    P.mark()
                    if j >= 2:
                        P.op("dve", lambda e, L=L: e.tensor_reduce(out=sm[:, 48:49], in_=scores[:, 0:L], axis=AX.X, op=ALU.max, apply_absolute_value=True),
                             reads=["scores"], writes=["sm48"])
                        P.op("dve", lambda e: e.tensor_scalar(out=sm[:, 2:3], in0=sm[:, 48:49], scalar1=1e-20, scalar2=None, op0=ALU.add), reads=["sm48"], writes=["sm48"])
                        P.op("dve", lambda e: e.tensor_scalar(out=halfs[:], in0=p2[:], scalar1=sm[:, 2:3], scalar2=None, op0=ALU.mult),
                             reads=["p2", "sm48"], writes=["halfs"])
                    P.op("dve", lambda e, L=L: e.tensor_tensor(out=scores[:, L - 128:L], in0=scores[:, L - 128:L], in1=negc[:], op=ALU.add),
                         reads=["scores", "negc"], writes=["scores"])
                    if j >= 2:
                        P.op("dve", lambda e: e.memset(sm[:, 50:51], 0.0), writes=["mid"])
                        for it in range(N_IT):
                            last = it == N_IT - 1
                            hb = N_IT - 1 if last else it + 1
                            hd = N_IT if last else it + 1
                            P.op("dve", lambda e, L=L: e.tensor_scalar(out=junkb[:, 0:L], in0=scores[:, 0:L], scalar1=sm[:, 50:51], scalar2=None,
                                                                       op0=ALU.is_ge, op1=ALU.add, accum_out=sm[:, 52:53]),
                                 reads=["scores", "mid"], writes=["junkb", "cnt"])
                            P.op("dve", lambda e, hb=hb: e.tensor_tensor(out=sm[:, 51:52], in0=sm[:, 50:51], in1=halfs[:, hb:hb + 1], op=ALU.subtract),
                                 reads=["mid", "halfs"], writes=["midm"])
                            P.op("dve", lambda e: e.tensor_scalar(out=sm[:, 53:54], in0=sm[:, 52:53], scalar1=255.5, scalar2=2.0, op0=ALU.is_ge, op1=ALU.mult),
                                 reads=["cnt"], writes=["ge2"])
                            P.op("dve", lambda e, hd=hd: e.scalar_tensor_tensor(out=sm[:, 50:51], in0=sm[:, 53:54], scalar=halfs[:, hd:hd + 1], in1=sm[:, 51:52],
                                                                               op0=ALU.mult, op1=ALU.add), reads=["ge2", "halfs", "midm"], writes=["mid"])
                    else:
                        P.op("dve", lambda e: e.memset(sm[:, 50:51], -5000.0), writes=["mid"])
                    P.op("dve", lambda e, L=L: e.tensor_scalar(out=junkb[:, 0:L], in0=scores[:, 0:L], scalar1=sm[:, 50:51], scalar2=None, op0=ALU.is_ge),
                         reads=["scores", "mid"], writes=["junkb"])
                    P.mark()
                    for g0 in range(0, j + 1, 8):
                        g1 = min(j + 1, g0 + 8)
                        for kc in range(g0, g1):
                            P.op("pe", lambda e, kc=kc, g0=g0: e.transpose(PSB[0][:, (kc - g0) * 128:(kc - g0 + 1) * 128], junkb[:, kc * 128:(kc + 1) * 128], identb[:]),
                                 reads=["junkb", "identb"], writes=["ps0"])
                        P.op("act", lambda e, g0=g0, g1=g1: e.activation(out=maskT[:, g0:g1, :].rearrange("p a b -> p (a b)"), in_=PSB[0][:, 0:(g1 - g0) * 128], func=AF.Copy),
                             reads=["ps0"], writes=["maskT"])
                    stage(7)
                    abanks = [1, 2, 4, 5, 6, 7]
                    mi = 0
                    for g in range(0 if os.environ.get('K_NOATT') else 2):
                        pr = slice(g * 64, g * 64 + 64)
                        kcs = list(range(j + 1))
                        agroups = [kcs[k:k + 3] for k in range(0, len(kcs), 3)]
                        slot = {}
                        for kc in kcs:
                            slot[kc] = mi
                            mi += 1

                        def a_qk(kc, prl=None):
                            prl = pr
                            db = abanks[slot[kc] % 6]
                            P.op("pe", lambda e: e.matmul(PS[db][:], lhsT=kT[prl, kc * 128:(kc + 1) * 128],
                                                          rhs=qT[prl, :, :].rearrange("p a b -> p (a b)"), start=True, stop=True),
                                 reads=["kT", "qT"], writes=[f"ps{db}"])

                        def a_exp(kc):
                            db = abanks[slot[kc] % 6]
                            pb = slot[kc] % 4
                            P.op("act", lambda e: e.activation(out=pebuf[pb][:], in_=PS[db][:], func=AF.Exp, scale=0.125),
                                 reads=[f"ps{db}"], writes=[f"pebuf{pb}"])

                        def a_mask(kc):
                            pb = slot[kc] % 4
                            P.op("dve", lambda e: e.tensor_tensor(out=ptb[:, kc, :].rearrange("p (h t) -> p h t", h=4),
                                                                  in0=pebuf[pb][:].rearrange("p (h t) -> p h t", h=4),
                                                                  in1=bc(maskT[:, kc, :], 1, [128, 4, 128]), op=ALU.mult),
                                 reads=[f"pebuf{pb}", "maskT"], writes=[f"ptb{kc}"])

                        for kc in agroups[0]:
                            a_qk(kc)
                        for gi, grp in enumerate(agroups):
                            for kc in grp:
                                a_exp(kc)
                            if gi + 1 < len(agroups):
                                for kc in agroups[gi + 1]:
                                    a_qk(kc)
                            for kc in grp:
                                a_mask(kc)
                        pvb = 3 if g == 0 else 0
                        for hh in range(0 if os.environ.get('K_NOPV') else 4):
                            for kc in range(j + 1):
                                P.op("pe", lambda e, hh=hh, kc=kc, g=g, pvb=pvb: e.matmul(PS[pvb][:, hh * 65:(hh + 1) * 65], lhsT=ptb[:, kc, hh * 128:(hh + 1) * 128],
                                                                                         rhs=vaug[:, kc, g, :], start=(kc == 0), stop=(kc == j)),
                                     reads=[f"ptb{kc}", "vaug"], writes=[f"ps{pvb}"])
                        pv = PS[pvb][:, 0:260].rearrange("p (h c) -> p h c", c=65)
                        P.op("dve", lambda e, pv=pv, g=g: e.reciprocal(out=sm[:, 56 + 4 * g:60 + 4 * g], in_=pv[:, :, 64]), reads=[f"ps{pvb}"], writes=[f"rs{g}"])
                        for hh in range(4):
                            P.op("act", lambda e, hh=hh, g=g, pvb=pvb: e.activation(out=att[:, g * 256 + hh * 64:g * 256 + (hh + 1) * 64], in_=PS[pvb][:, hh * 65:hh * 65 + 64],
                                                                                  func=AF.Copy, scale=sm[:, 56 + 4 * g + hh:57 + 4 * g + hh]),
                                 reads=[f"ps{pvb}", f"rs{g}"], writes=["att_a"])
                def part_tail():
                    for c in range(8):
                        src_ap = att[:, c * 128:(c + 1) * 128] if c < 4 else recb[:, (c - 4) * 128:(c - 3) * 128]
                        P.op("pe", lambda e, c=c, src_ap=src_ap: e.transpose(PSB[6][:, c * 128:(c + 1) * 128], src_ap, identb[:]),
                             reads=["att_a", recn, "identb"], writes=["ps6"])
                    P.op("act", lambda e: e.activation(out=mixT[:].rearrange("p a b -> p (a b)"), in_=PSB[6][:], func=AF.Copy), reads=["ps6"], writes=["mixT"])
                    for half in range(2):
                        ob = 4 + half
                        for c in range(8):
                            P.op("pe", lambda e, c=c, half=half, ob=ob: e.matmul(PS[ob][:], lhsT=mixT[:, c, :], rhs=wo[:, c, half * 512:(half + 1) * 512],
                                                                                 start=(c == 0), stop=(c == 7)), reads=["mixT", f"wo{c}"], writes=[f"ps{ob}"])
                        P.op("dve", lambda e, half=half, ob=ob, xb=xb, xmb=xmb: e.tensor_tensor(out=xmb[:, half * 512:(half + 1) * 512], in0=PS[ob][:],
                                                                                               in1=xb[:, half * 512:(half + 1) * 512], op=ALU.add),
                             reads=[f"ps{ob}", xbn], writes=[xmn])
                    P.dma("sp", lambda e, r0=r0, xmb=xmb: e.dma_start(out=yp[r0:r0 + 128, :], in_=xmb[:]), reads=[xmn], writes=[f"yd{ti}"], is_output=True)
                return part_E, part_prefix, part_a, part_tail, part_prefix_proj

            tiles = [tile_parts(sq_i, j) for sq_i in range(DBG_NSEQ) for j in range(DBG_NT)]
            if tiles:
                tiles[0][0]()
                tiles[0][4]()
            for t, (pE, pP, pA, pT, pPP) in enumerate(tiles):
                pP()
                P.begin("a")
                pA()
                P.end()
                if t + 1 < len(tiles):
                    P.begin("e")
                    tiles[t + 1][0]()
                    P.end()
                P.merge("a", "e")
                if t + 1 < len(tiles):
                    tiles[t + 1][4]()
                pT()
            P.finish()

        stW.close()

        with ExitStack() as st:
            wu = sb(st, "wu", [128, 8, 4 * D], BF16)
            wd_ = sb(st, "wd", [128, 32, D], BF16)
            xs = [sb(st, f"xs{i}", [128, 2, D], F32) for i in range(2)]
            xn2 = sb(st, "xn2", [128, D], BF16)
            junk2 = sb(st, "junk2", [128, D], BF16)
            h2T = sb(st, "h2T", [128, 8, 256], BF16)
            rl2 = [sb(st, f"r2{i}", [128, 256], F32) for i in range(2)]
            uT = sb(st, "uT", [128, 32, 256], BF16)
            yo = [sb(st, f"yo{i}", [128, D], F32) for i in range(2)]
            sm2 = sb(st, "sm2", [128, 8], F32)
            eps2 = sb(st, "eps2", [128, 1], F32)
            P.op("pool", lambda e: e.memset(eps2[:], EPS), writes=["eps2"])
            for c in range(8):
                P.dma("pool", lambda e, c=c: e.dma_start(out=wu[:, c, :], in_=w_up[c * 128:(c + 1) * 128, :]), writes=[f"wu{c}"])
            for c in range(32):
                P.dma("pool", lambda e, c=c: e.dma_start(out=wd_[:, c, :], in_=w_down[c * 128:(c + 1) * 128, :]), writes=[f"wd{c}"])
            NSUP = NTOK // 256 if DBG_B else 0
            def load_xs(su_):
                xb_ = xs[su_ % 2]
                r0_ = su_ * 256
                P.dma("sp", lambda e: e.dma_start(out=xb_[:], in_=yp[r0_:r0_ + 256, :].rearrange("(a p) n -> p a n", p=128)), writes=[f"xs{su_ % 2}"])

            if NSUP:
                load_xs(0)
            for su in range(NSUP):
                xb = xs[su % 2]
                xbn = f"xs{su % 2}"
                r0 = su * 256
                if su + 1 < NSUP:
                    load_xs(su + 1)
                for a in range(2):
                    P.op("act", lambda e, xb=xb, a=a: e.activation(out=junk2[:], in_=xb[:, a, :], func=AF.Square, accum_out=sm2[:, 0:1]),
                         reads=[xbn], writes=["junk2", "sm2"])
                    P.op("act", lambda e: e.activation(out=sm2[:, 1:2], in_=sm2[:, 0:1], func=AF.Ln, scale=1.0 / D, bias=eps2[:, 0:1]), reads=["sm2", "eps2"], writes=["sm2"])
                    P.op("act", lambda e: e.activation(out=sm2[:, 1:2], in_=sm2[:, 1:2], func=AF.Exp, scale=-0.5), reads=["sm2"], writes=["sm2"])
                    P.op("act", lambda e, xb=xb, a=a: e.activation(out=xn2[:], in_=xb[:, a, :], func=AF.Copy, scale=sm2[:, 1:2]), reads=[xbn, "sm2"], writes=["xn2"])
                    for c in range(8):
                        P.op("pe", lambda e, c=c: e.transpose(PSB[0][:, c * 128:(c + 1) * 128], xn2[:, c * 128:(c + 1) * 128], identb[:]),
                             reads=["xn2"], writes=["ps0"])
                    P.op("dve", lambda e, a=a: e.tensor_tensor(out=h2T[:, :, a * 128:(a + 1) * 128], in0=PSB[0][:].rearrange("p (c n) -> p c n", c=8),
                                                              in1=bc(g2T[:], 2, [128, 8, 128]), op=ALU.mult), reads=["ps0"], writes=["h2T"])
                for hc in range(32):
                    ub = 1 + (hc % 2)
                    rb = hc % 2
                    for c in range(8):
                        P.op("pe", lambda e, c=c, hc=hc, ub=ub: e.matmul(PS[ub][:, 0:256], lhsT=wu[:, c, hc * 128:(hc + 1) * 128], rhs=h2T[:, c, :],
                                                                         start=(c == 0), stop=(c == 7)), reads=[f"wu{c}", "h2T"], writes=[f"ps{ub}"])
                    P.op("act", lambda e, ub=ub, rb=rb: e.activation(out=rl2[rb][:], in_=PS[ub][:, 0:256], func=AF.Relu), reads=[f"ps{ub}"], writes=[f"r2{rb}"])
                    sqe = "dve"
                    P.op(sqe, lambda e, rb=rb, hc=hc: e.tensor_tensor(out=uT[:, hc, :], in0=rl2[rb][:], in1=rl2[rb][:], op=ALU.mult),
                         reads=[f"r2{rb}"], writes=["uT"])
                for a in range(2):
                    yb = yo[a]
                    for half in range(2):
                        ob = 3 + half + 2 * a
                        for hc in range(32):
                            P.op("pe", lambda e, hc=hc, a=a, half=half, ob=ob: e.matmul(PS[ob][:], lhsT=uT[:, hc, a * 128:(a + 1) * 128],
                                                                                        rhs=wd_[:, hc, half * 512:(half + 1) * 512], start=(hc == 0), stop=(hc == 31)),
                                 reads=["uT", f"wd{hc}"], writes=[f"ps{ob}"])
                        P.op("dve", lambda e, a=a, half=half, ob=ob, xb=xb, yb=yb: e.tensor_tensor(out=yb[:, half * 512:(half + 1) * 512], in0=PS[ob][:],
                                                                                                  in1=xb[:, a, half * 512:(half + 1) * 512], op=ALU.add),
                             reads=[f"ps{ob}", xbn], writes=[f"yo{a}"])
                    P.dma("sp", lambda e, r0=r0, a=a, yb=yb: e.dma_start(out=yp[r0 + a * 128:r0 + (a + 1) * 128, :], in_=yb[:]), reads=[f"yo{a}"], is_output=True)
            if with_sample:
                R4 = slice(0, 4)
                h2Ts = sb(st, "h2Ts", [128, 8, 4], BF16)
                uTs = sb(st, "uTs", [128, 32, 4], BF16)
                r2s = [sb(st, f"r2s{i}", [128, 4], F32) for i in range(2)]
                yos = sb(st, "yos", [4, D], F32)
                P.op("act", lambda e: e.activation(out=junk2[R4, :], in_=xmid_s[:], func=AF.Square, accum_out=sm2[R4, 4:5]), reads=["xmid_s"], writes=["junk2", "sm2s"])
                P.op("act", lambda e: e.activation(out=sm2[R4, 5:6], in_=sm2[R4, 4:5], func=AF.Ln, scale=1.0 / D, bias=eps2[R4, 0:1]), reads=["sm2s", "eps2"], writes=["sm2s"])
                P.op("act", lambda e: e.activation(out=sm2[R4, 5:6], in_=sm2[R4, 5:6], func=AF.Exp, scale=-0.5), reads=["sm2s"], writes=["sm2s"])
                P.op("act", lambda e: e.activation(out=xn2[R4, :], in_=xmid_s[:], func=AF.Copy, scale=sm2[R4, 5:6]), reads=["xmid_s", "sm2s"], writes=["xn2"])
                for c in range(8):
                    P.op("pe", lambda e, c=c: e.transpose(PSB[0][:, c * 128:c * 128 + 4], xn2[R4, c * 128:(c + 1) * 128], identb[0:4, 0:4]),
                         reads=["xn2"], writes=["ps0"])
                P.op("dve", lambda e: e.tensor_tensor(out=h2Ts[:], in0=PSB[0][:].rearrange("p (c n) -> p c n", c=8)[:, :, 0:4],
                                                      in1=bc(g2T[:], 2, [128, 8, 4]), op=ALU.mult), reads=["ps0"], writes=["h2Ts"])
                for hc in range(32):
                    ub = 1 + (hc % 2)
                    rb = hc % 2
                    for c in range(8):
                        P.op("pe", lambda e, c=c, hc=hc, ub=ub: e.matmul(PS[ub][:, 0:4], lhsT=wu[:, c, hc * 128:(hc + 1) * 128], rhs=h2Ts[:, c, :],
                                                                         start=(c == 0), stop=(c == 7)), reads=[f"wu{c}", "h2Ts"], writes=[f"ps{ub}"])
                    P.op("act", lambda e, ub=ub, rb=rb: e.activation(out=r2s[rb][:], in_=PS[ub][:, 0:4], func=AF.Relu), reads=[f"ps{ub}"], writes=[f"r2s{rb}"])
                    P.op("dve", lambda e, rb=rb, hc=hc: e.tensor_tensor(out=uTs[:, hc, :], in0=r2s[rb][:], in1=r2s[rb][:], op=ALU.mult),
                         reads=[f"r2s{rb}"], writes=["uTs"])
                for half in range(2):
                    ob = 3 + half
                    for hc in range(32):
                        P.op("pe", lambda e, hc=hc, half=half, ob=ob: e.matmul(PS[ob][R4, :], lhsT=uTs[:, hc, :], rhs=wd_[:, hc, half * 512:(half + 1) * 512],
                                                                               start=(hc == 0), stop=(hc == 31)), reads=["uTs", f"wd{hc}"], writes=[f"ps{ob}"])
                    P.op("dve", lambda e, half=half, ob=ob: e.tensor_tensor(out=yos[:, half * 512:(half + 1) * 512], in0=PS[ob][R4, :],
                                                                           in1=xmid_s[:, half * 512:(half + 1) * 512], op=ALU.add), reads=[f"ps{ob}", "xmid_s"], writes=["yos"])
                P.dma("sp", lambda e: e.dma_start(out=ys_d, in_=yos[:]), reads=["yos"], is_output=True)
            P.finish()
        block = st0.enter_context(nc.Block("blk"))
        P.emit(block)
        global _LAST_PROG
        _LAST_PROG = P
    return nc


def _consts():
    c = {}
    c["c_ident"] = np.eye(128, dtype=np.float32)
    s = np.arange(128)[:, None]
    t = np.arange(128)[None, :]
    mb = np.zeros((128, 128), np.float32)
    mb[(s >= 64) & (s <= t)] = 1.0
    mb[(s < 64) & (s > t)] = -1.0
    c["c_mb"] = mb
    ind = np.zeros((128, 2), np.float32)
    ind[:64, 0] = 1.0
    ind[:, 1] = 1.0
    c["c_ind"] = ind
    c["c_ct"] = (s <= t).astype(np.float32)
    c["c_negc"] = np.where(t <= s, 0.0, NEG).astype(np.float32)
    p2 = (1.001 * 0.5 ** np.arange(N_IT + 1)).astype(np.float32)
    c["c_p2"] = np.tile(p2[None, :], (128, 1))
    selb = np.zeros((4, 4, 128), np.float32)
    for b in range(4):
        selb[b, b, :] = 1.0
    c["c_selb"] = selb
    c["c_eye4"] = np.eye(4, dtype=np.float32)
    c["c_oh4"] = np.tile(np.eye(4, dtype=np.float32).reshape(1, 16), (128, 1))
    c["c_negeye"] = ((1.0 - np.eye(4)) * BIGNEG).astype(np.float32)
    hs = np.zeros((8, 2), np.float32)
    hs[:4, 0] = 1.0
    hs[4:, 1] = 1.0
    c["c_hsel"] = hs
    c["c_eye8"] = np.eye(8, dtype=np.float32)
    sel8 = np.zeros((8, 4, 4), np.float32)
    for b in range(4):
        sel8[:, b, b] = 1.0
    c["c_sel8"] = sel8
    c["c_ones"] = np.ones((128, 128), np.float32)
    p2s = (1.001 * 0.5 ** np.arange(NI_S + 1)).astype(np.float32)
    c["c_p2s"] = np.tile(p2s[None, :], (128, 1))
    return c


_NC_CACHE = {}
_LAST_PROG = None


def kernel(x_prompt, x_sample, cache_k, cache_v, cache_kidx, state_hgrn, page_table,
           norm1_g, w_in, q_norm_g, k_norm_g, idx_k_norm_g, lower_bounds, rec_norm_g,
           w_out, norm2_g, w_up, w_down):
    f32 = np.float32
    x_prompt = np.asarray(x_prompt, f32)
    w_in0 = np.asarray(w_in, f32)[0]
    aq = w_in0[:, 0:512].reshape(D, 8, 64)
    aq_perm = np.stack([aq[:, [m, m + 4], :] for m in range(4)], axis=1).reshape(D, 512)
    ak, av = w_in0[:, 512:640], w_in0[:, 640:768]
    iq, ik, iw = w_in0[:, 768:1280], w_in0[:, 1280:1344], w_in0[:, 1344:1352]
    rest = w_in0[:, 1352:]
    w_in_p = np.ascontiguousarray(np.concatenate([aq_perm, ak, ik, av, iw, iq, rest], axis=1))
    assert w_in_p.shape == (D, INW)
    shared = dict(_consts())
    shared["w_in"] = w_in_p
    shared["w_out"] = np.ascontiguousarray(np.asarray(w_out, f32)[0])
    shared["w_up"] = np.ascontiguousarray(np.asarray(w_up, f32)[0])
    shared["w_down"] = np.ascontiguousarray(np.asarray(w_down, f32)[0])
    shared["p_g1T"] = np.ascontiguousarray(np.asarray(norm1_g, f32)[0].reshape(8, 128).T)
    shared["p_g2T"] = np.ascontiguousarray(np.asarray(norm2_g, f32)[0].reshape(8, 128).T)
    qg = np.asarray(q_norm_g, f32)[0]
    kg = np.asarray(k_norm_g, f32)[0]
    ig = np.asarray(idx_k_norm_g, f32)[0]
    shared["p_qg2"] = np.concatenate([qg, qg])[:, None].astype(f32)
    shared["p_kig"] = np.tile(np.concatenate([kg, kg, ig])[None, :], (128, 1)).astype(f32)
    shared["p_rng"] = np.tile(np.tile(np.asarray(rec_norm_g, f32)[0], 4)[None, :], (128, 1)).astype(f32)
    lbs = np.asarray(lower_bounds, f32)
    shared["p_lb"] = np.tile(np.concatenate([lbs[0], lbs[1]])[None, :], (128, 1)).astype(f32)
    shared["p_qgb"] = np.tile(np.tile(qg, 8)[None, :], (128, 1)).astype(f32)
    shared["cache_kidx"] = np.asarray(cache_kidx, f32)[0].reshape(NPOOL, 8192)
    shared["cache_kv"] = np.concatenate([np.asarray(cache_k, f32)[0].reshape(NPOOL * 128, 128),
                                         np.asarray(cache_v, f32)[0].reshape(NPOOL * 128, 128)], axis=1)
    x_sample = np.asarray(x_sample, f32)
    state_hgrn = np.asarray(state_hgrn, f32)
    page_table = np.asarray(page_table, np.int32)

    if "nc" not in _NC_CACHE:
        _NC_CACHE["nc"] = build()
    nc = _NC_CACHE["nc"]
    in_maps = []
    for c in range(NCORES):
        m = dict(shared)
        m["xp"] = np.ascontiguousarray(x_prompt[NSEQ * c:NSEQ * (c + 1)].reshape(NTOK, D))
        m["xs"] = np.ascontiguousarray(x_sample[4 * c:4 * (c + 1), 0, :])
        m["state_s"] = np.ascontiguousarray(state_hgrn[0, 4 * c:4 * (c + 1)])
        m["ptT"] = np.ascontiguousarray(page_table[4 * c:4 * (c + 1)].T)
        in_maps.append(m)
    res = run_bass_kernel_spmd(nc, in_maps, core_ids=list(range(NCORES)))
    R = res.results
    B = x_prompt.shape[0]
    y_prompt = np.concatenate([r["yp"].reshape(NSEQ, SEQ, D) for r in R], axis=0)
    nkp = np.concatenate([r["nk"].reshape(NSEQ, SEQ, 2, 64) for r in R], axis=0)[None]
    nvp = np.concatenate([r["nv"].reshape(NSEQ, SEQ, 2, 64) for r in R], axis=0)[None]
    nkip = np.concatenate([r["nki"].reshape(NSEQ, SEQ, 64) for r in R], axis=0)[None]
    nstp = np.concatenate([r["nst"] for r in R], axis=0)[None]
    y_sample = np.concatenate([r["ys"] for r in R], axis=0)[:, None, :]
    nks = np.concatenate([r["nks"].reshape(4, 1, 2, 64) for r in R], axis=0)[None]
    nvs = np.concatenate([r["nvs"].reshape(4, 1, 2, 64) for r in R], axis=0)[None]
    nkis = np.concatenate([r["nkis"].reshape(4, 1, 64) for r in R], axis=0)[None]
    nsts = np.concatenate([r["nsts"] for r in R], axis=0)[None]
    return (y_prompt.astype(f32), y_sample.astype(f32), nkp.astype(f32), nvp.astype(f32), nkip.astype(f32), nstp.astype(f32),
            nks.astype(f32), nvs.astype(f32), nkis.astype(f32), nsts.astype(f32))
```
